# Optimizing a Trainium2 kernel written in Bass

```python
import math
import jax, jax.numpy as jnp
from jax import lax
import numpy as np

D_MODEL = 1024
BATCH = 32
SEQ = 2048
DEPTH = 1
DEC_BATCH = 32
DEC_SEQ = 64
PAST_LEN = 1024

CHUNK = 64
MIX_WIDTH = D_MODEL
SB_HEAD_DIM = 64
SB_HEADS = (MIX_WIDTH // 2) // SB_HEAD_DIM
SB_WIDTH = SB_HEADS * SB_HEAD_DIM
SB_BLOCK = 128
SSM_HEAD_DIM = 64
SSM_WIDTH = MIX_WIDTH - SB_WIDTH
SSM_HEADS = SSM_WIDTH // SSM_HEAD_DIM
SSM_GROUPS = 2
SSM_STATE = 64
CONV_WIDTH = 4
CONV_CH = SSM_WIDTH + 2 * SSM_GROUPS * SSM_STATE
D_FF = 4 * D_MODEL
EPS = 1e-5
IN_PROJ = 3 * SB_WIDTH + SSM_WIDTH + CONV_CH + SSM_HEADS
SPLITS = (SB_WIDTH, 2 * SB_WIDTH, 3 * SB_WIDTH, 3 * SB_WIDTH + SSM_WIDTH,
          3 * SB_WIDTH + SSM_WIDTH + CONV_CH)

kernel_name = "stickbreak_ssd_hybrid_stream_step"


def rms_norm(x, w):
    xf = x.astype(jnp.float32)
    xf = xf * lax.rsqrt(jnp.mean(xf * xf, axis=-1, keepdims=True) + EPS)
    return (xf * w.astype(jnp.float32)).astype(x.dtype)


def stick_breaking_block(q, k, v, q_pos):
    k_pos = jnp.arange(k.shape[1])
    z = jnp.einsum("bqhd,bkhd->bhqk", q, k).astype(jnp.float32) * (SB_HEAD_DIM ** -0.5)
    visible = k_pos[None, :] < q_pos[:, None]
    log_beta = jax.nn.log_sigmoid(z)
    log_keep = jnp.where(visible, jax.nn.log_sigmoid(-z), 0.0)
    later = lax.cumsum(log_keep, axis=3, reverse=True) - log_keep
    attn = jnp.where(visible, jnp.exp(log_beta + later), 0.0)
    return jnp.einsum("bhqk,bkhd->bqhd", attn.astype(v.dtype), v)


def causal_conv(xbc, buf, w, b):
    L = xbc.shape[1]
    xp = jnp.concatenate([buf.astype(xbc.dtype), xbc], axis=1)
    out = b
    for i in range(CONV_WIDTH):
        out = out + xp[:, i:i + L] * w[i]
    return jax.nn.silu(out), xp[:, xp.shape[1] - (CONV_WIDTH - 1):]


def segsum(a):
    T = a.shape[-1]
    rep = jnp.broadcast_to(a[..., :, None], a.shape + (T,))
    rep = jnp.where(jnp.tril(jnp.ones((T, T), bool), -1), rep, 0.0)
    cs = jnp.cumsum(rep, axis=-2)
    return jnp.where(jnp.tril(jnp.ones((T, T), bool)), cs, -jnp.inf)


def ssd_scan(x, dt, A, Bm, Cm, h0):
    Bsz, L = x.shape[:2]
    Q = min(CHUNK, L)
    nc = L // Q
    rep = SSM_HEADS // SSM_GROUPS
    Bc = jnp.repeat(Bm, rep, axis=2).reshape(Bsz, nc, Q, SSM_HEADS, SSM_STATE)
    Cc = jnp.repeat(Cm, rep, axis=2).reshape(Bsz, nc, Q, SSM_HEADS, SSM_STATE)
    xc = (x * dt[..., None]).reshape(Bsz, nc, Q, SSM_HEADS, SSM_HEAD_DIM)
    a = (dt * A).reshape(Bsz, nc, Q, SSM_HEADS).transpose(0, 3, 1, 2)
    a_cum = jnp.cumsum(a, axis=-1)
    decay_in = jnp.exp(segsum(a))
    y_diag = jnp.einsum("bclhn,bcshn,bhcls,bcshp->bclhp", Cc, Bc, decay_in, xc)
    decay_to_end = jnp.exp(a_cum[..., -1:] - a_cum)
    chunk_states = jnp.einsum("bclhn,bhcl,bclhp->bchpn", Bc, decay_to_end, xc)
    chunk_decay = jnp.exp(a_cum[..., -1])

    def step(h, inp):
        st, dec = inp
        return dec[..., None, None] * h + st, h

    h_final, h_prev = lax.scan(step, h0, (chunk_states.transpose(1, 0, 2, 3, 4),
                                          chunk_decay.transpose(2, 0, 1)))
    h_prev = h_prev.transpose(1, 0, 2, 3, 4)
    y_off = jnp.einsum("bclhn,bchpn,bhcl->bclhp", Cc, h_prev, jnp.exp(a_cum))
    return (y_diag + y_off).reshape(Bsz, L, SSM_HEADS, SSM_HEAD_DIM), h_final


def mamba2_mixer(z, xbc, dt_raw, conv_buf, h0, conv_w, conv_b, dt_bias, a_log, d_skip, ssm_norm_w):
    Bsz, L = z.shape[:2]
    xbc, new_buf = causal_conv(xbc, conv_buf, conv_w, conv_b)
    xf = xbc.astype(jnp.float32)
    xs = xf[..., :SSM_WIDTH].reshape(Bsz, L, SSM_HEADS, SSM_HEAD_DIM)
    Bm = xf[..., SSM_WIDTH:SSM_WIDTH + SSM_GROUPS * SSM_STATE].reshape(Bsz, L, SSM_GROUPS, SSM_STATE)
    Cm = xf[..., SSM_WIDTH + SSM_GROUPS * SSM_STATE:].reshape(Bsz, L, SSM_GROUPS, SSM_STATE)
    dt = jax.nn.softplus(dt_raw.astype(jnp.float32) + dt_bias.astype(jnp.float32))
    A = -jnp.exp(a_log.astype(jnp.float32))
    y, h_final = ssd_scan(xs, dt, A, Bm, Cm, h0.astype(jnp.float32))
    y = y + d_skip.astype(jnp.float32)[:, None] * xs
    y = y.reshape(Bsz, L, SSM_WIDTH) * jax.nn.silu(z.astype(jnp.float32))
    yg = y.reshape(Bsz, L, SSM_GROUPS, SSM_WIDTH // SSM_GROUPS)
    yg = yg * lax.rsqrt(jnp.mean(yg * yg, axis=-1, keepdims=True) + EPS)
    y = yg.reshape(Bsz, L, SSM_WIDTH) * ssm_norm_w.astype(jnp.float32)
    return y.astype(z.dtype), new_buf, h_final


def layer_forward(x, k_past, v_past, conv_buf, h0, norm1_w, w_in, conv_w, conv_b,
                  dt_bias, a_log, d_skip, ssm_norm_w, w_out, norm2_w, w_up, w_down):
    Bsz, L, _ = x.shape
    xn = rms_norm(x, norm1_w)
    proj = jnp.einsum("bld,de->ble", xn, w_in)
    q, k, v, z, xbc, dt_raw = jnp.split(proj, SPLITS, axis=-1)
    q = q.reshape(Bsz, L, SB_HEADS, SB_HEAD_DIM)
    k = k.reshape(Bsz, L, SB_HEADS, SB_HEAD_DIM)
    v = v.reshape(Bsz, L, SB_HEADS, SB_HEAD_DIM)
    if k_past is None:
        k_all, v_all, q0 = k, v, 0
    else:
        k_all = jnp.concatenate([k_past.astype(k.dtype), k], axis=1)
        v_all = jnp.concatenate([v_past.astype(v.dtype), v], axis=1)
        q0 = k_past.shape[1]
    q_pos = q0 + jnp.arange(L)
    if L > SB_BLOCK and L % SB_BLOCK == 0:
        nblk = L // SB_BLOCK
        qb = q.reshape(Bsz, nblk, SB_BLOCK, SB_HEADS, SB_HEAD_DIM).swapaxes(0, 1)
        pb = q_pos.reshape(nblk, SB_BLOCK)
        o = lax.map(lambda qp: stick_breaking_block(qp[0], k_all, v_all, qp[1]), (qb, pb))
        o = o.swapaxes(0, 1).reshape(Bsz, L, SB_WIDTH)
    else:
        o = stick_breaking_block(q, k_all, v_all, q_pos).reshape(Bsz, L, SB_WIDTH)
    y_ssm, new_buf, h_final = mamba2_mixer(z, xbc, dt_raw, conv_buf, h0, conv_w, conv_b,
                                           dt_bias, a_log, d_skip, ssm_norm_w)
    mix = jnp.concatenate([o.astype(x.dtype), y_ssm], axis=-1)
    x = x + jnp.einsum("ble,ed->bld", mix, w_out)
    hn = rms_norm(x, norm2_w)
    u = jnp.square(jax.nn.relu(jnp.einsum("bld,df->blf", hn, w_up)))
    x = x + jnp.einsum("blf,fd->bld", u, w_down)
    return x, k, v, new_buf, h_final


def setup_inputs(seed: int = 0) -> dict:
    key = jax.random.key(seed)
    ks = jax.random.split(key, 24)
    f32 = jnp.float32

    def nrm(k, shape, scale=1.0):
        return jax.random.normal(k, shape, f32) * scale

    dt0 = jnp.exp(jax.random.uniform(ks[10], (DEPTH, SSM_HEADS), f32,
                                     math.log(1e-3), math.log(1e-1)))
    return {
        "x_prompt": nrm(ks[0], (BATCH, SEQ, D_MODEL)),
        "x_sample": nrm(ks[1], (DEC_BATCH, DEC_SEQ, D_MODEL)),
        "cache_k": nrm(ks[2], (DEPTH, DEC_BATCH, PAST_LEN, SB_HEADS, SB_HEAD_DIM)),
        "cache_v": nrm(ks[3], (DEPTH, DEC_BATCH, PAST_LEN, SB_HEADS, SB_HEAD_DIM)),
        "state_conv": nrm(ks[4], (DEPTH, DEC_BATCH, CONV_WIDTH - 1, CONV_CH)),
        "state_ssm": nrm(ks[5], (DEPTH, DEC_BATCH, SSM_HEADS, SSM_HEAD_DIM, SSM_STATE), 0.5),
        "norm1_w": 1.0 + nrm(ks[6], (DEPTH, D_MODEL), 0.02),
        "w_in": nrm(ks[7], (DEPTH, D_MODEL, IN_PROJ), D_MODEL ** -0.5),
        "conv_w": nrm(ks[8], (DEPTH, CONV_WIDTH, CONV_CH), CONV_WIDTH ** -0.5),
        "conv_b": nrm(ks[9], (DEPTH, CONV_CH), 0.01),
        "dt_bias": dt0 + jnp.log(-jnp.expm1(-dt0)),
        "a_log": jnp.log(jax.random.uniform(ks[11], (DEPTH, SSM_HEADS), f32, 1.0, 16.0)),
        "d_skip": 1.0 + nrm(ks[12], (DEPTH, SSM_HEADS), 0.02),
        "ssm_norm_w": 1.0 + nrm(ks[13], (DEPTH, SSM_WIDTH), 0.02),
        "w_out": nrm(ks[14], (DEPTH, MIX_WIDTH, D_MODEL), MIX_WIDTH ** -0.5),
        "norm2_w": 1.0 + nrm(ks[15], (DEPTH, D_MODEL), 0.02),
        "w_up": nrm(ks[16], (DEPTH, D_MODEL, D_FF), D_MODEL ** -0.5),
        "w_down": nrm(ks[17], (DEPTH, D_FF, D_MODEL), D_FF ** -0.5),
        "final_norm_w": 1.0 + nrm(ks[18], (D_MODEL,), 0.02),
    }


def reference(x_prompt, x_sample, cache_k, cache_v, state_conv, state_ssm,
              norm1_w, w_in, conv_w, conv_b, dt_bias, a_log, d_skip, ssm_norm_w,
              w_out, norm2_w, w_up, w_down, final_norm_w):
    hp, hs = x_prompt, x_sample
    Bp = x_prompt.shape[0]
    kp_l, vp_l, cp_l, sp_l = [], [], [], []
    ks_l, vs_l, cs_l, ss_l = [], [], [], []
    for l in range(DEPTH):
        lw = (norm1_w[l], w_in[l], conv_w[l], conv_b[l], dt_bias[l], a_log[l], d_skip[l],
              ssm_norm_w[l], w_out[l], norm2_w[l], w_up[l], w_down[l])
        conv0 = jnp.zeros((Bp, CONV_WIDTH - 1, CONV_CH), x_prompt.dtype)
        h0 = jnp.zeros((Bp, SSM_HEADS, SSM_HEAD_DIM, SSM_STATE), jnp.float32)
        hp, kp, vp, cp, sp = layer_forward(hp, None, None, conv0, h0, *lw)
        hs, kk, vv, cc, ss = layer_forward(hs, cache_k[l], cache_v[l], state_conv[l],
                                           state_ssm[l], *lw)
        kp_l.append(kp); vp_l.append(vp); cp_l.append(cp); sp_l.append(sp.astype(x_prompt.dtype))
        ks_l.append(kk.astype(cache_k.dtype)); vs_l.append(vv.astype(cache_v.dtype))
        cs_l.append(cc.astype(state_conv.dtype)); ss_l.append(ss.astype(state_ssm.dtype))
    y_prompt = rms_norm(hp, final_norm_w)
    y_sample = rms_norm(hs, final_norm_w)
    k_prompt = jnp.stack(kp_l); v_prompt = jnp.stack(vp_l)
    conv_prompt = jnp.stack(cp_l); ssm_prompt = jnp.stack(sp_l)
    k_sample = jnp.stack(ks_l); v_sample = jnp.stack(vs_l)
    conv_sample = jnp.stack(cs_l); ssm_sample = jnp.stack(ss_l)
    return (y_prompt, y_sample, k_prompt, v_prompt, conv_prompt, ssm_prompt,
            k_sample, v_sample, conv_sample, ssm_sample)
```

```python
import numpy as np
from contextlib import ExitStack
import concourse.bass as bass
import concourse.mybir as mybir
from concourse.bass_utils import run_bass_kernel_spmd

F32 = mybir.dt.float32
BF16 = mybir.dt.bfloat16
AF = mybir.ActivationFunctionType
ALU = mybir.AluOpType

NCORES = 8
D = 1024
INP = 2824
DFF = 4096
EPS = 1e-5
NEG = -30000.0


class Buf:
    __slots__ = ("name", "w", "r", "excl")

    def __init__(self, name, excl=False):
        self.name = name
        self.w = None
        self.r = {}
        self.excl = excl


class Eng:
    def __init__(self, name, h, sem):
        self.name = name
        self.h = h
        self.sem = sem
        self.count = 0
        self.seen = {}


class FW:
    def __init__(self, nc, es, n_dma_sems=40):
        self.nc = nc
        self.sems = {}

        def mk(name):
            s = es.enter_context(nc.semaphore(name))
            self.sems[name] = s
            return s

        self.pe = Eng("pe", nc.tensor, mk("pe"))
        self.act = Eng("act", nc.scalar, mk("act"))
        self.dve = Eng("dve", nc.vector, mk("dve"))
        self.pool = Eng("pool", nc.gpsimd, mk("pool"))
        self.sp = Eng("sp", nc.sync, mk("sp"))
        self.engs = {e.name: e for e in (self.pe, self.act, self.dve, self.pool, self.sp)}
        self.dsem = [mk("d%d" % i) for i in range(n_dma_sems)]
        self.dcnt = [0] * n_dma_sems
        self.n_sw = 8
        self.dnext = {"hw": 0, "sw": 0}
        self.nins = 0

    def _need(self, eng, tok, needs):
        if tok is None:
            return
        key, val, _ = tok
        if eng.seen.get(key, 0) >= val:
            return
        if needs.get(key, 0) < val:
            needs[key] = val

    def _waits(self, eng, reads, writes, is_dma=False, strict=False):
        needs = {}
        for b in reads:
            self._need(eng, b.w, needs)
            if b.excl:
                for k, t in b.r.items():
                    if k != eng.name:
                        self._need(eng, t, needs)
        for b in writes:
            if b.w is not None and (is_dma or strict or b.w[2] != eng.name):
                self._need(eng, b.w, needs)
            for k, t in b.r.items():
                if is_dma or k != eng.name:
                    self._need(eng, t, needs)
        for key, val in needs.items():
            if key in self.engs:
                assert val <= self.engs[key].count, "wait on future inc %s %d > %d" % (key, val, self.engs[key].count)
            eng.h.wait_ge(self.sems[key], val)
            eng.seen[key] = val
            self.nins += 1

    def op(self, eng, fn, reads=(), writes=(), inc=True, strict=False):
        self._waits(eng, reads, writes, strict=strict)
        ins = fn()
        self.nins += 1
        if inc:
            ins.then_inc(eng.sem, 1)
            eng.count += 1
            tok = (eng.name, eng.count, eng.name)
        else:
            tok = (eng.name, eng.count + 1, eng.name)
        for b in reads:
            b.r[eng.name] = tok
        for b in writes:
            b.w = tok
            b.r = {}
        return ins

    def dma(self, eng, out, in_, reads=(), writes=(), **kw):
        if eng.name == "pool":
            i = self.dnext["sw"]
            self.dnext["sw"] = (i + 1) % self.n_sw
        else:
            i = self.n_sw + self.dnext["hw"]
            self.dnext["hw"] = (self.dnext["hw"] + 1) % (len(self.dsem) - self.n_sw)
        key = "d%d" % i
        prev = self.dcnt[i] * 16
        if prev and eng.seen.get(key, 0) < prev:
            eng.h.wait_ge(self.dsem[i], prev)
            eng.seen[key] = prev
            self.nins += 1
        self._waits(eng, reads, writes, is_dma=True)
        ins = eng.h.dma_start(out=out, in_=in_, **kw)
        self.nins += 1
        ins.then_inc(self.dsem[i], 16)
        self.dcnt[i] += 1
        tok = (key, self.dcnt[i] * 16, "dma%d_%d" % (i, self.dcnt[i]))
        for b in reads:
            b.r[tok[2]] = tok
        for b in writes:
            b.w = tok
            b.r = {}
        return tok

    def inherit(self, new_bufs, old_bufs):
        toks = {}
        for ob in old_bufs:
            if ob.w is not None:
                toks["w_" + ob.w[2] + ob.name] = ob.w
            for k, t in ob.r.items():
                toks[k + "_" + ob.name] = t
        for nb in new_bufs:
            nb.w = None
            nb.r = dict(toks)

    def finish(self):
        for i, s in enumerate(self.dsem):
            if self.dcnt[i]:
                self.sp.h.wait_ge(s, self.dcnt[i] * 16)


CST_NAMES = ["ident", "ones", "negLinc", "negLstr", "amask", "tri_le", "negtri_le", "trichunk", "triU",
             "negLinc_blk", "negL2", "amask_s"]
NCST = len(CST_NAMES) * 128 + 512


def make_consts():
    k = np.arange(128)[:, None]
    j = np.arange(128)[None, :]
    same = (k // 64) == (j // 64)
    m = {}
    m["ident"] = (k == j)
    m["ones"] = np.ones((128, 128), bool)
    m["negLinc"] = -(k >= j).astype(np.float32)
    m["negLstr"] = -(k < j).astype(np.float32)
    m["amask"] = (k < j)
    m["tri_le"] = (k <= j)
    m["negtri_le"] = -(k <= j).astype(np.float32)
    m["trichunk"] = (k <= j) & same
    m["triU"] = (k > j) & same
    m["negLinc_blk"] = -((k >= j) & same).astype(np.float32)
    m["negL2"] = np.where(same, -(k < j).astype(np.float32), -1.0)
    m["amask_s"] = ((k % 64) < j)
    cols = [np.asarray(m[n], np.float32) for n in CST_NAMES]
    negmask = np.where(same & (j >= k), 0.0, NEG).astype(np.float32)
    cols.append(np.tile(negmask, (1, 4)))
    return np.ascontiguousarray(np.concatenate(cols, axis=1).astype(np.float32))


P_N1W, P_N2W, P_CW, P_CB, P_DSK, P_SNW, P_DTB, P_ALOG = 0, 8, 16, 40, 46, 50, 54, 62
NPRM = 70


def build(NBP=4, SEQ=2048, NBS=4, PAST=1024, DSEQ=64):
    assert DSEQ == 64 and NBS % 2 == 0 and SEQ % 512 == 0 and PAST % 128 == 0
    nc = bass.Bass("TRN2", target_bir_lowering=False)

    def din(name, shape, dt=F32):
        return nc.dram_tensor(name, shape, dt, kind="ExternalInput").ap()

    def dout(name, shape):
        return nc.dram_tensor(name, shape, F32, kind="ExternalOutput").ap()

    def dint(name, shape, dt):
        return nc.dram_tensor(name, shape, dt, kind="Internal").ap()

    xp = din("xp", [NBP * SEQ, D])
    xs = din("xs", [NBS * DSEQ, D])
    ck = din("ck", [NBS, PAST, 512])
    cv = din("cv", [NBS, PAST, 512])
    sconv = din("sconv", [NBS, 3, 768])
    sssm = din("sssm", [NBS, 8, 64, 64])
    w_in = din("w_in", [D, INP])
    w_out = din("w_out", [D, D])
    w_up = din("w_up", [D, DFF])
    w_down = din("w_down", [DFF, D])
    prm_d = din("prm", [128, NPRM])
    fnw_d = din("fnw", [128, D])
    cst_d = din("cst", [128, NCST])
    identf_d = din("identf", [128, 128])

    yp = dout("yp", [NBP * SEQ, D])
    ys = dout("ys", [NBS * DSEQ, D])
    kp = dout("kp", [NBP * SEQ, 512])
    vp = dout("vp", [NBP * SEQ, 512])
    convp = dout("convp", [NBP, 3, 768])
    ssmp = dout("ssmp", [NBP, 8, 64, 64])
    ksm = dout("ksm", [NBS * DSEQ, 512])
    vsm = dout("vsm", [NBS * DSEQ, 512])
    convs = dout("convs", [NBS, 3, 768])
    ssms = dout("ssms", [NBS, 8, 64, 64])

    w_in_bf = dint("w_in_bf", [D, INP], BF16)
    w_out_bf = dint("w_out_bf", [D, D], BF16)
    w_up_bf = dint("w_up_bf", [D, DFF], BF16)
    w_down_bf = dint("w_down_bf", [DFF, D], BF16)

    es = ExitStack()
    with es:
        fw = FW(nc, es)

        def sb(name, shape, dt=F32):
            return es.enter_context(nc.sbuf_tensor("s_" + name, shape, dt))

        def PE(fn, reads, writes, inc=True):
            return fw.op(fw.pe, fn, reads, writes, inc)

        def ACT(fn, reads, writes, strict=False):
            return fw.op(fw.act, fn, reads, writes, strict=strict)

        def DVE(fn, reads, writes):
            return fw.op(fw.dve, fn, reads, writes)

        def POOL(fn, reads, writes):
            return fw.op(fw.pool, fn, reads, writes)

        def mm(out, lhsT, rhs, start, stop, reads, writes, inc=True, serial=False):
            if serial:
                nc.tensor.wait_ge(fw.pe.sem, fw.pe.count)
                fw.pe.seen["pe"] = fw.pe.count
            return PE(lambda: nc.tensor.matmul(out, lhsT=lhsT, rhs=rhs, start=start, stop=stop,
                                               skip_group_check=True), reads, writes, inc)

        class Bank:
            pass

        banks = []
        for i in range(8):
            b = Bank()
            b.t = es.enter_context(nc.psum_tensor("bank%d" % i, [128, 512], F32))
            b.ap = b.t[:]
            b.bf = b.t[:].bitcast(BF16)
            b.buf = Buf("bank%d" % i, excl=True)
            banks.append(b)
        bank_rr = [0, 0]

        def nextbank():
            b = banks[bank_rr[0]]
            bank_rr[0] = (bank_rr[0] + 1) % 6
            return b

        def nextbank_mlp():
            b = banks[6 + bank_rr[1]]
            bank_rr[1] = (bank_rr[1] + 1) % 2
            return b

        cst = sb("cst", [128, NCST], BF16)
        b_cst = Buf("cst")
        identf = sb("identf", [128, 128])
        b_identf = Buf("identf")
        prm = sb("prm", [128, NPRM])
        b_prm = Buf("prm")
        fnw = sb("fnw", [128, D])
        b_fnw = Buf("fnw")
        aneg = sb("aneg", [128, 8])
        b_aneg = Buf("aneg")

        def C(name, r0=0, r1=128, c0=0, c1=128):
            o = CST_NAMES.index(name) * 128
            return cst[r0:r1, o + c0:o + c1]

        negmaskD = cst[:, len(CST_NAMES) * 128: len(CST_NAMES) * 128 + 512]

        NSLOT = 3
        ring = [sb("ring%d" % i, [128, 4096], BF16) for i in range(NSLOT)]
        b_ring = [Buf("ring%d" % i) for i in range(NSLOT)]

        kT = sb("kT", [128, 4, 2048], BF16)
        b_kT = [Buf("kT0"), Buf("kT1")]
        v_bf = sb("v_bf", [128, 16, 512], BF16)
        b_v = [Buf("v0"), Buf("v1")]
        vnew = sb("vnew", [128, 2, 512], BF16)
        b_vnew = Buf("vnew")
        kTn = sb("kTn", [128, 4, 256], BF16)
        b_kTn = Buf("kTn")

        class XB:
            pass

        XS = []
        for i in range(2):
            X = XB()
            X.x = sb("x_sb%d" % i, [128, 4, D])
            X.bx = [Buf("x%d_%d" % (i, j)) for j in range(4)]
            X.xnT = sb("xnT%d" % i, [128, 8, 512], BF16)
            X.bxn = Buf("xnT%d" % i)
            XS.append(X)
        rtmp = sb("rtmp", [128, 2, 512])
        b_rtmp = [Buf("rtmp0"), Buf("rtmp1")]
        stat = sb("stat", [128, 16])
        b_stat = Buf("stat")

        qpad = [sb("qpad%d" % e, [128, 4, 512], BF16) for e in range(2)]
        b_qT = Buf("qT")
        mixT = sb("mixT", [128, 8, 512], BF16)
        b_mix = [Buf("mix%d" % c) for c in range(8)]
        uT = sb("uT", [128, 2, 4, 512], BF16)
        b_uT = [Buf("uT0"), Buf("uT1")]

        scrB = sb("scrB", [128, 1024])
        b_scrB = [Buf("scrB0"), Buf("scrB1")]
        zT = sb("zT", [128, 4, 512])
        b_z = [Buf("z%d" % c) for c in range(4)]
        scrC = sb("scrC", [128, 4608])
        xbc_bf = sb("xbc_bf", [128, 6, 512], BF16)
        b_xbcbf = [Buf("xbcbf%d" % c) for c in range(6)]
        scrA = sb("scrA", [128, 4, 512])
        b_scrA = [Buf("scrA%d" % c) for c in range(4)]
        cstate = sb("cstate", [128, 6, 3])
        b_cstate = Buf("cstate")

        xbcT = scrC[:, 0:6 * 520].rearrange("p (c n) -> p c n", c=6)
        b_xbcT = [Buf("xbcT%d" % c) for c in range(6)]
        cacc = [scrC[:, 3120 + i * 520: 3120 + (i + 1) * 520] for i in range(2)]
        b_cacc = [Buf("cacc0"), Buf("cacc1")]
        NE, NG, NSP, NA = 3, 2, 5, 3
        e_sb = [scrC[:, i * 512:(i + 1) * 512] for i in range(NE)]
        b_e = [Buf("e%d" % i) for i in range(NE)]
        g_sb = [scrC[:, 1536 + i * 512: 1536 + (i + 1) * 512] for i in range(NG)]
        b_g = [Buf("g%d" % i) for i in range(NG)]
        scrC_bf = scrC[:, 2560:4608].bitcast(BF16)
        sp_sb = [scrC_bf[:, i * 512:(i + 1) * 512] for i in range(NSP)]
        b_sp = [Buf("sp%d" % i) for i in range(NSP)]
        A_sb = [scrC_bf[:, 2560 + i * 512: 2560 + (i + 1) * 512] for i in range(NA)]
        b_A = [Buf("A%d" % i) for i in range(NA)]
        conv_bufs = b_xbcT + b_cacc
        attn_bufs = b_e + b_g + b_sp + b_A

        dts = sb("dts", [128, 3, 4, 8])
        b_dts = Buf("dts")
        a_bf = sb("a_bf", [128, 4, 8], BF16)
        b_abf = Buf("a_bf")
        abc = sb("abc", [128, 8, 128], BF16)
        b_abc = Buf("abc")
        AT1 = sb("AT1", [128, 8, 128], BF16)
        b_AT1 = Buf("AT1")
        M_bf = sb("M_bf", [128, 8, 128], BF16)
        b_M = Buf("M")
        xs_bf = [AT1[:].rearrange("p h l -> p (h l)"), M_bf[:].rearrange("p h l -> p (h l)")]
        b_xsbf = [b_AT1, b_M]
        xc = sb("xc", [128, 512], BF16)
        b_xc = Buf("xc")
        xcd = sb("xcd", [128, 512], BF16)
        b_xcd = Buf("xcd")
        Btok = sb("Btok", [128, 128], BF16)
        b_Btok = Buf("Btok")
        eaT = sb("eaT", [128, 4, 128])
        b_eaT = Buf("eaT")
        ytmp = sb("ytmp", [128, 4, 128])
        b_ytmp = Buf("ytmp")
        dcd = sb("dcd", [128, 16])
        b_dcd = Buf("dcd")
        hT = [sb("hT%d" % i, [128, 256]) for i in range(max(NBS, 1))]
        b_hT = [Buf("hT%d" % i) for i in range(max(NBS, 1))]
        hTb = [sb("hTb%d" % i, [128, 256], BF16) for i in range(max(NBS, 1))]
        b_hTb = [Buf("hTb%d" % i) for i in range(max(NBS, 1))]
        htmp = ytmp[:, 0:2, :].rearrange("p a b -> p (a b)")
        b_htmp = b_ytmp
        stT = ytmp[0:64, :, :]
        b_stT = b_ytmp
        sq_bf = abc[:].rearrange("p h l -> p (h l)").rearrange("p (a b) -> p a b", a=2)
        b_sq = [b_abc, b_abc]

        fw.dma(fw.pool, cst[:], cst_d[:, :], writes=[b_cst])
        fw.dma(fw.sp, identf[:], identf_d[:, :], writes=[b_identf])
        fw.dma(fw.sp, prm[:], prm_d[:, :], writes=[b_prm])
        fw.dma(fw.sp, fnw[:], fnw_d[:, :], writes=[b_fnw])
        b_win = [Buf("win%d" % i) for i in range(8)]
        b_wout = [Buf("wout%d" % i) for i in range(8)]
        b_wup = [Buf("wup%d" % i) for i in range(8)]
        b_wdn = [Buf("wdn%d" % i) for i in range(32)]
        for i in range(8):
            fw.dma(fw.pool, w_in_bf[i * 128:(i + 1) * 128, :], w_in[i * 128:(i + 1) * 128, :], writes=[b_win[i]])
        for i in range(8):
            fw.dma(fw.pool, w_out_bf[i * 128:(i + 1) * 128, :], w_out[i * 128:(i + 1) * 128, :], writes=[b_wout[i]])
        for i in range(8):
            fw.dma(fw.pool, w_up_bf[i * 128:(i + 1) * 128, :], w_up[i * 128:(i + 1) * 128, :], writes=[b_wup[i]])
        for i in range(32):
            fw.dma(fw.pool, w_down_bf[i * 128:(i + 1) * 128, :], w_down[i * 128:(i + 1) * 128, :], writes=[b_wdn[i]])
        for e_ in range(2):
            DVE(lambda e_=e_: nc.vector.memset(qpad[e_][:], 0.0), [], [b_qT])
        ACT(lambda: nc.scalar.activation(out=aneg[:], in_=prm[:, P_ALOG:P_ALOG + 8], func=AF.Exp), [b_prm], [b_aneg])
        DVE(lambda: nc.vector.tensor_scalar(out=aneg[:], in0=aneg[:], scalar1=-1.0, scalar2=None, op0=ALU.mult),
            [b_aneg], [b_aneg])

        w_in_v = w_in_bf.rearrange("(k p) c -> p k c", p=128)
        w_out_v = w_out_bf.rearrange("(k p) c -> p k c", p=128)
        w_up_v = w_up_bf.rearrange("(k p) c -> p k c", p=128)
        w_dn_v = w_down_bf.rearrange("(f p) d -> p f d", p=128)

        def piece_plan():
            pin = []
            for i in range(5):
                pin.append(("in%d" % i, w_in_v[:, :, i * 512:(i + 1) * 512], b_win, (8, 512)))
            pin.append(("in5", w_in_v[:, :, 2560:2824], b_win, (8, 264)))
            pwo = [("wo0", w_out_v[:, :, 0:512], b_wout, (8, 512)), ("wo1", w_out_v[:, :, 512:1024], b_wout, (8, 512))]
            order = ["up0"]
            for p in range(1, 8):
                order += ["up%d" % p, "dn%d" % (p - 1)]
            order.append("dn7")
            pml = []
            for nm in order:
                p = int(nm[2:])
                if nm[:2] == "up":
                    pml.append((nm, w_up_v[:, :, p * 512:(p + 1) * 512], b_wup, (8, 512)))
                else:
                    pml.append((nm, w_dn_v[:, 4 * p:4 * p + 4, :], b_wdn[4 * p:4 * p + 4], (4, 1024)))
            return pin, pwo, pml

        n_super = NBP * (SEQ // 512) + (1 if NBS else 0)
        pin, pwo, pml = piece_plan()
        plan = []
        for i_ in range(n_super):
            if i_ > 0:
                plan += pml[0:2]
            plan += pin
            if i_ > 0:
                plan += pml[2:]
            plan += pwo
        plan += pml
        ws = {"cur": -1, "issued": 0}

        def ws_get(name, hold=0):
            ws["cur"] += 1
            i = ws["cur"]
            assert plan[i][0] == name, (plan[i][0], name)
            while ws["issued"] <= min(i + NSLOT - 1 - hold, len(plan) - 1):
                j = ws["issued"]
                nm, src, sbufs, (a, b) = plan[j]
                dst = ring[j % NSLOT][:, 0:a * b].rearrange("p (k c) -> p k c", k=a)
                fw.dma(fw.sp, dst, src, reads=list(sbufs), writes=[b_ring[j % NSLOT]])
                ws["issued"] += 1
            nm, src, sbufs, (a, b) = plan[i]
            return ring[i % NSLOT][:, 0:a * b].rearrange("p (k c) -> p k c", k=a), b_ring[i % NSLOT]

        b_stats = [Buf("stat%d" % k) for k in range(4)]

        def norm_a(j, X, stg, bstg, slot):
            sc = stat[:, 4 * slot:4 * slot + 4]
            bs = b_stats[slot]
            DVE(lambda: nc.vector.memset(sc[:, 0:1], 0.0), [], [bs])
            ACT(lambda: nc.scalar.activation(out=stg, in_=X.x[:, j, :], func=AF.Square, accum_out=sc[:, 0:1]),
                [X.bx[j]], [bstg, bs])
            ACT(lambda: nc.scalar.activation(out=sc[:, 1:2], in_=sc[:, 0:1], func=AF.Ln, scale=1.0 / D, bias=EPS),
                [bs], [bs])
            ACT(lambda: nc.scalar.activation(out=sc[:, 2:3], in_=sc[:, 1:2], func=AF.Exp, scale=-0.5), [bs], [bs])
            DVE(lambda: nc.vector.tensor_scalar(out=stg, in0=X.x[:, j, :], scalar1=sc[:, 2:3],
                                                scalar2=None, op0=ALU.mult), [X.bx[j], bs], [bstg])

        def norm_b(j, nwcol, X, stg, bstg):
            nw = prm[:, nwcol:nwcol + 8]
            bk = nextbank()
            v3 = bk.bf.rearrange("p (c t) -> p c t", c=8)
            for c in range(8):
                PE(lambda c=c: nc.tensor.transpose(v3[:, c, :], stg[:, c * 128:(c + 1) * 128], C("ident")),
                   [bstg, b_cst], [bk.buf], inc=(c == 7))
            DVE(lambda: nc.vector.tensor_tensor(out=X.xnT[:, :, j * 128:(j + 1) * 128], in0=v3,
                                                in1=nw.unsqueeze(2).to_broadcast([128, 8, 128]),
                                                op=ALU.mult), [bk.buf, b_prm], [X.bxn])

        def norm1_stage(c):
            return scrA[:, c, :].bitcast(BF16), b_scrA[c]

        def norm_T(nsub, nwcol, X, tick=None):
            for j in range(nsub):
                stg, bstg = norm1_stage(j)
                norm_a(j, X, stg, bstg, 2)
                norm_b(j, nwcol, X, stg, bstg)

        def in_proj(T, nsub, nseg, L, kT_dst, kT_bufs, k_rows, v_rows, v_dst, v_bufs, X):
            segw = 3 + L
            xnT = X.xnT
            b_xnT = X.bxn

            def feat_group(w, wbuf, col0, T, evac):
                bk = nextbank()
                for kc in range(8):
                    mm(bk.ap[:, 0:T], w[:, kc, col0:col0 + 128], xnT[:, kc, 0:T], kc == 0, kc == 7,
                       [wbuf, b_xnT], [bk.buf], inc=(kc == 7))
                evac(bk)

            def tok_group(w, wbuf, c0, n, j, evac):
                bk = nextbank()
                for kc in range(8):
                    mm(bk.ap[:, 0:n], xnT[:, kc, j * 128:(j + 1) * 128], w[:, kc, c0:c0 + n], kc == 0, kc == 7,
                       [wbuf, b_xnT], [bk.buf], inc=(kc == 7))
                evac(bk)

            def xbc_dst(c):
                return xbcT[:, c, 0:nseg * segw].rearrange("p (s l) -> p s l", l=segw)[:, :, 3:3 + L]

            w, wb = ws_get("in0")
            for cc in range(4):
                def evq(bk, cc=cc):
                    for e_ in range(2):
                        pr_ = slice(64 * e_, 64 * e_ + 64)
                        ACT(lambda: nc.scalar.activation(out=qpad[e_][pr_, cc, 0:T], in_=bk.ap[pr_, 0:T], func=AF.Copy,
                                                         scale=0.125), [bk.buf], [b_qT])
                feat_group(w, wb, cc * 128, T, evq)
            w, wb = ws_get("in1")
            for cc in range(4):
                feat_group(w, wb, cc * 128, T, lambda bk, cc=cc: DVE(
                    lambda: nc.vector.tensor_copy(out=kT_dst(cc), in_=bk.ap[:, 0:T]), [bk.buf], kT_bufs))
            for j in range(nsub):
                def ev(bk, j=j):
                    ACT(lambda: nc.scalar.copy(out=scrA[:, j % 2, :], in_=bk.ap[:, :]), [bk.buf], [b_scrA[j % 2]])
                    fw.dma(fw.sp, k_rows(j), scrA[:, j % 2, :], reads=[b_scrA[j % 2]])
                tok_group(w, wb, 0, 512, j, ev)
            w, wb = ws_get("in2")
            for j in range(nsub):
                def ev(bk, j=j):
                    ACT(lambda: nc.scalar.copy(out=scrA[:, 2 + j % 2, :], in_=bk.ap[:, :]), [bk.buf], [b_scrA[2 + j % 2]])
                    DVE(lambda: nc.vector.tensor_copy(out=v_dst(j), in_=bk.ap[:, :]), [bk.buf], v_bufs)
                    fw.dma(fw.sp, v_rows(j), scrA[:, 2 + j % 2, :], reads=[b_scrA[2 + j % 2]])
                tok_group(w, wb, 0, 512, j, ev)
            w, wb = ws_get("in3")
            for cc in range(4):
                feat_group(w, wb, cc * 128, T, lambda bk, cc=cc: ACT(
                    lambda: nc.scalar.activation(out=zT[:, cc, 0:T], in_=bk.ap[:, 0:T], func=AF.Silu),
                    [bk.buf], [b_z[cc]]))
            w, wb = ws_get("in4")
            for cc in range(4):
                feat_group(w, wb, cc * 128, T, lambda bk, cc=cc: DVE(
                    lambda: nc.vector.tensor_copy(out=xbc_dst(cc), in_=bk.ap[:, 0:T].rearrange("p (s l) -> p s l", l=L)),
                    [bk.buf], [b_xbcT[cc]]))
            w, wb = ws_get("in5")
            for cc in range(2):
                feat_group(w, wb, cc * 128, T, lambda bk, cc=cc: DVE(
                    lambda: nc.vector.tensor_copy(out=xbc_dst(4 + cc), in_=bk.ap[:, 0:T].rearrange("p (s l) -> p s l", l=L)),
                    [bk.buf], [b_xbcT[4 + cc]]))
            for j in range(nsub):
                tok_group(w, wb, 256, 8, j, lambda bk, j=j: DVE(
                    lambda: nc.vector.tensor_tensor(out=dts[:, 0, j, :], in0=bk.ap[:, 0:8], in1=prm[:, P_DTB:P_DTB + 8],
                                                    op=ALU.add), [bk.buf, b_prm], [b_dts]))

        def conv_silu(T, nseg, L, tick=None):
            segw = 3 + L
            n = nseg * segw - 3
            for c in range(6):
                if tick:
                    tick(2.0)
                acc = cacc[c % 2]
                ba = b_cacc[c % 2]
                src = xbcT[:, c, :]
                eng = DVE
                h = nc.vector
                cw0 = P_CW + c * 4
                eng(lambda: h.tensor_scalar(out=acc[:, 0:n], in0=src[:, 0:n], scalar1=prm[:, cw0:cw0 + 1],
                                            scalar2=prm[:, P_CB + c:P_CB + c + 1], op0=ALU.mult, op1=ALU.add),
                    [b_xbcT[c], b_prm], [ba])
                for i in range(1, 4):
                    eng(lambda i=i: h.scalar_tensor_tensor(out=acc[:, 0:n], in0=src[:, i:i + n],
                                                           scalar=prm[:, cw0 + i:cw0 + i + 1], in1=acc[:, 0:n],
                                                           op0=ALU.mult, op1=ALU.add), [b_xbcT[c], b_prm, ba], [ba])
                accv = acc[:, 0:nseg * segw].rearrange("p (s l) -> p s l", l=segw)[:, :, 0:L]
                if c < 4:
                    xtmp = scrB[:, (c % 2) * 512:(c % 2) * 512 + T]
                    ACT(lambda: nc.scalar.activation(out=xtmp.rearrange("p (s l) -> p s l", l=L), in_=accv,
                                                     func=AF.Silu), [ba], [b_scrB[c % 2]])
                    POOL(lambda: nc.gpsimd.tensor_copy(out=xbc_bf[:, c, 0:T], in_=xtmp), [b_scrB[c % 2]], [b_xbcbf[c]])
                    ACT(lambda: nc.scalar.activation(out=scrA[:, c, 0:T], in_=xtmp, func=AF.Copy,
                                                     scale=prm[:, P_DSK + c:P_DSK + c + 1]), [b_scrB[c % 2], b_prm], [b_scrA[c]])
                else:
                    ACT(lambda: nc.scalar.activation(out=xbc_bf[:, c, 0:T].rearrange("p (s l) -> p s l", l=L), in_=accv,
                                                     func=AF.Silu), [ba], [b_xbcbf[c]])

        def ssd_subtile(j, nsub_T, st_in, st_mid, st_out, tick=None):
            cols = slice(j * 128, (j + 1) * 128)
            bk_tr, bk_sm, bk_D0, bk_D1, bk_G, bk_ea = banks[0:6]
            bk_yd = banks[4]
            bk_yo = banks[1]
            tk = tick if tick else (lambda us: None)
            trv = bk_tr.bf[:, 0:640].rearrange("p (c t) -> p c t", c=5)
            for c in range(5):
                PE(lambda c=c: nc.tensor.transpose(trv[:, c, :], xbc_bf[:, c, cols], C("ident")),
                   [b_xbcbf[c], b_cst], [bk_tr.buf], inc=(c == 4))
            DVE(lambda: nc.vector.tensor_tensor(out=xc[:].rearrange("p (h d) -> p h d", h=8),
                                                in0=bk_tr.bf[:, 0:512].rearrange("p (h d) -> p h d", h=8),
                                                in1=dts[:, 1, j, :].unsqueeze(2).to_broadcast([128, 8, 64]), op=ALU.mult),
                [bk_tr.buf, b_dts], [b_xc])
            ACT(lambda: nc.scalar.copy(out=Btok[:], in_=bk_tr.bf[:, 512:640]), [bk_tr.buf], [b_Btok])
            tk(1.4)
            DVE(lambda: nc.vector.tensor_copy(out=abc[:], in_=a_bf[:, j, :].unsqueeze(2).to_broadcast([128, 8, 128])),
                [b_abf], [b_abc])
            DVE(lambda: nc.vector.tensor_tensor(out=AT1[:], in0=abc[:],
                                                in1=C("tri_le").unsqueeze(1).to_broadcast([128, 8, 128]), op=ALU.mult),
                [b_abc, b_cst], [b_AT1])
            tk(1.4)
            mm(bk_sm.ap[:, 0:8], C("triU"), a_bf[:, j, :], True, True, [b_cst, b_abf], [bk_sm.buf], inc=False)
            for c in range(2):
                for g in range(2):
                    mm(bk_sm.ap[64 * g:64 * g + 64, 8 + 4 * c:12 + 4 * c], C("ones", 64 * c, 64 * c + 64, 0, 64),
                       a_bf[64 * c:64 * c + 64, j, 4 * g:4 * g + 4], True, True, [b_cst, b_abf], [bk_sm.buf],
                       inc=(g == 1), serial=(c == 1 and g == 0))
            ACT(lambda: nc.scalar.activation(out=dcd[:, 0:16], in_=bk_sm.ap[:, 0:16], func=AF.Exp), [bk_sm.buf], [b_dcd])
            DVE(lambda: nc.vector.tensor_tensor(out=xcd[:].rearrange("p (h d) -> p h d", h=8),
                                                in0=xc[:].rearrange("p (h d) -> p h d", h=8),
                                                in1=dcd[:, 0:8].unsqueeze(2).to_broadcast([128, 8, 64]), op=ALU.mult),
                [b_xc, b_dcd], [b_xcd])
            tk(1.4)
            E_sb = scrB[:].rearrange("p (h l) -> p h l", h=8)
            for half, bk in ((0, bk_D0), (1, bk_D1)):
                hs = slice(4 * half, 4 * half + 4)
                o = bk.ap.rearrange("p (h l) -> p h l", h=4)
                mm(o, C("ones"), AT1[:, hs, :], True, False, [b_cst, b_AT1], [bk.buf], inc=False)
                mm(o, C("negtri_le"), abc[:, hs, :], False, False, [b_cst, b_abc], [bk.buf], inc=False)
                mm(bk.ap, C("ident"), negmaskD, False, True, [b_cst], [bk.buf])
                ACT(lambda hs=hs, o=o: nc.scalar.activation(out=E_sb[:, hs, :], in_=o, func=AF.Exp),
                    [bk.buf], [b_scrB[half]])
            tk(1.4)
            Gv = bk_G.ap[:, 0:256].rearrange("p (g l) -> p g l", g=2)
            for g in range(2):
                mm(Gv[:, g, :], xbc_bf[64 * g:64 * g + 64, 4, cols], xbc_bf[64 * g:64 * g + 64, 5, cols], True, True,
                   [b_xbcbf[4], b_xbcbf[5]], [bk_G.buf], inc=True, serial=(g == 1))
            for g in range(2):
                DVE(lambda g=g: nc.vector.tensor_tensor(out=M_bf[:, 4 * g:4 * g + 4, :], in0=E_sb[:, 4 * g:4 * g + 4, :],
                                                        in1=Gv[:, g, :].unsqueeze(1).to_broadcast([128, 4, 128]),
                                                        op=ALU.mult), [b_scrB[g], bk_G.buf], [b_M])
            tk(1.4)
            abcf = abc[:].rearrange("p h l -> p (h l)")
            eav = bk_ea.ap.rearrange("p (h l) -> p h l", h=4)
            for hp in range(4):
                mm(eav[:, hp, :], abcf[:, 2 * hp * 128 + 64: 2 * hp * 128 + 192], C("trichunk"), True, True,
                   [b_abc, b_cst], [bk_ea.buf], inc=(hp == 3))
            ACT(lambda: nc.scalar.activation(out=eaT[:], in_=eav, func=AF.Exp), [bk_ea.buf], [b_eaT])
            tk(1.4)
            ydv = bk_yd.ap.rearrange("p (h l) -> p h l", h=4)
            for hp in range(4):
                for e in range(2):
                    hd = 2 * hp + e
                    mm(ydv[64 * e:64 * e + 64, hp, :], xc[:, hd * 64:(hd + 1) * 64], M_bf[:, hd, :], True, True,
                       [b_xc, b_M], [bk_yd.buf], inc=(hp == 3 and e == 1))
            tk(1.4)
            yov = bk_yo.ap.rearrange("p (h l) -> p h l", h=4)
            for c in range(2):
                si = st_in[c]
                ccols = slice(j * 128 + 64 * c, j * 128 + 64 * c + 64)
                for hp in range(4):
                    g = hp // 2
                    mm(yov[:, hp, 64 * c:64 * c + 64], hTb[si][64 * g:64 * g + 64, (hp % 2) * 128:(hp % 2) * 128 + 128],
                       xbc_bf[64 * g:64 * g + 64, 5, ccols], True, True, [b_hTb[si], b_xbcbf[5]], [bk_yo.buf],
                       inc=(hp % 2 == 1), serial=(hp == 2))
                tk(1.4)
                so = st_out[c]
                stv = bk_tr.ap[:, 0:256]
                for g in range(2):
                    mm(stv[64 * g:64 * g + 64, :], Btok[64 * c:64 * c + 64, 64 * g:64 * g + 64],
                       xcd[64 * c:64 * c + 64, 256 * g:256 * g + 256], True, True, [b_Btok, b_xcd], [bk_tr.buf],
                       inc=(g == 1))
                DVE(lambda si=si, c=c: nc.vector.tensor_tensor(
                    out=htmp[:].rearrange("p (h d) -> p h d", h=4), in0=hT[si][:].rearrange("p (h d) -> p h d", h=4),
                    in1=dcd[:, 8 + 4 * c:12 + 4 * c].unsqueeze(2).to_broadcast([128, 4, 64]), op=ALU.mult),
                    [b_hT[si], b_dcd], [b_htmp])
                DVE(lambda so=so: nc.vector.tensor_tensor(out=hT[so][:], in0=htmp[:], in1=stv, op=ALU.add),
                    [b_htmp, bk_tr.buf], [b_hT[so]])
                ACT(lambda so=so: nc.scalar.copy(out=hTb[so][:], in_=hT[so][:]), [b_hT[so]], [b_hTb[so]])
            tk(1.4)
            DVE(lambda: nc.vector.tensor_tensor(out=ytmp[:], in0=yov, in1=eaT[:], op=ALU.mult),
                [bk_yo.buf, b_eaT], [b_ytmp])
            DVE(lambda: nc.vector.tensor_tensor(out=ytmp[:], in0=ydv, in1=ytmp[:], op=ALU.add),
                [bk_yd.buf, b_ytmp], [b_ytmp])
            DVE(lambda: nc.vector.tensor_tensor(out=scrA[:, :, cols], in0=scrA[:, :, cols], in1=ytmp[:], op=ALU.add),
                list(b_scrA) + [b_ytmp], list(b_scrA))

        def ssd_prepare_dt(nsub):
            ACT(lambda: nc.scalar.activation(out=dts[:, 0, 0:nsub, :], in_=dts[:, 0, 0:nsub, :], func=AF.Exp),
                [b_dts], [b_dts])
            ACT(lambda: nc.scalar.activation(out=dts[:, 1, 0:nsub, :], in_=dts[:, 0, 0:nsub, :], func=AF.Ln, bias=1.0),
                [b_dts], [b_dts])
            DVE(lambda: nc.vector.tensor_tensor(out=dts[:, 2, 0:nsub, :], in0=dts[:, 1, 0:nsub, :],
                                                in1=aneg[:].unsqueeze(1).to_broadcast([128, nsub, 8]), op=ALU.mult),
                [b_dts, b_aneg], [b_dts])
            DVE(lambda: nc.vector.tensor_copy(out=a_bf[:, 0:nsub, :], in_=dts[:, 2, 0:nsub, :]), [b_dts], [b_abf])

        def ssd_gate_norm(T, tick=None):
            for g in range(2):
                if tick:
                    tick(1.5)
                bk = nextbank()
                for cc in range(2):
                    c = 2 * g + cc
                    DVE(lambda c=c: nc.vector.tensor_tensor(out=scrA[:, c, 0:T], in0=scrA[:, c, 0:T], in1=zT[:, c, 0:T],
                                                            op=ALU.mult), [b_scrA[c], b_z[c]], [b_scrA[c]])
                    ACT(lambda c=c, cc=cc: nc.scalar.activation(out=sq_bf[:, cc, 0:T], in_=scrA[:, c, 0:T],
                                                                func=AF.Square), [b_scrA[c]], [b_sq[cc]])
                    mm(bk.ap[:, 0:T], C("ones"), sq_bf[:, cc, 0:T], cc == 0, cc == 1, [b_cst, b_sq[cc]], [bk.buf])
                rn = scrB[:, 0:512]
                ACT(lambda: nc.scalar.activation(out=rn[:, 0:T], in_=bk.ap[:, 0:T], func=AF.Ln, scale=1.0 / 256, bias=EPS),
                    [bk.buf], [b_scrB[0]])
                ACT(lambda: nc.scalar.activation(out=rn[:, 0:T], in_=rn[:, 0:T], func=AF.Exp, scale=-0.5),
                    [b_scrB[0]], [b_scrB[0]])
                for cc in range(2):
                    c = 2 * g + cc
                    DVE(lambda c=c: nc.vector.scalar_tensor_tensor(out=mixT[:, 4 + c, 0:T], in0=scrA[:, c, 0:T],
                                                                   scalar=prm[:, P_SNW + c:P_SNW + c + 1], in1=rn[:, 0:T],
                                                                   op0=ALU.mult, op1=ALU.mult),
                        [b_scrA[c], b_prm, b_scrB[0]], [b_mix[4 + c]])

        class It:
            pass

        def attn_pipeline(its, tick=None):
            N = len(its)

            def S0(it, s):
                zb = banks[it.zb]
                nq = len(it.qk)
                for n_, (o, l, r) in enumerate(it.qk):
                    mm(o, l, r, True, True, it.qk_reads, [zb.buf], inc=(n_ == nq - 1))
                rows = slice(it.p0, it.p0 + it.nk)
                e = e_sb[s % NE]
                sp = sp_sb[s % NSP]
                ACT(lambda: nc.scalar.activation(out=e[rows, it.c0:it.c1], in_=zb.ap[rows, it.c0:it.c1], func=AF.Exp),
                    [zb.buf], [b_e[s % NE]])
                ACT(lambda: nc.scalar.activation(out=sp[rows, it.c0:it.c1], in_=e[rows, it.c0:it.c1], func=AF.Ln, bias=1.0),
                    [b_e[s % NE]], [b_sp[s % NSP]])
                if it.diag is not None:
                    vw, mk = it.diag
                    POOL(lambda: nc.gpsimd.tensor_tensor(out=vw(sp), in0=vw(sp), in1=mk, op=ALU.mult),
                         [b_sp[s % NSP], b_cst], [b_sp[s % NSP]])

            def S1a(it, s):
                cb = banks[it.cb]
                rows = slice(it.p0, it.p0 + it.nk)
                e = e_sb[s % NE]
                sp = sp_sb[s % NSP]
                g = g_sb[s % NG]
                A = A_sb[s % NA]
                mm(cb.ap[:, it.c0:it.c1], it.L1, sp[rows, it.c0:it.c1], it.first, False, [b_sp[s % NSP], b_cst], [cb.buf])
                ACT(lambda: nc.scalar.activation(out=g[rows, it.c0:it.c1], in_=cb.ap[rows, it.c0:it.c1], func=AF.Exp),
                    [cb.buf], [b_g[s % NG]])
                DVE(lambda: nc.vector.tensor_tensor(out=A[rows, it.c0:it.c1], in0=e[rows, it.c0:it.c1],
                                                    in1=g[rows, it.c0:it.c1], op=ALU.mult),
                    [b_e[s % NE], b_g[s % NG]], [b_A[s % NA]])
                if it.diag is not None:
                    vw, mk = it.diag
                    POOL(lambda: nc.gpsimd.tensor_tensor(out=vw(A), in0=vw(A), in1=mk, op=ALU.mult),
                         [b_A[s % NA], b_cst], [b_A[s % NA]])

            def S2(it, s):
                cb = banks[it.cb]
                rows = slice(it.p0, it.p0 + it.nk)
                sp = sp_sb[s % NSP]
                A = A_sb[s % NA]
                if not it.last:
                    mm(cb.ap[:, it.c0:it.c1], it.L2, sp[rows, it.c0:it.c1], False, True, [b_sp[s % NSP], b_cst], [cb.buf])
                ob = banks[it.ob]
                for n_, (o, l, acols, st) in enumerate(it.av):
                    mm(o, l, A[rows, acols], st, it.last, it.av_reads + [b_A[s % NA]], [ob.buf],
                       inc=(n_ == len(it.av) - 1))
                if it.fin is not None:
                    it.fin()

            for step in range(N + 4):
                if step < N:
                    S0(its[step], step)
                if 0 <= step - 4 < N:
                    S2(its[step - 4], step - 4)
                if 0 <= step - 2 < N:
                    S1a(its[step - 2], step - 2)
                if tick:
                    tick(max(0.3, back["left"] / max(1, N + 4 - step)))

        def attn_prompt(Q, T, tick=None):
            its = []
            nkb = 4 * Q + 4
            cnt = 0
            for hp in range(4):
                for i in range(nkb):
                    kb = nkb - 1 - i
                    for e in range(2):
                        it = It()
                        it.zb = cnt % 2
                        it.cb = 2 + e
                        it.ob = 4 + e
                        cnt += 1
                        it.nk = 128
                        it.p0 = 0
                        d = kb - 4 * Q
                        it.c0 = 128 * d if d >= 0 else 0
                        it.c1 = T
                        pr = slice(64 * e, 64 * e + 64)
                        it.qk = [(banks[it.zb].ap[:, it.c0:it.c1], kT[:, hp, kb * 128:(kb + 1) * 128],
                                  qpad[e][:, hp, it.c0:it.c1])]
                        it.qk_reads = [b_kT[0], b_kT[1], b_qT]
                        it.first = (i == 0)
                        it.last = (kb == 0)
                        it.L1 = C("negLinc")
                        it.L2 = C("negLstr")
                        if d >= 0:
                            c0 = it.c0
                            it.diag = ((lambda t, c0=c0: t[:, c0:c0 + 128]), C("amask"))
                        else:
                            it.diag = None
                        it.av = [(banks[it.ob].ap[:, it.c0:it.c1],
                                  v_bf[:, kb, hp * 128:(hp + 1) * 128], slice(it.c0, it.c1), i == 0)]
                        it.av_reads = [b_v[0], b_v[1]]
                        it.fin = None
                        if it.last:
                            def fin(hp=hp, obi=it.ob, pr=pr):
                                ACT(lambda: nc.scalar.copy(out=mixT[pr, hp, 0:T], in_=banks[obi].ap[pr, 0:T]),
                                    [banks[obi].buf], [b_mix[hp]])
                            it.fin = fin
                        its.append(it)
            attn_pipeline(its, tick)

        def attn_sample_pair(qa, qb, npast_blk, tick=None):
            its = []
            cnt = 0
            nblk = npast_blk + 1
            for i in range(nblk):
                for sl, q in ((0, qa), (1, qb)):
                    it = It()
                    it.zb = cnt % 2
                    it.cb = 2 + sl
                    it.ob = 4 + sl
                    cnt += 1
                    e = q % 2
                    j = q // 2
                    qcols = slice(q * 64, q * 64 + 64)
                    it.c0 = 0
                    it.c1 = 512
                    it.first = (i == 0)
                    it.last = (i == nblk - 1)
                    zb = banks[it.zb]
                    ob = banks[it.ob]
                    obv = ob.ap[:, 0:256].rearrange("p (h t) -> p h t", h=4)
                    if i == 0:
                        it.nk = 64
                        it.p0 = 64 * e
                        rows = slice(64 * e, 64 * e + 64)
                        it.qk = [(zb.ap[rows, h * 64:(h + 1) * 64],
                                  kTn[:, h // 2, qcols],
                                  qpad[h % 2][:, h // 2, qcols]) for h in range(8)]
                        it.qk_reads = [b_kTn, b_qT]
                        it.L1 = C("negLinc_blk", 64 * e, 64 * e + 64)
                        it.L2 = C("negL2", 64 * e, 64 * e + 64)
                        it.diag = ((lambda t, rows=rows: t[rows, :].rearrange("p (h t) -> p h t", h=8)),
                                   C("amask_s", 64 * e, 64 * e + 64, 0, 64).unsqueeze(1).to_broadcast([64, 8, 64]))
                        it.av = [(obv[64 * (h % 2):64 * (h % 2) + 64, h // 2, :], vnew[rows, j, h * 64:(h + 1) * 64],
                                  slice(h * 64, (h + 1) * 64), h < 2) for h in range(8)]
                        it.av_reads = [b_vnew]
                    else:
                        kb = npast_blk - i
                        it.nk = 128
                        it.p0 = 0
                        kc = slice(sl * 1024 + kb * 128, sl * 1024 + (kb + 1) * 128)
                        it.qk = [(zb.ap[:, h * 64:(h + 1) * 64],
                                  kT[:, h // 2, kc],
                                  qpad[h % 2][:, h // 2, qcols]) for h in range(8)]
                        it.qk_reads = [b_kT[sl], b_qT]
                        it.L1 = C("negLinc")
                        it.L2 = C("negLstr")
                        it.diag = None
                        it.av = [(obv[64 * (h % 2):64 * (h % 2) + 64, h // 2, :], v_bf[:, sl * 8 + kb, h * 64:(h + 1) * 64],
                                  slice(h * 64, (h + 1) * 64), False) for h in range(8)]
                        it.av_reads = [b_v[sl]]
                    it.fin = None
                    if it.last:
                        def fin(q=q, obi=it.ob):
                            ACT(lambda: nc.scalar.copy(
                                out=mixT[:, 0:4, q * 64:(q + 1) * 64],
                                in_=banks[obi].ap[:, 0:256].rearrange("p (h t) -> p h t", h=4)),
                                [banks[obi].buf], [b_mix[0], b_mix[1], b_mix[2], b_mix[3]])
                        it.fin = fin
                    its.append(it)
            attn_pipeline(its, tick)

        def load_past(sl, q, npast_blk):
            ktok = uT[:].rearrange("p a b c -> p (a b c)")[:, 0:npast_blk * 512].rearrange("p (b c) -> p b c", c=512)
            fw.dma(fw.pool, ktok, ck[q].rearrange("(b p) c -> p b c", p=128), writes=[b_uT[0], b_uT[1]])
            fw.dma(fw.pool, v_bf[:, sl * 8:sl * 8 + npast_blk, :], cv[q].rearrange("(b p) c -> p b c", p=128),
                   writes=[b_v[sl]])
            for kb in range(npast_blk):
                bk = nextbank()
                v3 = bk.bf[:, 0:512].rearrange("p (c t) -> p c t", c=4)
                for hp in range(4):
                    PE(lambda hp=hp, kb=kb, v3=v3: nc.tensor.transpose(v3[:, hp, :], ktok[:, kb, hp * 128:(hp + 1) * 128],
                                                                       C("ident")),
                       [b_uT[0], b_uT[1], b_cst], [bk.buf], inc=(hp == 3))
                DVE(lambda kb=kb, v3=v3: nc.vector.tensor_copy(
                    out=kT[:, :, sl * 1024 + kb * 128: sl * 1024 + (kb + 1) * 128], in_=v3), [bk.buf], [b_kT[sl]])

        def out_proj_norm2(T, nsub, X, Xn=None, nsub_n=0):
            w0, wb0 = ws_get("wo0")
            w1, wb1 = ws_get("wo1", hold=1)
            for j in range(max(nsub, nsub_n) + 2):
                if j < nsub:
                    for dh, (w, wb) in enumerate(((w0, wb0), (w1, wb1))):
                        bk = nextbank()
                        for c in range(8):
                            mm(bk.ap[:, :], mixT[:, c, j * 128:(j + 1) * 128], w[:, c, :], c == 0, c == 7,
                               [wb, b_mix[c]], [bk.buf], inc=(c == 7))
                        DVE(lambda j=j, dh=dh, bk=bk: nc.vector.tensor_tensor(
                            out=X.x[:, j, dh * 512:(dh + 1) * 512], in0=X.x[:, j, dh * 512:(dh + 1) * 512], in1=bk.ap[:, :],
                            op=ALU.add), [X.bx[j], bk.buf], [X.bx[j]])
                if Xn is not None and j < nsub_n:
                    stg, bstg = norm1_stage(j)
                    norm_a(j, Xn, stg, bstg, 2)
                    norm_b(j, P_N1W, Xn, stg, bstg)
                if 0 <= j - 1 < nsub:
                    norm_a(j - 1, X, xs_bf[(j - 1) % 2], b_xsbf[(j - 1) % 2], (j - 1) % 2)
                if 0 <= j - 2 < nsub:
                    norm_b(j - 2, P_N2W, X, xs_bf[j % 2], b_xsbf[j % 2])

        def mlp_gen(T, nsub, X, y_rows):
            xnT = X.xnT

            def up(p):
                w, wb = ws_get("up%d" % p)
                for fc in range(4):
                    bk = nextbank_mlp()
                    for kc in range(8):
                        mm(bk.ap[:, 0:T], w[:, kc, fc * 128:(fc + 1) * 128], xnT[:, kc, 0:T], kc == 0, kc == 7,
                           [wb, X.bxn], [bk.buf], inc=(kc == 7))
                    r = rtmp[:, fc % 2, 0:T]
                    ACT(lambda bk=bk, r=r: nc.scalar.activation(out=r, in_=bk.ap[:, 0:T], func=AF.Relu),
                        [bk.buf], [b_rtmp[fc % 2]])
                    eng, h = (DVE, nc.vector) if fc % 2 == 0 else (POOL, nc.gpsimd)
                    eng(lambda r=r, fc=fc, h=h: h.tensor_tensor(out=uT[:, p % 2, fc, 0:T], in0=r, in1=r, op=ALU.mult),
                        [b_rtmp[fc % 2]], [b_uT[p % 2]])
                    yield 2.1 * T / 512
                yield "P"

            def down(p):
                w, wb = ws_get("dn%d" % p)
                for j in range(nsub):
                    for dh in range(2):
                        bk = nextbank_mlp()
                        for fcl in range(4):
                            mm(bk.ap[:, :], uT[:, p % 2, fcl, j * 128:(j + 1) * 128], w[:, fcl, dh * 512:(dh + 1) * 512],
                               fcl == 0, fcl == 3, [wb, b_uT[p % 2]], [bk.buf], inc=(fcl == 3))
                        DVE(lambda j=j, dh=dh, bk=bk: nc.vector.tensor_tensor(
                            out=X.x[:, j, dh * 512:(dh + 1) * 512], in0=X.x[:, j, dh * 512:(dh + 1) * 512],
                            in1=bk.ap[:, :], op=ALU.add), [X.bx[j], bk.buf], [X.bx[j]])
                        yield 1.05
                yield "P"

            yield from up(0)
            for p in range(1, 8):
                yield from up(p)
                yield from down(p - 1)
            yield from down(7)
            DVE(lambda: nc.vector.memset(stat[:, 12:16], 0.0), [], [b_stat])
            for j in range(nsub):
                ACT(lambda j=j: nc.scalar.activation(out=rtmp[:].rearrange("p a b -> p (a b)").bitcast(BF16)[:, 0:D],
                                                     in_=X.x[:, j, :], func=AF.Square,
                                                     accum_out=stat[:, 12 + j:13 + j]), [X.bx[j]], [b_rtmp[0], b_stat],
                    strict=True)
                yield 0.5
            ACT(lambda: nc.scalar.activation(out=stat[:, 12:12 + nsub], in_=stat[:, 12:12 + nsub], func=AF.Ln,
                                             scale=1.0 / D, bias=EPS), [b_stat], [b_stat])
            ACT(lambda: nc.scalar.activation(out=stat[:, 12:12 + nsub], in_=stat[:, 12:12 + nsub], func=AF.Exp,
                                             scale=-0.5), [b_stat], [b_stat])
            for j in range(nsub):
                DVE(lambda j=j: nc.vector.scalar_tensor_tensor(out=X.x[:, j, :], in0=X.x[:, j, :],
                                                               scalar=stat[:, 12 + j:13 + j], in1=fnw[:],
                                                               op0=ALU.mult, op1=ALU.mult),
                    [X.bx[j], b_stat, b_fnw], [X.bx[j]])
                fw.dma(fw.sp, y_rows(j), X.x[:, j, :], reads=[X.bx[j]])
                yield 0.5

        def state_out(si, dst):
            bk = nextbank()
            for hl in range(4):
                PE(lambda hl=hl: nc.tensor.transpose(bk.ap[0:64, hl * 128:(hl + 1) * 128], hT[si][:, hl * 64:(hl + 1) * 64],
                                                     identf[:, :]), [b_hT[si], b_identf], [bk.buf], inc=(hl == 3))
            DVE(lambda: nc.vector.tensor_copy(out=stT[:].rearrange("p a b -> p (a b)"), in_=bk.ap[0:64, :]),
                [bk.buf], [b_stT])
            for g in range(2):
                fw.dma(fw.sp, dst[4 * g:4 * g + 4].rearrange("h p n -> p h n"), stT[:, :, 64 * g:64 * g + 64],
                       reads=[b_stT])

        def state_in(si, src):
            for g in range(2):
                fw.dma(fw.sp, stT[:, :, 64 * g:64 * g + 64], src[4 * g:4 * g + 4].rearrange("h p n -> p h n"),
                       writes=[b_stT])
            bk = nextbank()
            for hl in range(4):
                PE(lambda hl=hl: nc.tensor.transpose(bk.ap[:, hl * 64:(hl + 1) * 64], stT[:, hl, :], identf[0:64, 0:64]),
                   [b_stT, b_identf], [bk.buf], inc=(hl == 3))
            DVE(lambda: nc.vector.tensor_copy(out=hT[si][:], in_=bk.ap[:, 0:256]), [bk.buf], [b_hT[si]])
            ACT(lambda: nc.scalar.copy(out=hTb[si][:], in_=bk.ap[:, 0:256]), [bk.buf], [b_hTb[si]])

        def conv_state_out(nseg, L, dst_of_seg):
            for s in range(nseg):
                for c in range(6):
                    fw.dma(fw.sp, dst_of_seg(s)[:, c * 128:(c + 1) * 128].rearrange("r p -> p r"),
                           xbcT[:, c, s * (3 + L) + L: s * (3 + L) + L + 3], reads=[b_xbcT[c]],
                           allow_slow_non_contiguous=True)

        T = 512
        back = {"gen": None, "credit": 0.0, "left": 0.0}

        def tick(us):
            g = back["gen"]
            if g is None:
                return
            back["credit"] += us
            while back["credit"] > 0:
                try:
                    v = next(g)
                    if v != "P":
                        back["credit"] -= v
                        back["left"] -= v
                except StopIteration:
                    back["gen"] = None
                    back["credit"] = 0.0
                    return

        def tick_pieces(k):
            g = back["gen"]
            if g is None:
                return
            while k > 0:
                v = next(g)
                if v == "P":
                    k -= 1
                else:
                    back["left"] -= v

        def drain():
            g = back["gen"]
            if g is not None:
                for _ in g:
                    pass
            back["gen"] = None
            back["credit"] = 0.0

        tiles = []
        for b in range(NBP):
            for Q in range(SEQ // T):
                tiles.append(("p", b, Q))
        if NBS:
            tiles.append(("s", 0, 0))

        def load_x(n):
            kind, b, Q = tiles[n]
            X = XS[n % 2]
            if kind == "p":
                r0 = b * SEQ + Q * T
                fw.dma(fw.sp, X.x[:, :, :], xp[r0:r0 + T, :].rearrange("(j p) d -> p j d", p=128), writes=list(X.bx))
                return 4
            nsub_ = NBS // 2
            fw.dma(fw.sp, X.x[:, 0:nsub_, :], xs[:, :].rearrange("(j p) d -> p j d", p=128), writes=list(X.bx[0:nsub_]))
            return nsub_

        nsub0 = load_x(0)
        norm_T(nsub0, P_N1W, XS[0])
        for n, (kind, b, Q) in enumerate(tiles):
            X = XS[n % 2]
            if kind == "p":
                r0 = b * SEQ + Q * T
                tick_pieces(2)
                fw.inherit(conv_bufs, attn_bufs)
                if Q == 0:
                    DVE(lambda: nc.vector.memset(hT[0][:], 0.0), [], [b_hT[0]])
                    DVE(lambda: nc.vector.memset(hTb[0][:], 0.0), [], [b_hTb[0]])
                    DVE(lambda: nc.vector.memset(xbcT[:, :, 0:3], 0.0), [], list(b_xbcT))
                else:
                    DVE(lambda: nc.vector.tensor_copy(out=xbcT[:, :, 0:3], in_=cstate[:]), [b_cstate], list(b_xbcT))
                in_proj(T, 4, 1, T,
                        kT_dst=lambda cc, Q=Q: kT[:, cc, Q * T:(Q + 1) * T], kT_bufs=[b_kT[0], b_kT[1]],
                        k_rows=lambda j, r0=r0: kp[r0 + j * 128: r0 + (j + 1) * 128, :],
                        v_rows=lambda j, r0=r0: vp[r0 + j * 128: r0 + (j + 1) * 128, :],
                        v_dst=lambda j, Q=Q: v_bf[:, 4 * Q + j, :], v_bufs=[b_v[0], b_v[1]], X=X)
                DVE(lambda: nc.vector.tensor_copy(out=cstate[:], in_=xbcT[:, :, T:T + 3]), list(b_xbcT), [b_cstate])
                if Q == SEQ // T - 1:
                    conv_state_out(1, T, lambda s_, b=b: convp[b])
                conv_silu(T, 1, T, tick)
                ssd_prepare_dt(4)
                for j in range(4):
                    ssd_subtile(j, 4, (0, 0), True, (0, 0), tick)
                ssd_gate_norm(T, tick)
                if Q == SEQ // T - 1:
                    state_out(0, ssmp[b])
                fw.inherit(attn_bufs, conv_bufs)
                attn_prompt(Q, T, tick)
                drain()
                Tn, nsub = T, 4
                y_rows = (lambda j, r0=r0: yp[r0 + j * 128: r0 + (j + 1) * 128, :])
            else:
                Ts = NBS * 64
                nsub = NBS // 2
                npb = PAST // 128
                tick_pieces(2)
                fw.inherit(conv_bufs, attn_bufs)
                for q in range(NBS):
                    for c in range(6):
                        fw.dma(fw.sp, xbcT[:, c, q * 67:q * 67 + 3],
                               sconv[q][:, c * 128:(c + 1) * 128].rearrange("r p -> p r"),
                               writes=[b_xbcT[c]], allow_slow_non_contiguous=True)
                    state_in(q, sssm[q])
                in_proj(Ts, nsub, NBS, 64,
                        kT_dst=lambda cc: kTn[:, cc, 0:Ts], kT_bufs=[b_kTn],
                        k_rows=lambda j: ksm[j * 128:(j + 1) * 128, :],
                        v_rows=lambda j: vsm[j * 128:(j + 1) * 128, :],
                        v_dst=lambda j: vnew[:, j, :], v_bufs=[b_vnew], X=X)
                conv_state_out(NBS, 64, lambda s_: convs[s_])
                conv_silu(Ts, NBS, 64, tick)
                ssd_prepare_dt(nsub)
                for j in range(nsub):
                    ssd_subtile(j, nsub, (2 * j, 2 * j + 1), False, (2 * j, 2 * j + 1), tick)
                ssd_gate_norm(Ts, tick)
                for q in range(NBS):
                    state_out(q, ssms[q])
                fw.inherit(attn_bufs, conv_bufs)
                drain()
                for pr in range(NBS // 2):
                    load_past(0, 2 * pr, npb)
                    load_past(1, 2 * pr + 1, npb)
                    attn_sample_pair(2 * pr, 2 * pr + 1, npb)
                Tn = Ts
                y_rows = (lambda j: ys[j * 128:(j + 1) * 128, :])
            if n + 1 < len(tiles):
                nsub_n = load_x(n + 1)
                out_proj_norm2(Tn, nsub, X, XS[(n + 1) % 2], nsub_n)
            else:
                out_proj_norm2(Tn, nsub, X)
            back["gen"] = mlp_gen(Tn, nsub, X, y_rows)
            back["left"] = 140.0 * Tn / 512
        drain()

        fw.finish()
        build.nins = fw.nins
    return nc


def pack_params(norm1_w, norm2_w, conv_w, conv_b, dt_bias, a_log, d_skip, ssm_norm_w):
    prm = np.zeros((128, NPRM), np.float32)
    prm[:, P_N1W:P_N1W + 8] = norm1_w.reshape(8, 128).T
    prm[:, P_N2W:P_N2W + 8] = norm2_w.reshape(8, 128).T
    prm[:, P_CW:P_CW + 24] = conv_w.reshape(4, 6, 128).transpose(2, 1, 0).reshape(128, 24)
    prm[:, P_CB:P_CB + 6] = conv_b.reshape(6, 128).T
    prm[:, P_DSK:P_DSK + 4] = np.repeat(d_skip, 64).reshape(4, 128).T
    prm[:, P_SNW:P_SNW + 4] = ssm_norm_w.reshape(4, 128).T
    prm[:, P_DTB:P_DTB + 8] = np.broadcast_to(dt_bias.reshape(1, 8), (128, 8))
    prm[:, P_ALOG:P_ALOG + 8] = np.broadcast_to(a_log.reshape(1, 8), (128, 8))
    return prm


def make_in_maps(inputs, ncores, NBP, NBS):
    f = lambda a: np.ascontiguousarray(np.asarray(a, dtype=np.float32))
    x_prompt = f(inputs["x_prompt"])
    x_sample = f(inputs["x_sample"])
    cache_k = f(inputs["cache_k"])[0]
    cache_v = f(inputs["cache_v"])[0]
    state_conv = f(inputs["state_conv"])[0]
    state_ssm = f(inputs["state_ssm"])[0]
    SEQ = x_prompt.shape[1]
    PAST = cache_k.shape[1]
    prm = pack_params(f(inputs["norm1_w"])[0], f(inputs["norm2_w"])[0], f(inputs["conv_w"])[0], f(inputs["conv_b"])[0],
                      f(inputs["dt_bias"])[0], f(inputs["a_log"])[0], f(inputs["d_skip"])[0], f(inputs["ssm_norm_w"])[0])
    fnw = np.ascontiguousarray(np.broadcast_to(f(inputs["final_norm_w"]).reshape(1, D), (128, D)))
    shared = {
        "w_in": f(inputs["w_in"])[0], "w_out": f(inputs["w_out"])[0], "w_up": f(inputs["w_up"])[0],
        "w_down": f(inputs["w_down"])[0], "prm": prm, "fnw": fnw, "cst": make_consts(),
        "identf": np.eye(128, dtype=np.float32),
    }
    maps = []
    for c in range(ncores):
        m = dict(shared)
        m["xp"] = np.ascontiguousarray(x_prompt[c * NBP:(c + 1) * NBP].reshape(NBP * SEQ, D))
        m["xs"] = np.ascontiguousarray(x_sample[c * NBS:(c + 1) * NBS].reshape(NBS * 64, D))
        m["ck"] = np.ascontiguousarray(cache_k[c * NBS:(c + 1) * NBS].reshape(NBS, PAST, 512))
        m["cv"] = np.ascontiguousarray(cache_v[c * NBS:(c + 1) * NBS].reshape(NBS, PAST, 512))
        m["sconv"] = np.ascontiguousarray(state_conv[c * NBS:(c + 1) * NBS])
        m["sssm"] = np.ascontiguousarray(state_ssm[c * NBS:(c + 1) * NBS])
        maps.append(m)
    return maps


def gather(results, NBP, SEQ, NBS):
    cat = lambda k: np.concatenate([np.asarray(r[k]) for r in results], axis=0)
    nb = NBP * len(results)
    ns = NBS * len(results)
    y_prompt = cat("yp").reshape(nb, SEQ, D)
    y_sample = cat("ys").reshape(ns, 64, D)
    k_prompt = cat("kp").reshape(1, nb, SEQ, 8, 64)
    v_prompt = cat("vp").reshape(1, nb, SEQ, 8, 64)
    conv_prompt = cat("convp").reshape(1, nb, 3, 768)
    ssm_prompt = cat("ssmp").reshape(1, nb, 8, 64, 64)
    k_sample = cat("ksm").reshape(1, ns, 64, 8, 64)
    v_sample = cat("vsm").reshape(1, ns, 64, 8, 64)
    conv_sample = cat("convs").reshape(1, ns, 3, 768)
    ssm_sample = cat("ssms").reshape(1, ns, 8, 64, 64)
    return tuple(np.ascontiguousarray(a.astype(np.float32)) for a in (
        y_prompt, y_sample, k_prompt, v_prompt, conv_prompt, ssm_prompt, k_sample, v_sample, conv_sample, ssm_sample))


def kernel(**inputs):
    NBP, NBS = 4, 4
    SEQ = int(np.asarray(inputs["x_prompt"]).shape[1])
    PAST = int(np.asarray(inputs["cache_k"]).shape[2])
    nc = build(NBP=NBP, SEQ=SEQ, NBS=NBS, PAST=PAST)
    in_maps = make_in_maps(inputs, NCORES, NBP, NBS)
    res = run_bass_kernel_spmd(nc, in_maps, core_ids=list(range(NCORES)))
    return gather(res.results, NBP, SEQ, NBS)
```

```python
import numpy as np
from contextlib import ExitStack
import concourse.bass as bass
import concourse.mybir as mybir
from concourse.bass_utils import run_bass_kernel_spmd

F32 = mybir.dt.float32
BF16 = mybir.dt.bfloat16
AF = mybir.ActivationFunctionType
ALU = mybir.AluOpType

NCORES = 8
D = 1024
INP = 2824
DFF = 4096
EPS = 1e-5
NEG = -30000.0


class Buf:
    __slots__ = ("name", "w", "r", "excl")

    def __init__(self, name, excl=False):
        self.name = name
        self.w = None
        self.r = {}
        self.excl = excl


class Eng:
    def __init__(self, name, h, sem):
        self.name = name
        self.h = h
        self.sem = sem
        self.count = 0
        self.seen = {}


class FW:
    def __init__(self, nc, es, n_dma_sems=40):
        self.nc = nc
        self.sems = {}

        def mk(name):
            s = es.enter_context(nc.semaphore(name))
            self.sems[name] = s
            return s

        self.pe = Eng("pe", nc.tensor, mk("pe"))
        self.act = Eng("act", nc.scalar, mk("act"))
        self.dve = Eng("dve", nc.vector, mk("dve"))
        self.pool = Eng("pool", nc.gpsimd, mk("pool"))
        self.sp = Eng("sp", nc.sync, mk("sp"))
        self.engs = {e.name: e for e in (self.pe, self.act, self.dve, self.pool, self.sp)}
        self.dsem = [mk("d%d" % i) for i in range(n_dma_sems)]
        self.dcnt = [0] * n_dma_sems
        self.n_sw = 8
        self.dnext = {"hw": 0, "sw": 0}
        self.nins = 0

    def _need(self, eng, tok, needs):
        if tok is None:
            return
        key, val, _ = tok
        if eng.seen.get(key, 0) >= val:
            return
        if needs.get(key, 0) < val:
            needs[key] = val

    def _waits(self, eng, reads, writes, is_dma=False, strict=False):
        needs = {}
        for b in reads:
            self._need(eng, b.w, needs)
            if b.excl:
                for k, t in b.r.items():
                    if k != eng.name:
                        self._need(eng, t, needs)
        for b in writes:
            if b.w is not None and (is_dma or strict or b.w[2] != eng.name):
                self._need(eng, b.w, needs)
            for k, t in b.r.items():
                if is_dma or k != eng.name:
                    self._need(eng, t, needs)
        for key, val in needs.items():
            if key in self.engs:
                assert val <= self.engs[key].count, "wait on future inc %s %d > %d" % (key, val, self.engs[key].count)
            eng.h.wait_ge(self.sems[key], val)
            eng.seen[key] = val
            self.nins += 1

    def op(self, eng, fn, reads=(), writes=(), inc=True, strict=False):
        self._waits(eng, reads, writes, strict=strict)
        ins = fn()
        self.nins += 1
        if inc:
            ins.then_inc(eng.sem, 1)
            eng.count += 1
            tok = (eng.name, eng.count, eng.name)
        else:
            tok = (eng.name, eng.count + 1, eng.name)
        for b in reads:
            b.r[eng.name] = tok
        for b in writes:
            b.w = tok
            b.r = {}
        return ins

    def dma(self, eng, out, in_, reads=(), writes=(), **kw):
        if eng.name == "pool":
            i = self.dnext["sw"]
            self.dnext["sw"] = (i + 1) % self.n_sw
        else:
            i = self.n_sw + self.dnext["hw"]
            self.dnext["hw"] = (self.dnext["hw"] + 1) % (len(self.dsem) - self.n_sw)
        key = "d%d" % i
        prev = self.dcnt[i] * 16
        if prev and eng.seen.get(key, 0) < prev:
            eng.h.wait_ge(self.dsem[i], prev)
            eng.seen[key] = prev
            self.nins += 1
        self._waits(eng, reads, writes, is_dma=True)
        ins = eng.h.dma_start(out=out, in_=in_, **kw)
        self.nins += 1
        ins.then_inc(self.dsem[i], 16)
        self.dcnt[i] += 1
        tok = (key, self.dcnt[i] * 16, "dma%d_%d" % (i, self.dcnt[i]))
        for b in reads:
            b.r[tok[2]] = tok
        for b in writes:
            b.w = tok
            b.r = {}
        return tok

    def inherit(self, new_bufs, old_bufs):
        toks = {}
        for ob in old_bufs:
            if ob.w is not None:
                toks["w_" + ob.w[2] + ob.name] = ob.w
            for k, t in ob.r.items():
                toks[k + "_" + ob.name] = t
        for nb in new_bufs:
            nb.w = None
            nb.r = dict(toks)

    def finish(self):
        for i, s in enumerate(self.dsem):
            if self.dcnt[i]:
                self.sp.h.wait_ge(s, self.dcnt[i] * 16)


CST_NAMES = ["ident", "ones", "negLinc", "negLstr", "amask", "tri_le", "negtri_le", "trichunk", "triU",
             "negLinc_blk", "negL2", "amask_s"]
NCST = len(CST_NAMES) * 128 + 512


def make_consts():
    k = np.arange(128)[:, None]
    j = np.arange(128)[None, :]
    same = (k // 64) == (j // 64)
    m = {}
    m["ident"] = (k == j)
    m["ones"] = np.ones((128, 128), bool)
    m["negLinc"] = -(k >= j).astype(np.float32)
    m["negLstr"] = -(k < j).astype(np.float32)
    m["amask"] = (k < j)
    m["tri_le"] = (k <= j)
    m["negtri_le"] = -(k <= j).astype(np.float32)
    m["trichunk"] = (k <= j) & same
    m["triU"] = (k > j) & same
    m["negLinc_blk"] = -((k >= j) & same).astype(np.float32)
    m["negL2"] = np.where(same, -(k < j).astype(np.float32), -1.0)
    m["amask_s"] = ((k % 64) < j)
    cols = [np.asarray(m[n], np.float32) for n in CST_NAMES]
    negmask = np.where(same & (j >= k), 0.0, NEG).astype(np.float32)
    cols.append(np.tile(negmask, (1, 4)))
    return np.ascontiguousarray(np.concatenate(cols, axis=1).astype(np.float32))


P_N1W, P_N2W, P_CW, P_CB, P_DSK, P_SNW, P_DTB, P_ALOG = 0, 8, 16, 40, 46, 50, 54, 62
NPRM = 70


def build(NBP=4, SEQ=2048, NBS=4, PAST=1024, DSEQ=64):
    assert DSEQ == 64 and NBS % 2 == 0 and SEQ % 512 == 0 and PAST % 128 == 0
    nc = bass.Bass("TRN2", target_bir_lowering=False)

    def din(name, shape, dt=F32):
        return nc.dram_tensor(name, shape, dt, kind="ExternalInput").ap()

    def dout(name, shape):
        return nc.dram_tensor(name, shape, F32, kind="ExternalOutput").ap()

    def dint(name, shape, dt):
        return nc.dram_tensor(name, shape, dt, kind="Internal").ap()

    xp = din("xp", [NBP * SEQ, D])
    xs = din("xs", [NBS * DSEQ, D])
    ck = din("ck", [NBS, PAST, 512])
    cv = din("cv", [NBS, PAST, 512])
    sconv = din("sconv", [NBS, 3, 768])
    sssm = din("sssm", [NBS, 8, 64, 64])
    w_in = din("w_in", [D, INP])
    w_out = din("w_out", [D, D])
    w_up = din("w_up", [D, DFF])
    w_down = din("w_down", [DFF, D])
    prm_d = din("prm", [128, NPRM])
    fnw_d = din("fnw", [128, D])
    cst_d = din("cst", [128, NCST])
    identf_d = din("identf", [128, 128])

    yp = dout("yp", [NBP * SEQ, D])
    ys = dout("ys", [NBS * DSEQ, D])
    kp = dout("kp", [NBP * SEQ, 512])
    vp = dout("vp", [NBP * SEQ, 512])
    convp = dout("convp", [NBP, 3, 768])
    ssmp = dout("ssmp", [NBP, 8, 64, 64])
    ksm = dout("ksm", [NBS * DSEQ, 512])
    vsm = dout("vsm", [NBS * DSEQ, 512])
    convs = dout("convs", [NBS, 3, 768])
    ssms = dout("ssms", [NBS, 8, 64, 64])

    w_in_bf = dint("w_in_bf", [D, INP], BF16)
    w_out_bf = dint("w_out_bf", [D, D], BF16)
    w_up_bf = dint("w_up_bf", [D, DFF], BF16)
    w_down_bf = dint("w_down_bf", [DFF, D], BF16)

    es = ExitStack()
    with es:
        fw = FW(nc, es)

        def sb(name, shape, dt=F32):
            return es.enter_context(nc.sbuf_tensor("s_" + name, shape, dt))

        def PE(fn, reads, writes, inc=True):
            return fw.op(fw.pe, fn, reads, writes, inc)

        def ACT(fn, reads, writes, strict=False):
            return fw.op(fw.act, fn, reads, writes, strict=strict)

        def DVE(fn, reads, writes):
            return fw.op(fw.dve, fn, reads, writes)

        def POOL(fn, reads, writes):
            return fw.op(fw.pool, fn, reads, writes)

        def mm(out, lhsT, rhs, start, stop, reads, writes, inc=True, serial=False):
            if serial:
                nc.tensor.wait_ge(fw.pe.sem, fw.pe.count)
                fw.pe.seen["pe"] = fw.pe.count
            return PE(lambda: nc.tensor.matmul(out, lhsT=lhsT, rhs=rhs, start=start, stop=stop,
                                               skip_group_check=True), reads, writes, inc)

        class Bank:
            pass

        banks = []
        for i in range(8):
            b = Bank()
            b.t = es.enter_context(nc.psum_tensor("bank%d" % i, [128, 512], F32))
            b.ap = b.t[:]
            b.bf = b.t[:].bitcast(BF16)
            b.buf = Buf("bank%d" % i, excl=True)
            banks.append(b)
        bank_rr = [0, 0]

        def nextbank():
            b = banks[bank_rr[0]]
            bank_rr[0] = (bank_rr[0] + 1) % 6
            return b

        def nextbank_mlp():
            b = banks[6 + bank_rr[1]]
            bank_rr[1] = (bank_rr[1] + 1) % 2
            return b

        cst = sb("cst", [128, NCST], BF16)
        b_cst = Buf("cst")
        identf = sb("identf", [128, 128])
        b_identf = Buf("identf")
        prm = sb("prm", [128, NPRM])
        b_prm = Buf("prm")
        fnw = sb("fnw", [128, D])
        b_fnw = Buf("fnw")
        aneg = sb("aneg", [128, 8])
        b_aneg = Buf("aneg")

        def C(name, r0=0, r1=128, c0=0, c1=128):
            o = CST_NAMES.index(name) * 128
            return cst[r0:r1, o + c0:o + c1]

        negmaskD = cst[:, len(CST_NAMES) * 128: len(CST_NAMES) * 128 + 512]

        NSLOT = 3
        ring = [sb("ring%d" % i, [128, 4096], BF16) for i in range(NSLOT)]
        b_ring = [Buf("ring%d" % i) for i in range(NSLOT)]

        kT = sb("kT", [128, 4, 2048], BF16)
        b_kT = [Buf("kT0"), Buf("kT1")]
        v_bf = sb("v_bf", [128, 16, 512], BF16)
        b_v = [Buf("v0"), Buf("v1")]
        vnew = sb("vnew", [128, 2, 512], BF16)
        b_vnew = Buf("vnew")
        kTn = sb("kTn", [128, 4, 256], BF16)
        b_kTn = Buf("kTn")

        class XB:
            pass

        XS = []
        for i in range(2):
            X = XB()
            X.x = sb("x_sb%d" % i, [128, 4, D])
            X.bx = [Buf("x%d_%d" % (i, j)) for j in range(4)]
            X.xnT = sb("xnT%d" % i, [128, 8, 512], BF16)
            X.bxn = Buf("xnT%d" % i)
            XS.append(X)
        rtmp = sb("rtmp", [128, 2, 512])
        b_rtmp = [Buf("rtmp0"), Buf("rtmp1")]
        stat = sb("stat", [128, 16])
        b_stat = Buf("stat")

        qpad = [sb("qpad%d" % e, [128, 4, 512], BF16) for e in range(2)]
        b_qT = Buf("qT")
        mixT = sb("mixT", [128, 8, 512], BF16)
        b_mix = [Buf("mix%d" % c) for c in range(8)]
        uT = sb("uT", [128, 2, 4, 512], BF16)
        b_uT = [Buf("uT0"), Buf("uT1")]

        scrB = sb("scrB", [128, 1024])
        b_scrB = [Buf("scrB0"), Buf("scrB1")]
        zT = sb("zT", [128, 4, 512])
        b_z = [Buf("z%d" % c) for c in range(4)]
        scrC = sb("scrC", [128, 4352])
        xbc_bf = sb("xbc_bf", [128, 6, 512], BF16)
        b_xbcbf = [Buf("xbcbf%d" % c) for c in range(6)]
        scrA = sb("scrA", [128, 4, 512])
        b_scrA = [Buf("scrA%d" % c) for c in range(4)]
        cstate = sb("cstate", [128, 6, 3])
        b_cstate = Buf("cstate")

        xbcT = scrC[:, 0:6 * 520].rearrange("p (c n) -> p c n", c=6)
        b_xbcT = [Buf("xbcT%d" % c) for c in range(6)]
        cacc = [scrC[:, 3120 + i * 520: 3120 + (i + 1) * 520] for i in range(2)]
        b_cacc = [Buf("cacc0"), Buf("cacc1")]
        NE, NG, NSP, NA = 3, 2, 4, 3
        e_sb = [scrC[:, i * 512:(i + 1) * 512] for i in range(NE)]
        b_e = [Buf("e%d" % i) for i in range(NE)]
        g_sb = [scrC[:, 1536 + i * 512: 1536 + (i + 1) * 512] for i in range(NG)]
        b_g = [Buf("g%d" % i) for i in range(NG)]
        scrC_bf = scrC[:, 2560:4352].bitcast(BF16)
        sp_sb = [scrC_bf[:, i * 512:(i + 1) * 512] for i in range(NSP)]
        b_sp = [Buf("sp%d" % i) for i in range(NSP)]
        A_sb = [scrC_bf[:, 2048 + i * 512: 2048 + (i + 1) * 512] for i in range(NA)]
        b_A = [Buf("A%d" % i) for i in range(NA)]
        conv_bufs = b_xbcT + b_cacc
        attn_bufs = b_e + b_g + b_sp + b_A

        dts = sb("dts", [128, 3, 4, 8])
        b_dts = Buf("dts")
        a_bf = sb("a_bf", [128, 4, 8], BF16)
        b_abf = Buf("a_bf")
        abc = sb("abc", [128, 8, 128], BF16)
        b_abc = Buf("abc")
        AT1 = sb("AT1", [128, 8, 128], BF16)
        b_AT1 = Buf("AT1")
        M_bf = sb("M_bf", [128, 8, 128], BF16)
        b_M = Buf("M")
        xs_bf = [AT1[:].rearrange("p h l -> p (h l)"), M_bf[:].rearrange("p h l -> p (h l)")]
        b_xsbf = [b_AT1, b_M]
        xc = sb("xc", [128, 512], BF16)
        b_xc = Buf("xc")
        xcd = sb("xcd", [128, 512], BF16)
        b_xcd = Buf("xcd")
        Btok = sb("Btok", [128, 128], BF16)
        b_Btok = Buf("Btok")
        eaT = sb("eaT", [128, 4, 128])
        b_eaT = Buf("eaT")
        ytmp = sb("ytmp", [128, 4, 128])
        b_ytmp = Buf("ytmp")
        dcd = sb("dcd", [128, 16])
        b_dcd = Buf("dcd")
        hT = [sb("hT%d" % i, [128, 256]) for i in range(max(NBS, 1))]
        b_hT = [Buf("hT%d" % i) for i in range(max(NBS, 1))]
        hTb = [sb("hTb%d" % i, [128, 256], BF16) for i in range(max(NBS, 1))]
        b_hTb = [Buf("hTb%d" % i) for i in range(max(NBS, 1))]
        htmp = sb("htmp", [128, 256])
        b_htmp = Buf("htmp")
        stT = ytmp[0:64, :, :]
        b_stT = b_ytmp
        sq_bf = abc[:].rearrange("p h l -> p (h l)").rearrange("p (a b) -> p a b", a=2)
        b_sq = [b_abc, b_abc]

        fw.dma(fw.pool, cst[:], cst_d[:, :], writes=[b_cst])
        fw.dma(fw.sp, identf[:], identf_d[:, :], writes=[b_identf])
        fw.dma(fw.sp, prm[:], prm_d[:, :], writes=[b_prm])
        fw.dma(fw.sp, fnw[:], fnw_d[:, :], writes=[b_fnw])
        b_win = [Buf("win%d" % i) for i in range(8)]
        b_wout = [Buf("wout%d" % i) for i in range(8)]
        b_wup = [Buf("wup%d" % i) for i in range(8)]
        b_wdn = [Buf("wdn%d" % i) for i in range(32)]
        for i in range(8):
            fw.dma(fw.pool, w_in_bf[i * 128:(i + 1) * 128, :], w_in[i * 128:(i + 1) * 128, :], writes=[b_win[i]])
        for i in range(8):
            fw.dma(fw.pool, w_out_bf[i * 128:(i + 1) * 128, :], w_out[i * 128:(i + 1) * 128, :], writes=[b_wout[i]])
        for i in range(8):
            fw.dma(fw.pool, w_up_bf[i * 128:(i + 1) * 128, :], w_up[i * 128:(i + 1) * 128, :], writes=[b_wup[i]])
        for i in range(32):
            fw.dma(fw.pool, w_down_bf[i * 128:(i + 1) * 128, :], w_down[i * 128:(i + 1) * 128, :], writes=[b_wdn[i]])
        for e_ in range(2):
            DVE(lambda e_=e_: nc.vector.memset(qpad[e_][:], 0.0), [], [b_qT])
        ACT(lambda: nc.scalar.activation(out=aneg[:], in_=prm[:, P_ALOG:P_ALOG + 8], func=AF.Exp), [b_prm], [b_aneg])
        DVE(lambda: nc.vector.tensor_scalar(out=aneg[:], in0=aneg[:], scalar1=-1.0, scalar2=None, op0=ALU.mult),
            [b_aneg], [b_aneg])

        w_in_v = w_in_bf.rearrange("(k p) c -> p k c", p=128)
        w_out_v = w_out_bf.rearrange("(k p) c -> p k c", p=128)
        w_up_v = w_up_bf.rearrange("(k p) c -> p k c", p=128)
        w_dn_v = w_down_bf.rearrange("(f p) d -> p f d", p=128)

        def piece_plan():
            pin = []
            for i in range(5):
                pin.append(("in%d" % i, w_in_v[:, :, i * 512:(i + 1) * 512], b_win, (8, 512)))
            pin.append(("in5", w_in_v[:, :, 2560:2824], b_win, (8, 264)))
            pwo = [("wo0", w_out_v[:, :, 0:512], b_wout, (8, 512)), ("wo1", w_out_v[:, :, 512:1024], b_wout, (8, 512))]
            order = ["up0"]
            for p in range(1, 8):
                order += ["up%d" % p, "dn%d" % (p - 1)]
            order.append("dn7")
            pml = []
            for nm in order:
                p = int(nm[2:])
                if nm[:2] == "up":
                    pml.append((nm, w_up_v[:, :, p * 512:(p + 1) * 512], b_wup, (8, 512)))
                else:
                    pml.append((nm, w_dn_v[:, 4 * p:4 * p + 4, :], b_wdn[4 * p:4 * p + 4], (4, 1024)))
            return pin, pwo, pml

        n_super = NBP * (SEQ // 512) + (1 if NBS else 0)
        pin, pwo, pml = piece_plan()
        plan = []
        for i_ in range(n_super):
            if i_ > 0:
                plan += pml[0:2]
            plan += pin
            if i_ > 0:
                plan += pml[2:]
            plan += pwo
        plan += pml
        ws = {"cur": -1, "issued": 0}

        def ws_get(name, hold=0):
            ws["cur"] += 1
            i = ws["cur"]
            assert plan[i][0] == name, (plan[i][0], name)
            while ws["issued"] <= min(i + NSLOT - 1 - hold, len(plan) - 1):
                j = ws["issued"]
                nm, src, sbufs, (a, b) = plan[j]
                dst = ring[j % NSLOT][:, 0:a * b].rearrange("p (k c) -> p k c", k=a)
                fw.dma(fw.sp, dst, src, reads=list(sbufs), writes=[b_ring[j % NSLOT]])
                ws["issued"] += 1
            nm, src, sbufs, (a, b) = plan[i]
            return ring[i % NSLOT][:, 0:a * b].rearrange("p (k c) -> p k c", k=a), b_ring[i % NSLOT]

        b_stats = [Buf("stat%d" % k) for k in range(4)]

        def norm_a(j, X, stg, bstg, slot):
            sc = stat[:, 4 * slot:4 * slot + 4]
            bs = b_stats[slot]
            DVE(lambda: nc.vector.memset(sc[:, 0:1], 0.0), [], [bs])
            ACT(lambda: nc.scalar.activation(out=stg, in_=X.x[:, j, :], func=AF.Square, accum_out=sc[:, 0:1]),
                [X.bx[j]], [bstg, bs])
            ACT(lambda: nc.scalar.activation(out=sc[:, 1:2], in_=sc[:, 0:1], func=AF.Ln, scale=1.0 / D, bias=EPS),
                [bs], [bs])
            ACT(lambda: nc.scalar.activation(out=sc[:, 2:3], in_=sc[:, 1:2], func=AF.Exp, scale=-0.5), [bs], [bs])
            DVE(lambda: nc.vector.tensor_scalar(out=stg, in0=X.x[:, j, :], scalar1=sc[:, 2:3],
                                                scalar2=None, op0=ALU.mult), [X.bx[j], bs], [bstg])

        def norm_b(j, nwcol, X, stg, bstg):
            nw = prm[:, nwcol:nwcol + 8]
            bk = nextbank()
            v3 = bk.bf.rearrange("p (c t) -> p c t", c=8)
            for c in range(8):
                PE(lambda c=c: nc.tensor.transpose(v3[:, c, :], stg[:, c * 128:(c + 1) * 128], C("ident")),
                   [bstg, b_cst], [bk.buf], inc=(c == 7))
            DVE(lambda: nc.vector.tensor_tensor(out=X.xnT[:, :, j * 128:(j + 1) * 128], in0=v3,
                                                in1=nw.unsqueeze(2).to_broadcast([128, 8, 128]),
                                                op=ALU.mult), [bk.buf, b_prm], [X.bxn])

        def norm1_stage(c):
            return scrA[:, c, :].bitcast(BF16), b_scrA[c]

        def norm_T(nsub, nwcol, X, tick=None):
            for j in range(nsub):
                stg, bstg = norm1_stage(j)
                norm_a(j, X, stg, bstg, 2)
                norm_b(j, nwcol, X, stg, bstg)

        def in_proj(T, nsub, nseg, L, kT_dst, kT_bufs, k_rows, v_rows, v_dst, v_bufs, X):
            segw = 3 + L
            xnT = X.xnT
            b_xnT = X.bxn

            def feat_group(w, wbuf, col0, T, evac):
                bk = nextbank()
                for kc in range(8):
                    mm(bk.ap[:, 0:T], w[:, kc, col0:col0 + 128], xnT[:, kc, 0:T], kc == 0, kc == 7,
                       [wbuf, b_xnT], [bk.buf], inc=(kc == 7))
                evac(bk)

            def tok_group(w, wbuf, c0, n, j, evac):
                bk = nextbank()
                for kc in range(8):
                    mm(bk.ap[:, 0:n], xnT[:, kc, j * 128:(j + 1) * 128], w[:, kc, c0:c0 + n], kc == 0, kc == 7,
                       [wbuf, b_xnT], [bk.buf], inc=(kc == 7))
                evac(bk)

            def xbc_dst(c):
                return xbcT[:, c, 0:nseg * segw].rearrange("p (s l) -> p s l", l=segw)[:, :, 3:3 + L]

            w, wb = ws_get("in0")
            for cc in range(4):
                def evq(bk, cc=cc):
                    for e_ in range(2):
                        pr_ = slice(64 * e_, 64 * e_ + 64)
                        ACT(lambda: nc.scalar.activation(out=qpad[e_][pr_, cc, 0:T], in_=bk.ap[pr_, 0:T], func=AF.Copy,
                                                         scale=0.125), [bk.buf], [b_qT])
                feat_group(w, wb, cc * 128, T, evq)
            w, wb = ws_get("in1")
            for cc in range(4):
                feat_group(w, wb, cc * 128, T, lambda bk, cc=cc: DVE(
                    lambda: nc.vector.tensor_copy(out=kT_dst(cc), in_=bk.ap[:, 0:T]), [bk.buf], kT_bufs))
            for j in range(nsub):
                def ev(bk, j=j):
                    ACT(lambda: nc.scalar.copy(out=scrA[:, j % 2, :], in_=bk.ap[:, :]), [bk.buf], [b_scrA[j % 2]])
                    fw.dma(fw.sp, k_rows(j), scrA[:, j % 2, :], reads=[b_scrA[j % 2]])
                tok_group(w, wb, 0, 512, j, ev)
            w, wb = ws_get("in2")
            for j in range(nsub):
                def ev(bk, j=j):
                    ACT(lambda: nc.scalar.copy(out=scrA[:, 2 + j % 2, :], in_=bk.ap[:, :]), [bk.buf], [b_scrA[2 + j % 2]])
                    DVE(lambda: nc.vector.tensor_copy(out=v_dst(j), in_=bk.ap[:, :]), [bk.buf], v_bufs)
                    fw.dma(fw.sp, v_rows(j), scrA[:, 2 + j % 2, :], reads=[b_scrA[2 + j % 2]])
                tok_group(w, wb, 0, 512, j, ev)
            w, wb = ws_get("in3")
            for cc in range(4):
                feat_group(w, wb, cc * 128, T, lambda bk, cc=cc: ACT(
                    lambda: nc.scalar.activation(out=zT[:, cc, 0:T], in_=bk.ap[:, 0:T], func=AF.Silu),
                    [bk.buf], [b_z[cc]]))
            w, wb = ws_get("in4")
            for cc in range(4):
                feat_group(w, wb, cc * 128, T, lambda bk, cc=cc: DVE(
                    lambda: nc.vector.tensor_copy(out=xbc_dst(cc), in_=bk.ap[:, 0:T].rearrange("p (s l) -> p s l", l=L)),
                    [bk.buf], [b_xbcT[cc]]))
            w, wb = ws_get("in5")
            for cc in range(2):
                feat_group(w, wb, cc * 128, T, lambda bk, cc=cc: DVE(
                    lambda: nc.vector.tensor_copy(out=xbc_dst(4 + cc), in_=bk.ap[:, 0:T].rearrange("p (s l) -> p s l", l=L)),
                    [bk.buf], [b_xbcT[4 + cc]]))
            for j in range(nsub):
                tok_group(w, wb, 256, 8, j, lambda bk, j=j: DVE(
                    lambda: nc.vector.tensor_tensor(out=dts[:, 0, j, :], in0=bk.ap[:, 0:8], in1=prm[:, P_DTB:P_DTB + 8],
                                                    op=ALU.add), [bk.buf, b_prm], [b_dts]))

        def conv_silu(T, nseg, L, tick=None):
            segw = 3 + L
            n = nseg * segw - 3
            for c in range(6):
                if tick:
                    tick(2.0)
                acc = cacc[c % 2]
                ba = b_cacc[c % 2]
                src = xbcT[:, c, :]
                eng = DVE
                h = nc.vector
                cw0 = P_CW + c * 4
                eng(lambda: h.tensor_scalar(out=acc[:, 0:n], in0=src[:, 0:n], scalar1=prm[:, cw0:cw0 + 1],
                                            scalar2=prm[:, P_CB + c:P_CB + c + 1], op0=ALU.mult, op1=ALU.add),
                    [b_xbcT[c], b_prm], [ba])
                for i in range(1, 4):
                    eng(lambda i=i: h.scalar_tensor_tensor(out=acc[:, 0:n], in0=src[:, i:i + n],
                                                           scalar=prm[:, cw0 + i:cw0 + i + 1], in1=acc[:, 0:n],
                                                           op0=ALU.mult, op1=ALU.add), [b_xbcT[c], b_prm, ba], [ba])
                accv = acc[:, 0:nseg * segw].rearrange("p (s l) -> p s l", l=segw)[:, :, 0:L]
                if c < 4:
                    xtmp = scrB[:, (c % 2) * 512:(c % 2) * 512 + T]
                    ACT(lambda: nc.scalar.activation(out=xtmp.rearrange("p (s l) -> p s l", l=L), in_=accv,
                                                     func=AF.Silu), [ba], [b_scrB[c % 2]])
                    POOL(lambda: nc.gpsimd.tensor_copy(out=xbc_bf[:, c, 0:T], in_=xtmp), [b_scrB[c % 2]], [b_xbcbf[c]])
                    ACT(lambda: nc.scalar.activation(out=scrA[:, c, 0:T], in_=xtmp, func=AF.Copy,
                                                     scale=prm[:, P_DSK + c:P_DSK + c + 1]), [b_scrB[c % 2], b_prm], [b_scrA[c]])
                else:
                    ACT(lambda: nc.scalar.activation(out=xbc_bf[:, c, 0:T].rearrange("p (s l) -> p s l", l=L), in_=accv,
                                                     func=AF.Silu), [ba], [b_xbcbf[c]])

        def ssd_subtile(j, nsub_T, st_in, st_mid, st_out, tick=None):
            cols = slice(j * 128, (j + 1) * 128)
            bk_tr, bk_sm, bk_D0, bk_D1, bk_G, bk_ea = banks[0:6]
            bk_yd = banks[4]
            bk_yo = banks[1]
            tk = tick if tick else (lambda us: None)
            trv = bk_tr.bf[:, 0:640].rearrange("p (c t) -> p c t", c=5)
            for c in range(5):
                PE(lambda c=c: nc.tensor.transpose(trv[:, c, :], xbc_bf[:, c, cols], C("ident")),
                   [b_xbcbf[c], b_cst], [bk_tr.buf], inc=(c == 4))
            DVE(lambda: nc.vector.tensor_tensor(out=xc[:].rearrange("p (h d) -> p h d", h=8),
                                                in0=bk_tr.bf[:, 0:512].rearrange("p (h d) -> p h d", h=8),
                                                in1=dts[:, 1, j, :].unsqueeze(2).to_broadcast([128, 8, 64]), op=ALU.mult),
                [bk_tr.buf, b_dts], [b_xc])
            ACT(lambda: nc.scalar.copy(out=Btok[:], in_=bk_tr.bf[:, 512:640]), [bk_tr.buf], [b_Btok])
            tk(1.4)
            DVE(lambda: nc.vector.tensor_copy(out=abc[:], in_=a_bf[:, j, :].unsqueeze(2).to_broadcast([128, 8, 128])),
                [b_abf], [b_abc])
            DVE(lambda: nc.vector.tensor_tensor(out=AT1[:], in0=abc[:],
                                                in1=C("tri_le").unsqueeze(1).to_broadcast([128, 8, 128]), op=ALU.mult),
                [b_abc, b_cst], [b_AT1])
            tk(1.4)
            mm(bk_sm.ap[:, 0:8], C("triU"), a_bf[:, j, :], True, True, [b_cst, b_abf], [bk_sm.buf], inc=False)
            for c in range(2):
                for g in range(2):
                    mm(bk_sm.ap[64 * g:64 * g + 64, 8 + 4 * c:12 + 4 * c], C("ones", 64 * c, 64 * c + 64, 0, 64),
                       a_bf[64 * c:64 * c + 64, j, 4 * g:4 * g + 4], True, True, [b_cst, b_abf], [bk_sm.buf],
                       inc=(g == 1), serial=(c == 1 and g == 0))
            ACT(lambda: nc.scalar.activation(out=dcd[:, 0:16], in_=bk_sm.ap[:, 0:16], func=AF.Exp), [bk_sm.buf], [b_dcd])
            DVE(lambda: nc.vector.tensor_tensor(out=xcd[:].rearrange("p (h d) -> p h d", h=8),
                                                in0=xc[:].rearrange("p (h d) -> p h d", h=8),
                                                in1=dcd[:, 0:8].unsqueeze(2).to_broadcast([128, 8, 64]), op=ALU.mult),
                [b_xc, b_dcd], [b_xcd])
            tk(1.4)
            E_sb = scrB[:].rearrange("p (h l) -> p h l", h=8)
            for half, bk in ((0, bk_D0), (1, bk_D1)):
                hs = slice(4 * half, 4 * half + 4)
                o = bk.ap.rearrange("p (h l) -> p h l", h=4)
                mm(o, C("ones"), AT1[:, hs, :], True, False, [b_cst, b_AT1], [bk.buf], inc=False)
                mm(o, C("negtri_le"), abc[:, hs, :], False, False, [b_cst, b_abc], [bk.buf], inc=False)
                mm(bk.ap, C("ident"), negmaskD, False, True, [b_cst], [bk.buf])
                ACT(lambda hs=hs, o=o: nc.scalar.activation(out=E_sb[:, hs, :], in_=o, func=AF.Exp),
                    [bk.buf], [b_scrB[half]])
            tk(1.4)
            Gv = bk_G.ap[:, 0:256].rearrange("p (g l) -> p g l", g=2)
            for g in range(2):
                mm(Gv[:, g, :], xbc_bf[64 * g:64 * g + 64, 4, cols], xbc_bf[64 * g:64 * g + 64, 5, cols], True, True,
                   [b_xbcbf[4], b_xbcbf[5]], [bk_G.buf], inc=True, serial=(g == 1))
            for g in range(2):
                DVE(lambda g=g: nc.vector.tensor_tensor(out=M_bf[:, 4 * g:4 * g + 4, :], in0=E_sb[:, 4 * g:4 * g + 4, :],
                                                        in1=Gv[:, g, :].unsqueeze(1).to_broadcast([128, 4, 128]),
                                                        op=ALU.mult), [b_scrB[g], bk_G.buf], [b_M])
            tk(1.4)
            abcf = abc[:].rearrange("p h l -> p (h l)")
            eav = bk_ea.ap.rearrange("p (h l) -> p h l", h=4)
            for hp in range(4):
                mm(eav[:, hp, :], abcf[:, 2 * hp * 128 + 64: 2 * hp * 128 + 192], C("trichunk"), True, True,
                   [b_abc, b_cst], [bk_ea.buf], inc=(hp == 3))
            ACT(lambda: nc.scalar.activation(out=eaT[:], in_=eav, func=AF.Exp), [bk_ea.buf], [b_eaT])
            tk(1.4)
            ydv = bk_yd.ap.rearrange("p (h l) -> p h l", h=4)
            for hp in range(4):
                for e in range(2):
                    hd = 2 * hp + e
                    mm(ydv[64 * e:64 * e + 64, hp, :], xc[:, hd * 64:(hd + 1) * 64], M_bf[:, hd, :], True, True,
                       [b_xc, b_M], [bk_yd.buf], inc=(hp == 3 and e == 1))
            tk(1.4)
            yov = bk_yo.ap.rearrange("p (h l) -> p h l", h=4)
            for c in range(2):
                si = st_in[c]
                ccols = slice(j * 128 + 64 * c, j * 128 + 64 * c + 64)
                for hp in range(4):
                    g = hp // 2
                    mm(yov[:, hp, 64 * c:64 * c + 64], hTb[si][64 * g:64 * g + 64, (hp % 2) * 128:(hp % 2) * 128 + 128],
                       xbc_bf[64 * g:64 * g + 64, 5, ccols], True, True, [b_hTb[si], b_xbcbf[5]], [bk_yo.buf],
                       inc=(hp % 2 == 1), serial=(hp == 2))
                tk(1.4)
                so = st_out[c]
                stv = bk_tr.ap[:, 0:256]
                for g in range(2):
                    mm(stv[64 * g:64 * g + 64, :], Btok[64 * c:64 * c + 64, 64 * g:64 * g + 64],
                       xcd[64 * c:64 * c + 64, 256 * g:256 * g + 256], True, True, [b_Btok, b_xcd], [bk_tr.buf],
                       inc=(g == 1))
                DVE(lambda si=si, c=c: nc.vector.tensor_tensor(
                    out=htmp[:].rearrange("p (h d) -> p h d", h=4), in0=hT[si][:].rearrange("p (h d) -> p h d", h=4),
                    in1=dcd[:, 8 + 4 * c:12 + 4 * c].unsqueeze(2).to_broadcast([128, 4, 64]), op=ALU.mult),
                    [b_hT[si], b_dcd], [b_htmp])
                DVE(lambda so=so: nc.vector.tensor_tensor(out=hT[so][:], in0=htmp[:], in1=stv, op=ALU.add),
                    [b_htmp, bk_tr.buf], [b_hT[so]])
                ACT(lambda so=so: nc.scalar.copy(out=hTb[so][:], in_=hT[so][:]), [b_hT[so]], [b_hTb[so]])
            tk(1.4)
            DVE(lambda: nc.vector.tensor_tensor(out=ytmp[:], in0=yov, in1=eaT[:], op=ALU.mult),
                [bk_yo.buf, b_eaT], [b_ytmp])
            DVE(lambda: nc.vector.tensor_tensor(out=ytmp[:], in0=ydv, in1=ytmp[:], op=ALU.add),
                [bk_yd.buf, b_ytmp], [b_ytmp])
            DVE(lambda: nc.vector.tensor_tensor(out=scrA[:, :, cols], in0=scrA[:, :, cols], in1=ytmp[:], op=ALU.add),
                list(b_scrA) + [b_ytmp], list(b_scrA))

        def ssd_prepare_dt(nsub):
            ACT(lambda: nc.scalar.activation(out=dts[:, 0, 0:nsub, :], in_=dts[:, 0, 0:nsub, :], func=AF.Exp),
                [b_dts], [b_dts])
            ACT(lambda: nc.scalar.activation(out=dts[:, 1, 0:nsub, :], in_=dts[:, 0, 0:nsub, :], func=AF.Ln, bias=1.0),
                [b_dts], [b_dts])
            DVE(lambda: nc.vector.tensor_tensor(out=dts[:, 2, 0:nsub, :], in0=dts[:, 1, 0:nsub, :],
                                                in1=aneg[:].unsqueeze(1).to_broadcast([128, nsub, 8]), op=ALU.mult),
                [b_dts, b_aneg], [b_dts])
            DVE(lambda: nc.vector.tensor_copy(out=a_bf[:, 0:nsub, :], in_=dts[:, 2, 0:nsub, :]), [b_dts], [b_abf])

        def ssd_gate_norm(T, tick=None):
            for g in range(2):
                if tick:
                    tick(1.5)
                bk = nextbank()
                for cc in range(2):
                    c = 2 * g + cc
                    DVE(lambda c=c: nc.vector.tensor_tensor(out=scrA[:, c, 0:T], in0=scrA[:, c, 0:T], in1=zT[:, c, 0:T],
                                                            op=ALU.mult), [b_scrA[c], b_z[c]], [b_scrA[c]])
                    ACT(lambda c=c, cc=cc: nc.scalar.activation(out=sq_bf[:, cc, 0:T], in_=scrA[:, c, 0:T],
                                                                func=AF.Square), [b_scrA[c]], [b_sq[cc]])
                    mm(bk.ap[:, 0:T], C("ones"), sq_bf[:, cc, 0:T], cc == 0, cc == 1, [b_cst, b_sq[cc]], [bk.buf])
                rn = scrB[:, 0:512]
                ACT(lambda: nc.scalar.activation(out=rn[:, 0:T], in_=bk.ap[:, 0:T], func=AF.Ln, scale=1.0 / 256, bias=EPS),
                    [bk.buf], [b_scrB[0]])
                ACT(lambda: nc.scalar.activation(out=rn[:, 0:T], in_=rn[:, 0:T], func=AF.Exp, scale=-0.5),
                    [b_scrB[0]], [b_scrB[0]])
                for cc in range(2):
                    c = 2 * g + cc
                    DVE(lambda c=c: nc.vector.scalar_tensor_tensor(out=mixT[:, 4 + c, 0:T], in0=scrA[:, c, 0:T],
                                                                   scalar=prm[:, P_SNW + c:P_SNW + c + 1], in1=rn[:, 0:T],
                                                                   op0=ALU.mult, op1=ALU.mult),
                        [b_scrA[c], b_prm, b_scrB[0]], [b_mix[4 + c]])

        class It:
            pass

        def attn_pipeline(its, tick=None):
            N = len(its)

            def S0pe(it):
                zb = banks[it.zb]
                nq = len(it.qk)
                for n_, (o, l, r) in enumerate(it.qk):
                    mm(o, l, r, True, True, it.qk_reads, [zb.buf], inc=(n_ == nq - 1))

            def S0(it, s):
                zb = banks[it.zb]
                rows = slice(it.p0, it.p0 + it.nk)
                e = e_sb[s % NE]
                sp = sp_sb[s % NSP]
                ACT(lambda: nc.scalar.activation(out=e[rows, it.c0:it.c1], in_=zb.ap[rows, it.c0:it.c1], func=AF.Exp),
                    [zb.buf], [b_e[s % NE]])
                ACT(lambda: nc.scalar.activation(out=sp[rows, it.c0:it.c1], in_=e[rows, it.c0:it.c1], func=AF.Ln, bias=1.0),
                    [b_e[s % NE]], [b_sp[s % NSP]])
                if it.diag is not None:
                    vw, mk = it.diag
                    POOL(lambda: nc.gpsimd.tensor_tensor(out=vw(sp), in0=vw(sp), in1=mk, op=ALU.mult),
                         [b_sp[s % NSP], b_cst], [b_sp[s % NSP]])

            def S1a(it, s):
                cb = banks[it.cb]
                rows = slice(it.p0, it.p0 + it.nk)
                e = e_sb[s % NE]
                sp = sp_sb[s % NSP]
                g = g_sb[s % NG]
                A = A_sb[s % NA]
                mm(cb.ap[:, it.c0:it.c1], it.L1, sp[rows, it.c0:it.c1], it.first, False, [b_sp[s % NSP], b_cst], [cb.buf])
                ACT(lambda: nc.scalar.activation(out=g[rows, it.c0:it.c1], in_=cb.ap[rows, it.c0:it.c1], func=AF.Exp),
                    [cb.buf], [b_g[s % NG]])
                DVE(lambda: nc.vector.tensor_tensor(out=A[rows, it.c0:it.c1], in0=e[rows, it.c0:it.c1],
                                                    in1=g[rows, it.c0:it.c1], op=ALU.mult),
                    [b_e[s % NE], b_g[s % NG]], [b_A[s % NA]])
                if it.diag is not None:
                    vw, mk = it.diag
                    POOL(lambda: nc.gpsimd.tensor_tensor(out=vw(A), in0=vw(A), in1=mk, op=ALU.mult),
                         [b_A[s % NA], b_cst], [b_A[s % NA]])

            def S2(it, s):
                cb = banks[it.cb]
                rows = slice(it.p0, it.p0 + it.nk)
                sp = sp_sb[s % NSP]
                A = A_sb[s % NA]
                if not it.last:
                    mm(cb.ap[:, it.c0:it.c1], it.L2, sp[rows, it.c0:it.c1], False, True, [b_sp[s % NSP], b_cst], [cb.buf])
                ob = banks[it.ob]
                for n_, (o, l, acols, st) in enumerate(it.av):
                    mm(o, l, A[rows, acols], st, it.last, it.av_reads + [b_A[s % NA]], [ob.buf],
                       inc=(n_ == len(it.av) - 1))
                if it.fin is not None:
                    it.fin()

            S0pe(its[0])
            for step in range(N + 3):
                if step + 1 < N:
                    S0pe(its[step + 1])
                if step < N:
                    S0(its[step], step)
                if 0 <= step - 3 < N:
                    S2(its[step - 3], step - 3)
                if 0 <= step - 1 < N:
                    S1a(its[step - 1], step - 1)
                if tick:
                    tick(max(0.3, back["left"] / max(1, N + 3 - step)))

        def attn_prompt(Q, T, tick=None):
            its = []
            nkb = 4 * Q + 4
            cnt = 0
            for hp in range(4):
                for i in range(nkb):
                    kb = nkb - 1 - i
                    for e in range(2):
                        it = It()
                        it.zb = cnt % 2
                        it.cb = 2 + e
                        it.ob = 4 + e
                        cnt += 1
                        it.nk = 128
                        it.p0 = 0
                        d = kb - 4 * Q
                        it.c0 = 128 * d if d >= 0 else 0
                        it.c1 = T
                        pr = slice(64 * e, 64 * e + 64)
                        it.qk = [(banks[it.zb].ap[:, it.c0:it.c1], kT[:, hp, kb * 128:(kb + 1) * 128],
                                  qpad[e][:, hp, it.c0:it.c1])]
                        it.qk_reads = [b_kT[0], b_kT[1], b_qT]
                        it.first = (i == 0)
                        it.last = (kb == 0)
                        it.L1 = C("negLinc")
                        it.L2 = C("negLstr")
                        if d >= 0:
                            c0 = it.c0
                            it.diag = ((lambda t, c0=c0: t[:, c0:c0 + 128]), C("amask"))
                        else:
                            it.diag = None
                        it.av = [(banks[it.ob].ap[:, it.c0:it.c1],
                                  v_bf[:, kb, hp * 128:(hp + 1) * 128], slice(it.c0, it.c1), i == 0)]
                        it.av_reads = [b_v[0], b_v[1]]
                        it.fin = None
                        if it.last:
                            def fin(hp=hp, obi=it.ob, pr=pr):
                                ACT(lambda: nc.scalar.copy(out=mixT[pr, hp, 0:T], in_=banks[obi].ap[pr, 0:T]),
                                    [banks[obi].buf], [b_mix[hp]])
                            it.fin = fin
                        its.append(it)
            attn_pipeline(its, tick)

        def attn_sample_pair(qa, qb, npast_blk, tick=None):
            its = []
            cnt = 0
            nblk = npast_blk + 1
            for i in range(nblk):
                for sl, q in ((0, qa), (1, qb)):
                    it = It()
                    it.zb = cnt % 2
                    it.cb = 2 + sl
                    it.ob = 4 + sl
                    cnt += 1
                    e = q % 2
                    j = q // 2
                    qcols = slice(q * 64, q * 64 + 64)
                    it.c0 = 0
                    it.c1 = 512
                    it.first = (i == 0)
                    it.last = (i == nblk - 1)
                    zb = banks[it.zb]
                    ob = banks[it.ob]
                    obv = ob.ap[:, 0:256].rearrange("p (h t) -> p h t", h=4)
                    if i == 0:
                        it.nk = 64
                        it.p0 = 64 * e
                        rows = slice(64 * e, 64 * e + 64)
                        it.qk = [(zb.ap[rows, h * 64:(h + 1) * 64],
                                  kTn[:, h // 2, qcols],
                                  qpad[h % 2][:, h // 2, qcols]) for h in range(8)]
                        it.qk_reads = [b_kTn, b_qT]
                        it.L1 = C("negLinc_blk", 64 * e, 64 * e + 64)
                        it.L2 = C("negL2", 64 * e, 64 * e + 64)
                        it.diag = ((lambda t, rows=rows: t[rows, :].rearrange("p (h t) -> p h t", h=8)),
                                   C("amask_s", 64 * e, 64 * e + 64, 0, 64).unsqueeze(1).to_broadcast([64, 8, 64]))
                        it.av = [(obv[64 * (h % 2):64 * (h % 2) + 64, h // 2, :], vnew[rows, j, h * 64:(h + 1) * 64],
                                  slice(h * 64, (h + 1) * 64), h < 2) for h in range(8)]
                        it.av_reads = [b_vnew]
                    else:
                        kb = npast_blk - i
                        it.nk = 128
                        it.p0 = 0
                        kc = slice(sl * 1024 + kb * 128, sl * 1024 + (kb + 1) * 128)
                        it.qk = [(zb.ap[:, h * 64:(h + 1) * 64],
                                  kT[:, h // 2, kc],
                                  qpad[h % 2][:, h // 2, qcols]) for h in range(8)]
                        it.qk_reads = [b_kT[sl], b_qT]
                        it.L1 = C("negLinc")
                        it.L2 = C("negLstr")
                        it.diag = None
                        it.av = [(obv[64 * (h % 2):64 * (h % 2) + 64, h // 2, :], v_bf[:, sl * 8 + kb, h * 64:(h + 1) * 64],
                                  slice(h * 64, (h + 1) * 64), False) for h in range(8)]
                        it.av_reads = [b_v[sl]]
                    it.fin = None
                    if it.last:
                        def fin(q=q, obi=it.ob):
                            ACT(lambda: nc.scalar.copy(
                                out=mixT[:, 0:4, q * 64:(q + 1) * 64],
                                in_=banks[obi].ap[:, 0:256].rearrange("p (h t) -> p h t", h=4)),
                                [banks[obi].buf], [b_mix[0], b_mix[1], b_mix[2], b_mix[3]])
                        it.fin = fin
                    its.append(it)
            attn_pipeline(its, tick)

        def load_past(sl, q, npast_blk):
            ktok = uT[:].rearrange("p a b c -> p (a b c)")[:, 0:npast_blk * 512].rearrange("p (b c) -> p b c", c=512)
            fw.dma(fw.pool, ktok, ck[q].rearrange("(b p) c -> p b c", p=128), writes=[b_uT[0], b_uT[1]])
            fw.dma(fw.pool, v_bf[:, sl * 8:sl * 8 + npast_blk, :], cv[q].rearrange("(b p) c -> p b c", p=128),
                   writes=[b_v[sl]])
            for kb in range(npast_blk):
                bk = nextbank()
                v3 = bk.bf[:, 0:512].rearrange("p (c t) -> p c t", c=4)
                for hp in range(4):
                    PE(lambda hp=hp, kb=kb, v3=v3: nc.tensor.transpose(v3[:, hp, :], ktok[:, kb, hp * 128:(hp + 1) * 128],
                                                                       C("ident")),
                       [b_uT[0], b_uT[1], b_cst], [bk.buf], inc=(hp == 3))
                DVE(lambda kb=kb, v3=v3: nc.vector.tensor_copy(
                    out=kT[:, :, sl * 1024 + kb * 128: sl * 1024 + (kb + 1) * 128], in_=v3), [bk.buf], [b_kT[sl]])

        def out_proj_norm2(T, nsub, X, Xn=None, nsub_n=0):
            w0, wb0 = ws_get("wo0")
            w1, wb1 = ws_get("wo1", hold=1)
            for j in range(max(nsub, nsub_n) + 2):
                if j < nsub:
                    for dh, (w, wb) in enumerate(((w0, wb0), (w1, wb1))):
                        bk = nextbank()
                        for c in range(8):
                            mm(bk.ap[:, :], mixT[:, c, j * 128:(j + 1) * 128], w[:, c, :], c == 0, c == 7,
                               [wb, b_mix[c]], [bk.buf], inc=(c == 7))
                        DVE(lambda j=j, dh=dh, bk=bk: nc.vector.tensor_tensor(
                            out=X.x[:, j, dh * 512:(dh + 1) * 512], in0=X.x[:, j, dh * 512:(dh + 1) * 512], in1=bk.ap[:, :],
                            op=ALU.add), [X.bx[j], bk.buf], [X.bx[j]])
                if Xn is not None and j < nsub_n:
                    stg, bstg = norm1_stage(j)
                    norm_a(j, Xn, stg, bstg, 2)
                    norm_b(j, P_N1W, Xn, stg, bstg)
                if 0 <= j - 1 < nsub:
                    norm_a(j - 1, X, xs_bf[(j - 1) % 2], b_xsbf[(j - 1) % 2], (j - 1) % 2)
                if 0 <= j - 2 < nsub:
                    norm_b(j - 2, P_N2W, X, xs_bf[j % 2], b_xsbf[j % 2])

        def mlp_gen(T, nsub, X, y_rows):
            xnT = X.xnT

            def up(p):
                w, wb = ws_get("up%d" % p)
                for fc in range(4):
                    bk = nextbank_mlp()
                    for kc in range(8):
                        mm(bk.ap[:, 0:T], w[:, kc, fc * 128:(fc + 1) * 128], xnT[:, kc, 0:T], kc == 0, kc == 7,
                           [wb, X.bxn], [bk.buf], inc=(kc == 7))
                    r = rtmp[:, fc % 2, 0:T]
                    ACT(lambda bk=bk, r=r: nc.scalar.activation(out=r, in_=bk.ap[:, 0:T], func=AF.Relu),
                        [bk.buf], [b_rtmp[fc % 2]])
                    eng, h = (DVE, nc.vector) if fc % 2 == 0 else (POOL, nc.gpsimd)
                    eng(lambda r=r, fc=fc, h=h: h.tensor_tensor(out=uT[:, p % 2, fc, 0:T], in0=r, in1=r, op=ALU.mult),
                        [b_rtmp[fc % 2]], [b_uT[p % 2]])
                    yield 2.1 * T / 512
                yield "P"

            def down(p):
                w, wb = ws_get("dn%d" % p)
                for j in range(nsub):
                    for dh in range(2):
                        bk = nextbank_mlp()
                        for fcl in range(4):
                            mm(bk.ap[:, :], uT[:, p % 2, fcl, j * 128:(j + 1) * 128], w[:, fcl, dh * 512:(dh + 1) * 512],
                               fcl == 0, fcl == 3, [wb, b_uT[p % 2]], [bk.buf], inc=(fcl == 3))
                        DVE(lambda j=j, dh=dh, bk=bk: nc.vector.tensor_tensor(
                            out=X.x[:, j, dh * 512:(dh + 1) * 512], in0=X.x[:, j, dh * 512:(dh + 1) * 512],
                            in1=bk.ap[:, :], op=ALU.add), [X.bx[j], bk.buf], [X.bx[j]])
                        yield 1.05
                yield "P"

            yield from up(0)
            for p in range(1, 8):
                yield from up(p)
                yield from down(p - 1)
            yield from down(7)
            DVE(lambda: nc.vector.memset(stat[:, 12:16], 0.0), [], [b_stat])
            for j in range(nsub):
                ACT(lambda j=j: nc.scalar.activation(out=rtmp[:].rearrange("p a b -> p (a b)").bitcast(BF16)[:, 0:D],
                                                     in_=X.x[:, j, :], func=AF.Square,
                                                     accum_out=stat[:, 12 + j:13 + j]), [X.bx[j]], [b_rtmp[0], b_stat],
                    strict=True)
                yield 0.5
            ACT(lambda: nc.scalar.activation(out=stat[:, 12:12 + nsub], in_=stat[:, 12:12 + nsub], func=AF.Ln,
                                             scale=1.0 / D, bias=EPS), [b_stat], [b_stat])
            ACT(lambda: nc.scalar.activation(out=stat[:, 12:12 + nsub], in_=stat[:, 12:12 + nsub], func=AF.Exp,
                                             scale=-0.5), [b_stat], [b_stat])
            for j in range(nsub):
                DVE(lambda j=j: nc.vector.scalar_tensor_tensor(out=X.x[:, j, :], in0=X.x[:, j, :],
                                                               scalar=stat[:, 12 + j:13 + j], in1=fnw[:],
                                                               op0=ALU.mult, op1=ALU.mult),
                    [X.bx[j], b_stat, b_fnw], [X.bx[j]])
                fw.dma(fw.sp, y_rows(j), X.x[:, j, :], reads=[X.bx[j]])
                yield 0.5

        def state_out(si, dst):
            bk = nextbank()
            for hl in range(4):
                PE(lambda hl=hl: nc.tensor.transpose(bk.ap[0:64, hl * 128:(hl + 1) * 128], hT[si][:, hl * 64:(hl + 1) * 64],
                                                     identf[:, :]), [b_hT[si], b_identf], [bk.buf], inc=(hl == 3))
            DVE(lambda: nc.vector.tensor_copy(out=stT[:].rearrange("p a b -> p (a b)"), in_=bk.ap[0:64, :]),
                [bk.buf], [b_stT])
            for g in range(2):
                fw.dma(fw.sp, dst[4 * g:4 * g + 4].rearrange("h p n -> p h n"), stT[:, :, 64 * g:64 * g + 64],
                       reads=[b_stT])

        def state_in(si, src):
            for g in range(2):
                fw.dma(fw.sp, stT[:, :, 64 * g:64 * g + 64], src[4 * g:4 * g + 4].rearrange("h p n -> p h n"),
                       writes=[b_stT])
            bk = nextbank()
            for hl in range(4):
                PE(lambda hl=hl: nc.tensor.transpose(bk.ap[:, hl * 64:(hl + 1) * 64], stT[:, hl, :], identf[0:64, 0:64]),
                   [b_stT, b_identf], [bk.buf], inc=(hl == 3))
            DVE(lambda: nc.vector.tensor_copy(out=hT[si][:], in_=bk.ap[:, 0:256]), [bk.buf], [b_hT[si]])
            ACT(lambda: nc.scalar.copy(out=hTb[si][:], in_=bk.ap[:, 0:256]), [bk.buf], [b_hTb[si]])

        def conv_state_out(nseg, L, dst_of_seg):
            for s in range(nseg):
                for c in range(6):
                    fw.dma(fw.sp, dst_of_seg(s)[:, c * 128:(c + 1) * 128].rearrange("r p -> p r"),
                           xbcT[:, c, s * (3 + L) + L: s * (3 + L) + L + 3], reads=[b_xbcT[c]],
                           allow_slow_non_contiguous=True)

        T = 512
        back = {"gen": None, "credit": 0.0, "left": 0.0}

        def tick(us):
            g = back["gen"]
            if g is None:
                return
            back["credit"] += us
            while back["credit"] > 0:
                try:
                    v = next(g)
                    if v != "P":
                        back["credit"] -= v
                        back["left"] -= v
                except StopIteration:
                    back["gen"] = None
                    back["credit"] = 0.0
                    return

        def tick_pieces(k):
            g = back["gen"]
            if g is None:
                return
            while k > 0:
                v = next(g)
                if v == "P":
                    k -= 1
                else:
                    back["left"] -= v

        def drain():
            g = back["gen"]
            if g is not None:
                for _ in g:
                    pass
            back["gen"] = None
            back["credit"] = 0.0

        tiles = []
        for b in range(NBP):
            for Q in range(SEQ // T):
                tiles.append(("p", b, Q))
        if NBS:
            tiles.append(("s", 0, 0))

        def load_x(n):
            kind, b, Q = tiles[n]
            X = XS[n % 2]
            if kind == "p":
                r0 = b * SEQ + Q * T
                fw.dma(fw.sp, X.x[:, :, :], xp[r0:r0 + T, :].rearrange("(j p) d -> p j d", p=128), writes=list(X.bx))
                return 4
            nsub_ = NBS // 2
            fw.dma(fw.sp, X.x[:, 0:nsub_, :], xs[:, :].rearrange("(j p) d -> p j d", p=128), writes=list(X.bx[0:nsub_]))
            return nsub_

        nsub0 = load_x(0)
        norm_T(nsub0, P_N1W, XS[0])
        for n, (kind, b, Q) in enumerate(tiles):
            X = XS[n % 2]
            if kind == "p":
                r0 = b * SEQ + Q * T
                tick_pieces(2)
                fw.inherit(conv_bufs, attn_bufs)
                if Q == 0:
                    DVE(lambda: nc.vector.memset(hT[0][:], 0.0), [], [b_hT[0]])
                    DVE(lambda: nc.vector.memset(hTb[0][:], 0.0), [], [b_hTb[0]])
                    DVE(lambda: nc.vector.memset(xbcT[:, :, 0:3], 0.0), [], list(b_xbcT))
                else:
                    DVE(lambda: nc.vector.tensor_copy(out=xbcT[:, :, 0:3], in_=cstate[:]), [b_cstate], list(b_xbcT))
                in_proj(T, 4, 1, T,
                        kT_dst=lambda cc, Q=Q: kT[:, cc, Q * T:(Q + 1) * T], kT_bufs=[b_kT[0], b_kT[1]],
                        k_rows=lambda j, r0=r0: kp[r0 + j * 128: r0 + (j + 1) * 128, :],
                        v_rows=lambda j, r0=r0: vp[r0 + j * 128: r0 + (j + 1) * 128, :],
                        v_dst=lambda j, Q=Q: v_bf[:, 4 * Q + j, :], v_bufs=[b_v[0], b_v[1]], X=X)
                DVE(lambda: nc.vector.tensor_copy(out=cstate[:], in_=xbcT[:, :, T:T + 3]), list(b_xbcT), [b_cstate])
                if Q == SEQ // T - 1:
                    conv_state_out(1, T, lambda s_, b=b: convp[b])
                conv_silu(T, 1, T, tick)
                ssd_prepare_dt(4)
                for j in range(4):
                    ssd_subtile(j, 4, (0, 0), True, (0, 0), tick)
                ssd_gate_norm(T, tick)
                if Q == SEQ // T - 1:
                    state_out(0, ssmp[b])
                fw.inherit(attn_bufs, conv_bufs)
                attn_prompt(Q, T, tick)
                drain()
                Tn, nsub = T, 4
                y_rows = (lambda j, r0=r0: yp[r0 + j * 128: r0 + (j + 1) * 128, :])
            else:
                Ts = NBS * 64
                nsub = NBS // 2
                npb = PAST // 128
                tick_pieces(2)
                fw.inherit(conv_bufs, attn_bufs)
                for q in range(NBS):
                    for c in range(6):
                        fw.dma(fw.sp, xbcT[:, c, q * 67:q * 67 + 3],
                               sconv[q][:, c * 128:(c + 1) * 128].rearrange("r p -> p r"),
                               writes=[b_xbcT[c]], allow_slow_non_contiguous=True)
                    state_in(q, sssm[q])
                in_proj(Ts, nsub, NBS, 64,
                        kT_dst=lambda cc: kTn[:, cc, 0:Ts], kT_bufs=[b_kTn],
                        k_rows=lambda j: ksm[j * 128:(j + 1) * 128, :],
                        v_rows=lambda j: vsm[j * 128:(j + 1) * 128, :],
                        v_dst=lambda j: vnew[:, j, :], v_bufs=[b_vnew], X=X)
                conv_state_out(NBS, 64, lambda s_: convs[s_])
                conv_silu(Ts, NBS, 64, tick)
                ssd_prepare_dt(nsub)
                for j in range(nsub):
                    ssd_subtile(j, nsub, (2 * j, 2 * j + 1), False, (2 * j, 2 * j + 1), tick)
                ssd_gate_norm(Ts, tick)
                for q in range(NBS):
                    state_out(q, ssms[q])
                fw.inherit(attn_bufs, conv_bufs)
                drain()
                for pr in range(NBS // 2):
                    load_past(0, 2 * pr, npb)
                    load_past(1, 2 * pr + 1, npb)
                    attn_sample_pair(2 * pr, 2 * pr + 1, npb)
                Tn = Ts
                y_rows = (lambda j: ys[j * 128:(j + 1) * 128, :])
            if n + 1 < len(tiles):
                nsub_n = load_x(n + 1)
                out_proj_norm2(Tn, nsub, X, XS[(n + 1) % 2], nsub_n)
            else:
                out_proj_norm2(Tn, nsub, X)
            back["gen"] = mlp_gen(Tn, nsub, X, y_rows)
            back["left"] = 140.0 * Tn / 512
        drain()

        fw.finish()
        build.nins = fw.nins
    return nc


def pack_params(norm1_w, norm2_w, conv_w, conv_b, dt_bias, a_log, d_skip, ssm_norm_w):
    prm = np.zeros((128, NPRM), np.float32)
    prm[:, P_N1W:P_N1W + 8] = norm1_w.reshape(8, 128).T
    prm[:, P_N2W:P_N2W + 8] = norm2_w.reshape(8, 128).T
    prm[:, P_CW:P_CW + 24] = conv_w.reshape(4, 6, 128).transpose(2, 1, 0).reshape(128, 24)
    prm[:, P_CB:P_CB + 6] = conv_b.reshape(6, 128).T
    prm[:, P_DSK:P_DSK + 4] = np.repeat(d_skip, 64).reshape(4, 128).T
    prm[:, P_SNW:P_SNW + 4] = ssm_norm_w.reshape(4, 128).T
    prm[:, P_DTB:P_DTB + 8] = np.broadcast_to(dt_bias.reshape(1, 8), (128, 8))
    prm[:, P_ALOG:P_ALOG + 8] = np.broadcast_to(a_log.reshape(1, 8), (128, 8))
    return prm


def make_in_maps(inputs, ncores, NBP, NBS):
    f = lambda a: np.ascontiguousarray(np.asarray(a, dtype=np.float32))
    x_prompt = f(inputs["x_prompt"])
    x_sample = f(inputs["x_sample"])
    cache_k = f(inputs["cache_k"])[0]
    cache_v = f(inputs["cache_v"])[0]
    state_conv = f(inputs["state_conv"])[0]
    state_ssm = f(inputs["state_ssm"])[0]
    SEQ = x_prompt.shape[1]
    PAST = cache_k.shape[1]
    prm = pack_params(f(inputs["norm1_w"])[0], f(inputs["norm2_w"])[0], f(inputs["conv_w"])[0], f(inputs["conv_b"])[0],
                      f(inputs["dt_bias"])[0], f(inputs["a_log"])[0], f(inputs["d_skip"])[0], f(inputs["ssm_norm_w"])[0])
    fnw = np.ascontiguousarray(np.broadcast_to(f(inputs["final_norm_w"]).reshape(1, D), (128, D)))
    shared = {
        "w_in": f(inputs["w_in"])[0], "w_out": f(inputs["w_out"])[0], "w_up": f(inputs["w_up"])[0],
        "w_down": f(inputs["w_down"])[0], "prm": prm, "fnw": fnw, "cst": make_consts(),
        "identf": np.eye(128, dtype=np.float32),
    }
    maps = []
    for c in range(ncores):
        m = dict(shared)
        m["xp"] = np.ascontiguousarray(x_prompt[c * NBP:(c + 1) * NBP].reshape(NBP * SEQ, D))
        m["xs"] = np.ascontiguousarray(x_sample[c * NBS:(c + 1) * NBS].reshape(NBS * 64, D))
        m["ck"] = np.ascontiguousarray(cache_k[c * NBS:(c + 1) * NBS].reshape(NBS, PAST, 512))
        m["cv"] = np.ascontiguousarray(cache_v[c * NBS:(c + 1) * NBS].reshape(NBS, PAST, 512))
        m["sconv"] = np.ascontiguousarray(state_conv[c * NBS:(c + 1) * NBS])
        m["sssm"] = np.ascontiguousarray(state_ssm[c * NBS:(c + 1) * NBS])
        maps.append(m)
    return maps


def gather(results, NBP, SEQ, NBS):
    cat = lambda k: np.concatenate([np.asarray(r[k]) for r in results], axis=0)
    nb = NBP * len(results)
    ns = NBS * len(results)
    y_prompt = cat("yp").reshape(nb, SEQ, D)
    y_sample = cat("ys").reshape(ns, 64, D)
    k_prompt = cat("kp").reshape(1, nb, SEQ, 8, 64)
    v_prompt = cat("vp").reshape(1, nb, SEQ, 8, 64)
    conv_prompt = cat("convp").reshape(1, nb, 3, 768)
    ssm_prompt = cat("ssmp").reshape(1, nb, 8, 64, 64)
    k_sample = cat("ksm").reshape(1, ns, 64, 8, 64)
    v_sample = cat("vsm").reshape(1, ns, 64, 8, 64)
    conv_sample = cat("convs").reshape(1, ns, 3, 768)
    ssm_sample = cat("ssms").reshape(1, ns, 8, 64, 64)
    return tuple(np.ascontiguousarray(a.astype(np.float32)) for a in (
        y_prompt, y_sample, k_prompt, v_prompt, conv_prompt, ssm_prompt, k_sample, v_sample, conv_sample, ssm_sample))


def kernel(**inputs):
    NBP, NBS = 4, 4
    SEQ = int(np.asarray(inputs["x_prompt"]).shape[1])
    PAST = int(np.asarray(inputs["cache_k"]).shape[2])
    nc = build(NBP=NBP, SEQ=SEQ, NBS=NBS, PAST=PAST)
    in_maps = make_in_maps(inputs, NCORES, NBP, NBS)
    res = run_bass_kernel_spmd(nc, in_maps, core_ids=list(range(NCORES)))
    return gather(res.results, NBP, SEQ, NBS)
```

```python
import numpy as np
from contextlib import ExitStack
import concourse.bass as bass
import concourse.mybir as mybir
from concourse.bass_utils import run_bass_kernel_spmd

F32 = mybir.dt.float32
BF16 = mybir.dt.bfloat16
AF = mybir.ActivationFunctionType
ALU = mybir.AluOpType

NCORES = 8
D = 1024
INP = 2824
DFF = 4096
EPS = 1e-5
NEG = -30000.0


class Buf:
    __slots__ = ("name", "w", "r", "excl")

    def __init__(self, name, excl=False):
        self.name = name
        self.w = None
        self.r = {}
        self.excl = excl


class Eng:
    def __init__(self, name, h, sem):
        self.name = name
        self.h = h
        self.sem = sem
        self.count = 0
        self.seen = {}


class FW:
    def __init__(self, nc, es, n_dma_sems=40):
        self.nc = nc
        self.sems = {}

        def mk(name):
            s = es.enter_context(nc.semaphore(name))
            self.sems[name] = s
            return s

        self.pe = Eng("pe", nc.tensor, mk("pe"))
        self.act = Eng("act", nc.scalar, mk("act"))
        self.dve = Eng("dve", nc.vector, mk("dve"))
        self.pool = Eng("pool", nc.gpsimd, mk("pool"))
        self.sp = Eng("sp", nc.sync, mk("sp"))
        self.engs = {e.name: e for e in (self.pe, self.act, self.dve, self.pool, self.sp)}
        self.dsem = [mk("d%d" % i) for i in range(n_dma_sems)]
        self.dcnt = [0] * n_dma_sems
        self.n_sw = 8
        self.dnext = {"hw": 0, "sw": 0}
        self.nins = 0

    def _need(self, eng, tok, needs):
        if tok is None:
            return
        key, val, _ = tok
        if eng.seen.get(key, 0) >= val:
            return
        if needs.get(key, 0) < val:
            needs[key] = val

    def _waits(self, eng, reads, writes, is_dma=False, strict=False):
        needs = {}
        for b in reads:
            self._need(eng, b.w, needs)
            if b.excl:
                for k, t in b.r.items():
                    if k != eng.name:
                        self._need(eng, t, needs)
        for b in writes:
            if b.w is not None and (is_dma or strict or b.w[2] != eng.name):
                self._need(eng, b.w, needs)
            for k, t in b.r.items():
                if is_dma or k != eng.name:
                    self._need(eng, t, needs)
        for key, val in needs.items():
            if key in self.engs:
                assert val <= self.engs[key].count, "wait on future inc %s %d > %d" % (key, val, self.engs[key].count)
            eng.h.wait_ge(self.sems[key], val)
            eng.seen[key] = val
            self.nins += 1

    def op(self, eng, fn, reads=(), writes=(), inc=True, strict=False):
        self._waits(eng, reads, writes, strict=strict)
        ins = fn()
        self.nins += 1
        if inc:
            ins.then_inc(eng.sem, 1)
            eng.count += 1
            tok = (eng.name, eng.count, eng.name)
        else:
            tok = (eng.name, eng.count + 1, eng.name)
        for b in reads:
            b.r[eng.name] = tok
        for b in writes:
            b.w = tok
            b.r = {}
        return ins

    def dma(self, eng, out, in_, reads=(), writes=(), **kw):
        if eng.name == "pool":
            i = self.dnext["sw"]
            self.dnext["sw"] = (i + 1) % self.n_sw
        else:
            i = self.n_sw + self.dnext["hw"]
            self.dnext["hw"] = (self.dnext["hw"] + 1) % (len(self.dsem) - self.n_sw)
        key = "d%d" % i
        prev = self.dcnt[i] * 16
        if prev and eng.seen.get(key, 0) < prev:
            eng.h.wait_ge(self.dsem[i], prev)
            eng.seen[key] = prev
            self.nins += 1
        self._waits(eng, reads, writes, is_dma=True)
        ins = eng.h.dma_start(out=out, in_=in_, **kw)
        self.nins += 1
        ins.then_inc(self.dsem[i], 16)
        self.dcnt[i] += 1
        tok = (key, self.dcnt[i] * 16, "dma%d_%d" % (i, self.dcnt[i]))
        for b in reads:
            b.r[tok[2]] = tok
        for b in writes:
            b.w = tok
            b.r = {}
        return tok

    def inherit(self, new_bufs, old_bufs):
        toks = {}
        for ob in old_bufs:
            if ob.w is not None:
                toks["w_" + ob.w[2] + ob.name] = ob.w
            for k, t in ob.r.items():
                toks[k + "_" + ob.name] = t
        for nb in new_bufs:
            nb.w = None
            nb.r = dict(toks)

    def finish(self):
        for i, s in enumerate(self.dsem):
            if self.dcnt[i]:
                self.sp.h.wait_ge(s, self.dcnt[i] * 16)


CST_NAMES = ["ident", "ones", "negLinc", "negLstr", "amask", "tri_le", "negtri_le", "trichunk", "triU",
             "negLinc_blk", "negL2", "amask_s"]
NCST = len(CST_NAMES) * 128 + 512


def make_consts():
    k = np.arange(128)[:, None]
    j = np.arange(128)[None, :]
    same = (k // 64) == (j // 64)
    m = {}
    m["ident"] = (k == j)
    m["ones"] = np.ones((128, 128), bool)
    m["negLinc"] = -(k >= j).astype(np.float32)
    m["negLstr"] = -(k < j).astype(np.float32)
    m["amask"] = (k < j)
    m["tri_le"] = (k <= j)
    m["negtri_le"] = -(k <= j).astype(np.float32)
    m["trichunk"] = (k <= j) & same
    m["triU"] = (k > j) & same
    m["negLinc_blk"] = -((k >= j) & same).astype(np.float32)
    m["negL2"] = np.where(same, -(k < j).astype(np.float32), -1.0)
    m["amask_s"] = ((k % 64) < j)
    cols = [np.asarray(m[n], np.float32) for n in CST_NAMES]
    negmask = np.where(same & (j >= k), 0.0, NEG).astype(np.float32)
    cols.append(np.tile(negmask, (1, 4)))
    return np.ascontiguousarray(np.concatenate(cols, axis=1).astype(np.float32))


P_N1W, P_N2W, P_CW, P_CB, P_DSK, P_SNW, P_DTB, P_ALOG = 0, 8, 16, 40, 46, 50, 54, 62
NPRM = 70


def build(NBP=4, SEQ=2048, NBS=4, PAST=1024, DSEQ=64):
    assert DSEQ == 64 and NBS % 2 == 0 and SEQ % 512 == 0 and PAST % 128 == 0
    nc = bass.Bass("TRN2", target_bir_lowering=False)

    def din(name, shape, dt=F32):
        return nc.dram_tensor(name, shape, dt, kind="ExternalInput").ap()

    def dout(name, shape):
        return nc.dram_tensor(name, shape, F32, kind="ExternalOutput").ap()

    def dint(name, shape, dt):
        return nc.dram_tensor(name, shape, dt, kind="Internal").ap()

    xp = din("xp", [NBP * SEQ, D])
    xs = din("xs", [NBS * DSEQ, D])
    ck = din("ck", [NBS, PAST, 512])
    cv = din("cv", [NBS, PAST, 512])
    sconv = din("sconv", [NBS, 3, 768])
    sssm = din("sssm", [NBS, 8, 64, 64])
    w_in = din("w_in", [D, INP])
    w_out = din("w_out", [D, D])
    w_up = din("w_up", [D, DFF])
    w_down = din("w_down", [DFF, D])
    prm_d = din("prm", [128, NPRM])
    fnw_d = din("fnw", [128, D])
    cst_d = din("cst", [128, NCST])
    identf_d = din("identf", [128, 128])

    yp = dout("yp", [NBP * SEQ, D])
    ys = dout("ys", [NBS * DSEQ, D])
    kp = dout("kp", [NBP * SEQ, 512])
    vp = dout("vp", [NBP * SEQ, 512])
    convp = dout("convp", [NBP, 3, 768])
    ssmp = dout("ssmp", [NBP, 8, 64, 64])
    ksm = dout("ksm", [NBS * DSEQ, 512])
    vsm = dout("vsm", [NBS * DSEQ, 512])
    convs = dout("convs", [NBS, 3, 768])
    ssms = dout("ssms", [NBS, 8, 64, 64])

    w_in_bf = dint("w_in_bf", [D, INP], BF16)
    w_out_bf = dint("w_out_bf", [D, D], BF16)
    w_up_bf = dint("w_up_bf", [D, DFF], BF16)
    w_down_bf = dint("w_down_bf", [DFF, D], BF16)

    es = ExitStack()
    with es:
        fw = FW(nc, es)

        def sb(name, shape, dt=F32):
            return es.enter_context(nc.sbuf_tensor("s_" + name, shape, dt))

        def PE(fn, reads, writes, inc=True):
            return fw.op(fw.pe, fn, reads, writes, inc)

        def ACT(fn, reads, writes, strict=False):
            return fw.op(fw.act, fn, reads, writes, strict=strict)

        def DVE(fn, reads, writes):
            return fw.op(fw.dve, fn, reads, writes)

        def POOL(fn, reads, writes):
            return fw.op(fw.pool, fn, reads, writes)

        def mm(out, lhsT, rhs, start, stop, reads, writes, inc=True, serial=False):
            if serial:
                nc.tensor.wait_ge(fw.pe.sem, fw.pe.count)
                fw.pe.seen["pe"] = fw.pe.count
            return PE(lambda: nc.tensor.matmul(out, lhsT=lhsT, rhs=rhs, start=start, stop=stop,
                                               skip_group_check=True), reads, writes, inc)

        class Bank:
            pass

        banks = []
        for i in range(8):
            b = Bank()
            b.t = es.enter_context(nc.psum_tensor("bank%d" % i, [128, 512], F32))
            b.ap = b.t[:]
            b.bf = b.t[:].bitcast(BF16)
            b.buf = Buf("bank%d" % i, excl=True)
            banks.append(b)
        bank_rr = [0, 0]

        def nextbank():
            b = banks[bank_rr[0]]
            bank_rr[0] = (bank_rr[0] + 1) % 6
            return b

        def nextbank_mlp():
            b = banks[6 + bank_rr[1]]
            bank_rr[1] = (bank_rr[1] + 1) % 2
            return b

        cst = sb("cst", [128, NCST], BF16)
        b_cst = Buf("cst")
        identf = sb("identf", [128, 128])
        b_identf = Buf("identf")
        prm = sb("prm", [128, NPRM])
        b_prm = Buf("prm")
        fnw = sb("fnw", [128, D])
        b_fnw = Buf("fnw")
        aneg = sb("aneg", [128, 8])
        b_aneg = Buf("aneg")

        def C(name, r0=0, r1=128, c0=0, c1=128):
            o = CST_NAMES.index(name) * 128
            return cst[r0:r1, o + c0:o + c1]

        negmaskD = cst[:, len(CST_NAMES) * 128: len(CST_NAMES) * 128 + 512]

        NSLOT = 3
        ring = [sb("ring%d" % i, [128, 4096], BF16) for i in range(NSLOT)]
        b_ring = [Buf("ring%d" % i) for i in range(NSLOT)]

        kT = sb("kT", [128, 4, 2048], BF16)
        b_kT = [Buf("kT0"), Buf("kT1")]
        v_bf = sb("v_bf", [128, 16, 512], BF16)
        b_v = [Buf("v0"), Buf("v1")]
        vnew = sb("vnew", [128, 2, 512], BF16)
        b_vnew = Buf("vnew")
        kTn = sb("kTn", [128, 4, 256], BF16)
        b_kTn = Buf("kTn")

        class XB:
            pass

        XS = []
        for i in range(2):
            X = XB()
            X.x = sb("x_sb%d" % i, [128, 4, D])
            X.bx = [Buf("x%d_%d" % (i, j)) for j in range(4)]
            X.xnT = sb("xnT%d" % i, [128, 8, 512], BF16)
            X.bxn = Buf("xnT%d" % i)
            XS.append(X)
        rtmp = sb("rtmp", [128, 2, 512])
        b_rtmp = [Buf("rtmp0"), Buf("rtmp1")]
        stat = sb("stat", [128, 16])
        b_stat = Buf("stat")

        qpad = [sb("qpad%d" % e, [128, 4, 512], BF16) for e in range(2)]
        b_qT = Buf("qT")
        mixT = sb("mixT", [128, 8, 512], BF16)
        b_mix = [Buf("mix%d" % c) for c in range(8)]
        uT = sb("uT", [128, 2, 4, 512], BF16)
        b_uT = [Buf("uT0"), Buf("uT1")]

        scrB = sb("scrB", [128, 1024])
        b_scrB = [Buf("scrB0"), Buf("scrB1")]
        zT = sb("zT", [128, 4, 512])
        b_z = [Buf("z%d" % c) for c in range(4)]
        scrC = sb("scrC", [128, 4352])
        xbc_bf = sb("xbc_bf", [128, 6, 512], BF16)
        b_xbcbf = [Buf("xbcbf%d" % c) for c in range(6)]
        scrA = sb("scrA", [128, 4, 512])
        b_scrA = [Buf("scrA%d" % c) for c in range(4)]
        cstate = sb("cstate", [128, 6, 3])
        b_cstate = Buf("cstate")

        xbcT = scrC[:, 0:6 * 520].rearrange("p (c n) -> p c n", c=6)
        b_xbcT = [Buf("xbcT%d" % c) for c in range(6)]
        cacc = [scrC[:, 3120 + i * 520: 3120 + (i + 1) * 520] for i in range(2)]
        b_cacc = [Buf("cacc0"), Buf("cacc1")]
        NE, NG, NSP, NA = 3, 2, 4, 3
        e_sb = [scrC[:, i * 512:(i + 1) * 512] for i in range(NE)]
        b_e = [Buf("e%d" % i) for i in range(NE)]
        g_sb = [scrC[:, 1536 + i * 512: 1536 + (i + 1) * 512] for i in range(NG)]
        b_g = [Buf("g%d" % i) for i in range(NG)]
        scrC_bf = scrC[:, 2560:4352].bitcast(BF16)
        sp_sb = [scrC_bf[:, i * 512:(i + 1) * 512] for i in range(NSP)]
        b_sp = [Buf("sp%d" % i) for i in range(NSP)]
        A_sb = [scrC_bf[:, 2048 + i * 512: 2048 + (i + 1) * 512] for i in range(NA)]
        b_A = [Buf("A%d" % i) for i in range(NA)]
        conv_bufs = b_xbcT + b_cacc
        attn_bufs = b_e + b_g + b_sp + b_A

        dts = sb("dts", [128, 3, 4, 8])
        b_dts = Buf("dts")
        a_bf = sb("a_bf", [128, 4, 8], BF16)
        b_abf = Buf("a_bf")
        abc = sb("abc", [128, 8, 128], BF16)
        b_abc = Buf("abc")
        AT1 = sb("AT1", [128, 8, 128], BF16)
        b_AT1 = Buf("AT1")
        M_bf = sb("M_bf", [128, 8, 128], BF16)
        b_M = Buf("M")
        xs_bf = [AT1[:].rearrange("p h l -> p (h l)"), M_bf[:].rearrange("p h l -> p (h l)")]
        b_xsbf = [b_AT1, b_M]
        xc = sb("xc", [128, 512], BF16)
        b_xc = Buf("xc")
        xcd = sb("xcd", [128, 512], BF16)
        b_xcd = Buf("xcd")
        Btok = sb("Btok", [128, 128], BF16)
        b_Btok = Buf("Btok")
        eaT = sb("eaT", [128, 4, 128])
        b_eaT = Buf("eaT")
        ytmp = sb("ytmp", [128, 4, 128])
        b_ytmp = Buf("ytmp")
        dcd = sb("dcd", [128, 16])
        b_dcd = Buf("dcd")
        hT = [sb("hT%d" % i, [128, 256]) for i in range(max(NBS, 1))]
        b_hT = [Buf("hT%d" % i) for i in range(max(NBS, 1))]
        hTb = [sb("hTb%d" % i, [128, 256], BF16) for i in range(max(NBS, 1))]
        b_hTb = [Buf("hTb%d" % i) for i in range(max(NBS, 1))]
        htmp = sb("htmp", [128, 256])
        b_htmp = Buf("htmp")
        stT = ytmp[0:64, :, :]
        b_stT = b_ytmp
        sq_bf = abc[:].rearrange("p h l -> p (h l)").rearrange("p (a b) -> p a b", a=2)
        b_sq = [b_abc, b_abc]

        fw.dma(fw.pool, cst[:], cst_d[:, :], writes=[b_cst])
        fw.dma(fw.sp, identf[:], identf_d[:, :], writes=[b_identf])
        fw.dma(fw.sp, prm[:], prm_d[:, :], writes=[b_prm])
        fw.dma(fw.sp, fnw[:], fnw_d[:, :], writes=[b_fnw])
        b_win = [Buf("win%d" % i) for i in range(8)]
        b_wout = [Buf("wout%d" % i) for i in range(8)]
        b_wup = [Buf("wup%d" % i) for i in range(8)]
        b_wdn = [Buf("wdn%d" % i) for i in range(32)]
        for i in range(8):
            fw.dma(fw.pool, w_in_bf[i * 128:(i + 1) * 128, :], w_in[i * 128:(i + 1) * 128, :], writes=[b_win[i]])
        for i in range(8):
            fw.dma(fw.pool, w_out_bf[i * 128:(i + 1) * 128, :], w_out[i * 128:(i + 1) * 128, :], writes=[b_wout[i]])
        for i in range(8):
            fw.dma(fw.pool, w_up_bf[i * 128:(i + 1) * 128, :], w_up[i * 128:(i + 1) * 128, :], writes=[b_wup[i]])
        for i in range(32):
            fw.dma(fw.pool, w_down_bf[i * 128:(i + 1) * 128, :], w_down[i * 128:(i + 1) * 128, :], writes=[b_wdn[i]])
        for e_ in range(2):
            DVE(lambda e_=e_: nc.vector.memset(qpad[e_][:], 0.0), [], [b_qT])
        ACT(lambda: nc.scalar.activation(out=aneg[:], in_=prm[:, P_ALOG:P_ALOG + 8], func=AF.Exp), [b_prm], [b_aneg])
        DVE(lambda: nc.vector.tensor_scalar(out=aneg[:], in0=aneg[:], scalar1=-1.0, scalar2=None, op0=ALU.mult),
            [b_aneg], [b_aneg])

        w_in_v = w_in_bf.rearrange("(k p) c -> p k c", p=128)
        w_out_v = w_out_bf.rearrange("(k p) c -> p k c", p=128)
        w_up_v = w_up_bf.rearrange("(k p) c -> p k c", p=128)
        w_dn_v = w_down_bf.rearrange("(f p) d -> p f d", p=128)

        def piece_plan():
            pin = []
            for i in range(5):
                pin.append(("in%d" % i, w_in_v[:, :, i * 512:(i + 1) * 512], b_win, (8, 512)))
            pin.append(("in5", w_in_v[:, :, 2560:2824], b_win, (8, 264)))
            pwo = [("wo0", w_out_v[:, :, 0:512], b_wout, (8, 512)), ("wo1", w_out_v[:, :, 512:1024], b_wout, (8, 512))]
            order = ["up0"]
            for p in range(1, 8):
                order += ["up%d" % p, "dn%d" % (p - 1)]
            order.append("dn7")
            pml = []
            for nm in order:
                p = int(nm[2:])
                if nm[:2] == "up":
                    pml.append((nm, w_up_v[:, :, p * 512:(p + 1) * 512], b_wup, (8, 512)))
                else:
                    pml.append((nm, w_dn_v[:, 4 * p:4 * p + 4, :], b_wdn[4 * p:4 * p + 4], (4, 1024)))
            return pin, pwo, pml

        n_super = NBP * (SEQ // 512) + (1 if NBS else 0)
        pin, pwo, pml = piece_plan()
        plan = []
        for i_ in range(n_super):
            if i_ > 0:
                plan += pml[0:2]
            plan += pin
            if i_ > 0:
                plan += pml[2:]
            plan += pwo
        plan += pml
        ws = {"cur": -1, "issued": 0}

        def ws_get(name, hold=0):
            ws["cur"] += 1
            i = ws["cur"]
            assert plan[i][0] == name, (plan[i][0], name)
            while ws["issued"] <= min(i + NSLOT - 1 - hold, len(plan) - 1):
                j = ws["issued"]
                nm, src, sbufs, (a, b) = plan[j]
                dst = ring[j % NSLOT][:, 0:a * b].rearrange("p (k c) -> p k c", k=a)
                fw.dma(fw.sp, dst, src, reads=list(sbufs), writes=[b_ring[j % NSLOT]])
                ws["issued"] += 1
            nm, src, sbufs, (a, b) = plan[i]
            return ring[i % NSLOT][:, 0:a * b].rearrange("p (k c) -> p k c", k=a), b_ring[i % NSLOT]

        b_stats = [Buf("stat%d" % k) for k in range(4)]

        def norm_a(j, X, stg, bstg, slot):
            sc = stat[:, 4 * slot:4 * slot + 4]
            bs = b_stats[slot]
            DVE(lambda: nc.vector.memset(sc[:, 0:1], 0.0), [], [bs])
            ACT(lambda: nc.scalar.activation(out=stg, in_=X.x[:, j, :], func=AF.Square, accum_out=sc[:, 0:1]),
                [X.bx[j]], [bstg, bs])
            ACT(lambda: nc.scalar.activation(out=sc[:, 1:2], in_=sc[:, 0:1], func=AF.Ln, scale=1.0 / D, bias=EPS),
                [bs], [bs])
            ACT(lambda: nc.scalar.activation(out=sc[:, 2:3], in_=sc[:, 1:2], func=AF.Exp, scale=-0.5), [bs], [bs])
            DVE(lambda: nc.vector.tensor_scalar(out=stg, in0=X.x[:, j, :], scalar1=sc[:, 2:3],
                                                scalar2=None, op0=ALU.mult), [X.bx[j], bs], [bstg])

        def norm_b(j, nwcol, X, stg, bstg):
            nw = prm[:, nwcol:nwcol + 8]
            bk = nextbank()
            v3 = bk.bf.rearrange("p (c t) -> p c t", c=8)
            for c in range(8):
                PE(lambda c=c: nc.tensor.transpose(v3[:, c, :], stg[:, c * 128:(c + 1) * 128], C("ident")),
                   [bstg, b_cst], [bk.buf], inc=(c == 7))
            DVE(lambda: nc.vector.tensor_tensor(out=X.xnT[:, :, j * 128:(j + 1) * 128], in0=v3,
                                                in1=nw.unsqueeze(2).to_broadcast([128, 8, 128]),
                                                op=ALU.mult), [bk.buf, b_prm], [X.bxn])

        def norm1_stage(c):
            return scrA[:, c, :].bitcast(BF16), b_scrA[c]

        def norm_T(nsub, nwcol, X, tick=None):
            for j in range(nsub):
                stg, bstg = norm1_stage(j)
                norm_a(j, X, stg, bstg, 2)
                norm_b(j, nwcol, X, stg, bstg)

        def in_proj(T, nsub, nseg, L, kT_dst, kT_bufs, k_rows, v_rows, v_dst, v_bufs, X):
            segw = 3 + L
            xnT = X.xnT
            b_xnT = X.bxn

            def feat_group(w, wbuf, col0, T, evac):
                bk = nextbank()
                for kc in range(8):
                    mm(bk.ap[:, 0:T], w[:, kc, col0:col0 + 128], xnT[:, kc, 0:T], kc == 0, kc == 7,
                       [wbuf, b_xnT], [bk.buf], inc=(kc == 7))
                evac(bk)

            def tok_group(w, wbuf, c0, n, j, evac):
                bk = nextbank()
                for kc in range(8):
                    mm(bk.ap[:, 0:n], xnT[:, kc, j * 128:(j + 1) * 128], w[:, kc, c0:c0 + n], kc == 0, kc == 7,
                       [wbuf, b_xnT], [bk.buf], inc=(kc == 7))
                evac(bk)

            def xbc_dst(c):
                return xbcT[:, c, 0:nseg * segw].rearrange("p (s l) -> p s l", l=segw)[:, :, 3:3 + L]

            w, wb = ws_get("in0")
            for cc in range(4):
                def evq(bk, cc=cc):
                    for e_ in range(2):
                        pr_ = slice(64 * e_, 64 * e_ + 64)
                        ACT(lambda: nc.scalar.activation(out=qpad[e_][pr_, cc, 0:T], in_=bk.ap[pr_, 0:T], func=AF.Copy,
                                                         scale=0.125), [bk.buf], [b_qT])
                feat_group(w, wb, cc * 128, T, evq)
            w, wb = ws_get("in1")
            for cc in range(4):
                feat_group(w, wb, cc * 128, T, lambda bk, cc=cc: DVE(
                    lambda: nc.vector.tensor_copy(out=kT_dst(cc), in_=bk.ap[:, 0:T]), [bk.buf], kT_bufs))
            for j in range(nsub):
                def ev(bk, j=j):
                    ACT(lambda: nc.scalar.copy(out=scrA[:, j % 2, :], in_=bk.ap[:, :]), [bk.buf], [b_scrA[j % 2]])
                    fw.dma(fw.sp, k_rows(j), scrA[:, j % 2, :], reads=[b_scrA[j % 2]])
                tok_group(w, wb, 0, 512, j, ev)
            w, wb = ws_get("in2")
            for j in range(nsub):
                def ev(bk, j=j):
                    ACT(lambda: nc.scalar.copy(out=scrA[:, 2 + j % 2, :], in_=bk.ap[:, :]), [bk.buf], [b_scrA[2 + j % 2]])
                    DVE(lambda: nc.vector.tensor_copy(out=v_dst(j), in_=bk.ap[:, :]), [bk.buf], v_bufs)
                    fw.dma(fw.sp, v_rows(j), scrA[:, 2 + j % 2, :], reads=[b_scrA[2 + j % 2]])
                tok_group(w, wb, 0, 512, j, ev)
            w, wb = ws_get("in3")
            for cc in range(4):
                feat_group(w, wb, cc * 128, T, lambda bk, cc=cc: ACT(
                    lambda: nc.scalar.activation(out=zT[:, cc, 0:T], in_=bk.ap[:, 0:T], func=AF.Silu),
                    [bk.buf], [b_z[cc]]))
            w, wb = ws_get("in4")
            for cc in range(4):
                feat_group(w, wb, cc * 128, T, lambda bk, cc=cc: DVE(
                    lambda: nc.vector.tensor_copy(out=xbc_dst(cc), in_=bk.ap[:, 0:T].rearrange("p (s l) -> p s l", l=L)),
                    [bk.buf], [b_xbcT[cc]]))
            w, wb = ws_get("in5")
            for cc in range(2):
                feat_group(w, wb, cc * 128, T, lambda bk, cc=cc: DVE(
                    lambda: nc.vector.tensor_copy(out=xbc_dst(4 + cc), in_=bk.ap[:, 0:T].rearrange("p (s l) -> p s l", l=L)),
                    [bk.buf], [b_xbcT[4 + cc]]))
            for j in range(nsub):
                tok_group(w, wb, 256, 8, j, lambda bk, j=j: DVE(
                    lambda: nc.vector.tensor_tensor(out=dts[:, 0, j, :], in0=bk.ap[:, 0:8], in1=prm[:, P_DTB:P_DTB + 8],
                                                    op=ALU.add), [bk.buf, b_prm], [b_dts]))

        def conv_silu(T, nseg, L, tick=None):
            segw = 3 + L
            n = nseg * segw - 3
            for c in range(6):
                if tick:
                    tick(2.0)
                acc = cacc[c % 2]
                ba = b_cacc[c % 2]
                src = xbcT[:, c, :]
                eng = DVE
                h = nc.vector
                cw0 = P_CW + c * 4
                eng(lambda: h.tensor_scalar(out=acc[:, 0:n], in0=src[:, 0:n], scalar1=prm[:, cw0:cw0 + 1],
                                            scalar2=prm[:, P_CB + c:P_CB + c + 1], op0=ALU.mult, op1=ALU.add),
                    [b_xbcT[c], b_prm], [ba])
                for i in range(1, 4):
                    eng(lambda i=i: h.scalar_tensor_tensor(out=acc[:, 0:n], in0=src[:, i:i + n],
                                                           scalar=prm[:, cw0 + i:cw0 + i + 1], in1=acc[:, 0:n],
                                                           op0=ALU.mult, op1=ALU.add), [b_xbcT[c], b_prm, ba], [ba])
                accv = acc[:, 0:nseg * segw].rearrange("p (s l) -> p s l", l=segw)[:, :, 0:L]
                if c < 4:
                    xtmp = scrB[:, (c % 2) * 512:(c % 2) * 512 + T]
                    ACT(lambda: nc.scalar.activation(out=xtmp.rearrange("p (s l) -> p s l", l=L), in_=accv,
                                                     func=AF.Silu), [ba], [b_scrB[c % 2]])
                    POOL(lambda: nc.gpsimd.tensor_copy(out=xbc_bf[:, c, 0:T], in_=xtmp), [b_scrB[c % 2]], [b_xbcbf[c]])
                    ACT(lambda: nc.scalar.activation(out=scrA[:, c, 0:T], in_=xtmp, func=AF.Copy,
                                                     scale=prm[:, P_DSK + c:P_DSK + c + 1]), [b_scrB[c % 2], b_prm], [b_scrA[c]])
                else:
                    ACT(lambda: nc.scalar.activation(out=xbc_bf[:, c, 0:T].rearrange("p (s l) -> p s l", l=L), in_=accv,
                                                     func=AF.Silu), [ba], [b_xbcbf[c]])

        def ssd_subtile(j, nsub_T, st_in, st_mid, st_out, tick=None):
            cols = slice(j * 128, (j + 1) * 128)
            bk_tr, bk_sm, bk_D0, bk_D1, bk_G, bk_ea = banks[0:6]
            bk_yd = banks[4]
            bk_yo = banks[1]
            tk = tick if tick else (lambda us: None)
            trv = bk_tr.bf[:, 0:640].rearrange("p (c t) -> p c t", c=5)
            for c in range(5):
                PE(lambda c=c: nc.tensor.transpose(trv[:, c, :], xbc_bf[:, c, cols], C("ident")),
                   [b_xbcbf[c], b_cst], [bk_tr.buf], inc=(c == 4))
            DVE(lambda: nc.vector.tensor_tensor(out=xc[:].rearrange("p (h d) -> p h d", h=8),
                                                in0=bk_tr.bf[:, 0:512].rearrange("p (h d) -> p h d", h=8),
                                                in1=dts[:, 1, j, :].unsqueeze(2).to_broadcast([128, 8, 64]), op=ALU.mult),
                [bk_tr.buf, b_dts], [b_xc])
            ACT(lambda: nc.scalar.copy(out=Btok[:], in_=bk_tr.bf[:, 512:640]), [bk_tr.buf], [b_Btok])
            tk(1.4)
            DVE(lambda: nc.vector.tensor_copy(out=abc[:], in_=a_bf[:, j, :].unsqueeze(2).to_broadcast([128, 8, 128])),
                [b_abf], [b_abc])
            DVE(lambda: nc.vector.tensor_tensor(out=AT1[:], in0=abc[:],
                                                in1=C("tri_le").unsqueeze(1).to_broadcast([128, 8, 128]), op=ALU.mult),
                [b_abc, b_cst], [b_AT1])
            tk(1.4)
            mm(bk_sm.ap[:, 0:8], C("triU"), a_bf[:, j, :], True, True, [b_cst, b_abf], [bk_sm.buf], inc=False)
            for c in range(2):
                for g in range(2):
                    mm(bk_sm.ap[64 * g:64 * g + 64, 8 + 4 * c:12 + 4 * c], C("ones", 64 * c, 64 * c + 64, 0, 64),
                       a_bf[64 * c:64 * c + 64, j, 4 * g:4 * g + 4], True, True, [b_cst, b_abf], [bk_sm.buf],
                       inc=(g == 1), serial=(c == 1 and g == 0))
            ACT(lambda: nc.scalar.activation(out=dcd[:, 0:16], in_=bk_sm.ap[:, 0:16], func=AF.Exp), [bk_sm.buf], [b_dcd])
            DVE(lambda: nc.vector.tensor_tensor(out=xcd[:].rearrange("p (h d) -> p h d", h=8),
                                                in0=xc[:].rearrange("p (h d) -> p h d", h=8),
                                                in1=dcd[:, 0:8].unsqueeze(2).to_broadcast([128, 8, 64]), op=ALU.mult),
                [b_xc, b_dcd], [b_xcd])
            tk(1.4)
            E_sb = scrB[:].rearrange("p (h l) -> p h l", h=8)
            for half, bk in ((0, bk_D0), (1, bk_D1)):
                hs = slice(4 * half, 4 * half + 4)
                o = bk.ap.rearrange("p (h l) -> p h l", h=4)
                mm(o, C("ones"), AT1[:, hs, :], True, False, [b_cst, b_AT1], [bk.buf], inc=False)
                mm(o, C("negtri_le"), abc[:, hs, :], False, False, [b_cst, b_abc], [bk.buf], inc=False)
                mm(bk.ap, C("ident"), negmaskD, False, True, [b_cst], [bk.buf])
                ACT(lambda hs=hs, o=o: nc.scalar.activation(out=E_sb[:, hs, :], in_=o, func=AF.Exp),
                    [bk.buf], [b_scrB[half]])
            tk(1.4)
            Gv = bk_G.ap[:, 0:256].rearrange("p (g l) -> p g l", g=2)
            for g in range(2):
                mm(Gv[:, g, :], xbc_bf[64 * g:64 * g + 64, 4, cols], xbc_bf[64 * g:64 * g + 64, 5, cols], True, True,
                   [b_xbcbf[4], b_xbcbf[5]], [bk_G.buf], inc=True, serial=(g == 1))
            for g in range(2):
                DVE(lambda g=g: nc.vector.tensor_tensor(out=M_bf[:, 4 * g:4 * g + 4, :], in0=E_sb[:, 4 * g:4 * g + 4, :],
                                                        in1=Gv[:, g, :].unsqueeze(1).to_broadcast([128, 4, 128]),
                                                        op=ALU.mult), [b_scrB[g], bk_G.buf], [b_M])
            tk(1.4)
            abcf = abc[:].rearrange("p h l -> p (h l)")
            eav = bk_ea.ap.rearrange("p (h l) -> p h l", h=4)
            for hp in range(4):
                mm(eav[:, hp, :], abcf[:, 2 * hp * 128 + 64: 2 * hp * 128 + 192], C("trichunk"), True, True,
                   [b_abc, b_cst], [bk_ea.buf], inc=(hp == 3))
            ACT(lambda: nc.scalar.activation(out=eaT[:], in_=eav, func=AF.Exp), [bk_ea.buf], [b_eaT])
            tk(1.4)
            ydv = bk_yd.ap.rearrange("p (h l) -> p h l", h=4)
            for hp in range(4):
                for e in range(2):
                    hd = 2 * hp + e
                    mm(ydv[64 * e:64 * e + 64, hp, :], xc[:, hd * 64:(hd + 1) * 64], M_bf[:, hd, :], True, True,
                       [b_xc, b_M], [bk_yd.buf], inc=(hp == 3 and e == 1))
            tk(1.4)
            yov = bk_yo.ap.rearrange("p (h l) -> p h l", h=4)
            for c in range(2):
                si = st_in[c]
                ccols = slice(j * 128 + 64 * c, j * 128 + 64 * c + 64)
                for hp in range(4):
                    g = hp // 2
                    mm(yov[:, hp, 64 * c:64 * c + 64], hTb[si][64 * g:64 * g + 64, (hp % 2) * 128:(hp % 2) * 128 + 128],
                       xbc_bf[64 * g:64 * g + 64, 5, ccols], True, True, [b_hTb[si], b_xbcbf[5]], [bk_yo.buf],
                       inc=(hp % 2 == 1), serial=(hp == 2))
                tk(1.4)
                so = st_out[c]
                stv = bk_tr.ap[:, 0:256]
                for g in range(2):
                    mm(stv[64 * g:64 * g + 64, :], Btok[64 * c:64 * c + 64, 64 * g:64 * g + 64],
                       xcd[64 * c:64 * c + 64, 256 * g:256 * g + 256], True, True, [b_Btok, b_xcd], [bk_tr.buf],
                       inc=(g == 1))
                DVE(lambda si=si, c=c: nc.vector.tensor_tensor(
                    out=htmp[:].rearrange("p (h d) -> p h d", h=4), in0=hT[si][:].rearrange("p (h d) -> p h d", h=4),
                    in1=dcd[:, 8 + 4 * c:12 + 4 * c].unsqueeze(2).to_broadcast([128, 4, 64]), op=ALU.mult),
                    [b_hT[si], b_dcd], [b_htmp])
                DVE(lambda so=so: nc.vector.tensor_tensor(out=hT[so][:], in0=htmp[:], in1=stv, op=ALU.add),
                    [b_htmp, bk_tr.buf], [b_hT[so]])
                ACT(lambda so=so: nc.scalar.copy(out=hTb[so][:], in_=hT[so][:]), [b_hT[so]], [b_hTb[so]])
            tk(1.4)
            DVE(lambda: nc.vector.tensor_tensor(out=ytmp[:], in0=yov, in1=eaT[:], op=ALU.mult),
                [bk_yo.buf, b_eaT], [b_ytmp])
            DVE(lambda: nc.vector.tensor_tensor(out=ytmp[:], in0=ydv, in1=ytmp[:], op=ALU.add),
                [bk_yd.buf, b_ytmp], [b_ytmp])
            DVE(lambda: nc.vector.tensor_tensor(out=scrA[:, :, cols], in0=scrA[:, :, cols], in1=ytmp[:], op=ALU.add),
                list(b_scrA) + [b_ytmp], list(b_scrA))

        def ssd_prepare_dt(nsub):
            ACT(lambda: nc.scalar.activation(out=dts[:, 0, 0:nsub, :], in_=dts[:, 0, 0:nsub, :], func=AF.Exp),
                [b_dts], [b_dts])
            ACT(lambda: nc.scalar.activation(out=dts[:, 1, 0:nsub, :], in_=dts[:, 0, 0:nsub, :], func=AF.Ln, bias=1.0),
                [b_dts], [b_dts])
            DVE(lambda: nc.vector.tensor_tensor(out=dts[:, 2, 0:nsub, :], in0=dts[:, 1, 0:nsub, :],
                                                in1=aneg[:].unsqueeze(1).to_broadcast([128, nsub, 8]), op=ALU.mult),
                [b_dts, b_aneg], [b_dts])
            DVE(lambda: nc.vector.tensor_copy(out=a_bf[:, 0:nsub, :], in_=dts[:, 2, 0:nsub, :]), [b_dts], [b_abf])

        def ssd_gate_norm(T, tick=None):
            for g in range(2):
                if tick:
                    tick(1.5)
                bk = nextbank()
                for cc in range(2):
                    c = 2 * g + cc
                    DVE(lambda c=c: nc.vector.tensor_tensor(out=scrA[:, c, 0:T], in0=scrA[:, c, 0:T], in1=zT[:, c, 0:T],
                                                            op=ALU.mult), [b_scrA[c], b_z[c]], [b_scrA[c]])
                    ACT(lambda c=c, cc=cc: nc.scalar.activation(out=sq_bf[:, cc, 0:T], in_=scrA[:, c, 0:T],
                                                                func=AF.Square), [b_scrA[c]], [b_sq[cc]])
                    mm(bk.ap[:, 0:T], C("ones"), sq_bf[:, cc, 0:T], cc == 0, cc == 1, [b_cst, b_sq[cc]], [bk.buf])
                rn = scrB[:, 0:512]
                ACT(lambda: nc.scalar.activation(out=rn[:, 0:T], in_=bk.ap[:, 0:T], func=AF.Ln, scale=1.0 / 256, bias=EPS),
                    [bk.buf], [b_scrB[0]])
                ACT(lambda: nc.scalar.activation(out=rn[:, 0:T], in_=rn[:, 0:T], func=AF.Exp, scale=-0.5),
                    [b_scrB[0]], [b_scrB[0]])
                for cc in range(2):
                    c = 2 * g + cc
                    DVE(lambda c=c: nc.vector.scalar_tensor_tensor(out=mixT[:, 4 + c, 0:T], in0=scrA[:, c, 0:T],
                                                                   scalar=prm[:, P_SNW + c:P_SNW + c + 1], in1=rn[:, 0:T],
                                                                   op0=ALU.mult, op1=ALU.mult),
                        [b_scrA[c], b_prm, b_scrB[0]], [b_mix[4 + c]])

        class It:
            pass

        def attn_pipeline(its, tick=None):
            N = len(its)

            def S0pe(it):
                zb = banks[it.zb]
                nq = len(it.qk)
                for n_, (o, l, r) in enumerate(it.qk):
                    mm(o, l, r, True, True, it.qk_reads, [zb.buf], inc=(n_ == nq - 1))

            def S0(it, s):
                zb = banks[it.zb]
                rows = slice(it.p0, it.p0 + it.nk)
                e = e_sb[s % NE]
                sp = sp_sb[s % NSP]
                ACT(lambda: nc.scalar.activation(out=e[rows, it.c0:it.c1], in_=zb.ap[rows, it.c0:it.c1], func=AF.Exp),
                    [zb.buf], [b_e[s % NE]])
                ACT(lambda: nc.scalar.activation(out=sp[rows, it.c0:it.c1], in_=e[rows, it.c0:it.c1], func=AF.Ln, bias=1.0),
                    [b_e[s % NE]], [b_sp[s % NSP]])
                if it.diag is not None:
                    vw, mk = it.diag
                    POOL(lambda: nc.gpsimd.tensor_tensor(out=vw(sp), in0=vw(sp), in1=mk, op=ALU.mult),
                         [b_sp[s % NSP], b_cst], [b_sp[s % NSP]])

            def S1a(it, s):
                cb = banks[it.cb]
                rows = slice(it.p0, it.p0 + it.nk)
                e = e_sb[s % NE]
                sp = sp_sb[s % NSP]
                g = g_sb[s % NG]
                A = A_sb[s % NA]
                mm(cb.ap[:, it.c0:it.c1], it.L1, sp[rows, it.c0:it.c1], it.first, False, [b_sp[s % NSP], b_cst], [cb.buf])
                ACT(lambda: nc.scalar.activation(out=g[rows, it.c0:it.c1], in_=cb.ap[rows, it.c0:it.c1], func=AF.Exp),
                    [cb.buf], [b_g[s % NG]])
                DVE(lambda: nc.vector.tensor_tensor(out=A[rows, it.c0:it.c1], in0=e[rows, it.c0:it.c1],
                                                    in1=g[rows, it.c0:it.c1], op=ALU.mult),
                    [b_e[s % NE], b_g[s % NG]], [b_A[s % NA]])
                if it.diag is not None:
                    vw, mk = it.diag
                    POOL(lambda: nc.gpsimd.tensor_tensor(out=vw(A), in0=vw(A), in1=mk, op=ALU.mult),
                         [b_A[s % NA], b_cst], [b_A[s % NA]])

            def S2a(it, s):
                cb = banks[it.cb]
                rows = slice(it.p0, it.p0 + it.nk)
                sp = sp_sb[s % NSP]
                if not it.last:
                    mm(cb.ap[:, it.c0:it.c1], it.L2, sp[rows, it.c0:it.c1], False, True, [b_sp[s % NSP], b_cst], [cb.buf])

            def S2(it, s):
                rows = slice(it.p0, it.p0 + it.nk)
                A = A_sb[s % NA]
                ob = banks[it.ob]
                for n_, (o, l, acols, st) in enumerate(it.av):
                    mm(o, l, A[rows, acols], st, it.last, it.av_reads + [b_A[s % NA]], [ob.buf],
                       inc=(n_ == len(it.av) - 1))
                if it.fin is not None:
                    it.fin()

            S0pe(its[0])
            for step in range(N + 3):
                if step + 1 < N:
                    S0pe(its[step + 1])
                if step < N:
                    S0(its[step], step)
                if 0 <= step - 3 < N:
                    S2a(its[step - 3], step - 3)
                if 0 <= step - 1 < N:
                    S1a(its[step - 1], step - 1)
                if 0 <= step - 3 < N:
                    S2(its[step - 3], step - 3)
                if tick:
                    tick(max(0.3, back["left"] / max(1, N + 3 - step)))

        def attn_prompt(Q, T, tick=None):
            its = []
            nkb = 4 * Q + 4
            cnt = 0
            for hp in range(4):
                for i in range(nkb):
                    kb = nkb - 1 - i
                    for e in range(2):
                        it = It()
                        it.zb = cnt % 2
                        it.cb = 2 + e
                        it.ob = 4 + e
                        cnt += 1
                        it.nk = 128
                        it.p0 = 0
                        d = kb - 4 * Q
                        it.c0 = 128 * d if d >= 0 else 0
                        it.c1 = T
                        pr = slice(64 * e, 64 * e + 64)
                        it.qk = [(banks[it.zb].ap[:, it.c0:it.c1], kT[:, hp, kb * 128:(kb + 1) * 128],
                                  qpad[e][:, hp, it.c0:it.c1])]
                        it.qk_reads = [b_kT[0], b_kT[1], b_qT]
                        it.first = (i == 0)
                        it.last = (kb == 0)
                        it.L1 = C("negLinc")
                        it.L2 = C("negLstr")
                        if d >= 0:
                            c0 = it.c0
                            it.diag = ((lambda t, c0=c0: t[:, c0:c0 + 128]), C("amask"))
                        else:
                            it.diag = None
                        it.av = [(banks[it.ob].ap[:, it.c0:it.c1],
                                  v_bf[:, kb, hp * 128:(hp + 1) * 128], slice(it.c0, it.c1), i == 0)]
                        it.av_reads = [b_v[0], b_v[1]]
                        it.fin = None
                        if it.last:
                            def fin(hp=hp, obi=it.ob, pr=pr):
                                ACT(lambda: nc.scalar.copy(out=mixT[pr, hp, 0:T], in_=banks[obi].ap[pr, 0:T]),
                                    [banks[obi].buf], [b_mix[hp]])
                            it.fin = fin
                        its.append(it)
            attn_pipeline(its, tick)

        def attn_sample_pair(qa, qb, npast_blk, tick=None):
            its = []
            cnt = 0
            nblk = npast_blk + 1
            for i in range(nblk):
                for sl, q in ((0, qa), (1, qb)):
                    it = It()
                    it.zb = cnt % 2
                    it.cb = 2 + sl
                    it.ob = 4 + sl
                    cnt += 1
                    e = q % 2
                    j = q // 2
                    qcols = slice(q * 64, q * 64 + 64)
                    it.c0 = 0
                    it.c1 = 512
                    it.first = (i == 0)
                    it.last = (i == nblk - 1)
                    zb = banks[it.zb]
                    ob = banks[it.ob]
                    obv = ob.ap[:, 0:256].rearrange("p (h t) -> p h t", h=4)
                    if i == 0:
                        it.nk = 64
                        it.p0 = 64 * e
                        rows = slice(64 * e, 64 * e + 64)
                        it.qk = [(zb.ap[rows, h * 64:(h + 1) * 64],
                                  kTn[:, h // 2, qcols],
                                  qpad[h % 2][:, h // 2, qcols]) for h in range(8)]
                        it.qk_reads = [b_kTn, b_qT]
                        it.L1 = C("negLinc_blk", 64 * e, 64 * e + 64)
                        it.L2 = C("negL2", 64 * e, 64 * e + 64)
                        it.diag = ((lambda t, rows=rows: t[rows, :].rearrange("p (h t) -> p h t", h=8)),
                                   C("amask_s", 64 * e, 64 * e + 64, 0, 64).unsqueeze(1).to_broadcast([64, 8, 64]))
                        it.av = [(obv[64 * (h % 2):64 * (h % 2) + 64, h // 2, :], vnew[rows, j, h * 64:(h + 1) * 64],
                                  slice(h * 64, (h + 1) * 64), h < 2) for h in range(8)]
                        it.av_reads = [b_vnew]
                    else:
                        kb = npast_blk - i
                        it.nk = 128
                        it.p0 = 0
                        kc = slice(sl * 1024 + kb * 128, sl * 1024 + (kb + 1) * 128)
                        it.qk = [(zb.ap[:, h * 64:(h + 1) * 64],
                                  kT[:, h // 2, kc],
                                  qpad[h % 2][:, h // 2, qcols]) for h in range(8)]
                        it.qk_reads = [b_kT[sl], b_qT]
                        it.L1 = C("negLinc")
                        it.L2 = C("negLstr")
                        it.diag = None
                        it.av = [(obv[64 * (h % 2):64 * (h % 2) + 64, h // 2, :], v_bf[:, sl * 8 + kb, h * 64:(h + 1) * 64],
                                  slice(h * 64, (h + 1) * 64), False) for h in range(8)]
                        it.av_reads = [b_v[sl]]
                    it.fin = None
                    if it.last:
                        def fin(q=q, obi=it.ob):
                            ACT(lambda: nc.scalar.copy(
                                out=mixT[:, 0:4, q * 64:(q + 1) * 64],
                                in_=banks[obi].ap[:, 0:256].rearrange("p (h t) -> p h t", h=4)),
                                [banks[obi].buf], [b_mix[0], b_mix[1], b_mix[2], b_mix[3]])
                        it.fin = fin
                    its.append(it)
            attn_pipeline(its, tick)

        def load_past(sl, q, npast_blk):
            ktok = uT[:].rearrange("p a b c -> p (a b c)")[:, 0:npast_blk * 512].rearrange("p (b c) -> p b c", c=512)
            fw.dma(fw.pool, ktok, ck[q].rearrange("(b p) c -> p b c", p=128), writes=[b_uT[0], b_uT[1]])
            fw.dma(fw.pool, v_bf[:, sl * 8:sl * 8 + npast_blk, :], cv[q].rearrange("(b p) c -> p b c", p=128),
                   writes=[b_v[sl]])
            for kb in range(npast_blk):
                bk = nextbank()
                v3 = bk.bf[:, 0:512].rearrange("p (c t) -> p c t", c=4)
                for hp in range(4):
                    PE(lambda hp=hp, kb=kb, v3=v3: nc.tensor.transpose(v3[:, hp, :], ktok[:, kb, hp * 128:(hp + 1) * 128],
                                                                       C("ident")),
                       [b_uT[0], b_uT[1], b_cst], [bk.buf], inc=(hp == 3))
                DVE(lambda kb=kb, v3=v3: nc.vector.tensor_copy(
                    out=kT[:, :, sl * 1024 + kb * 128: sl * 1024 + (kb + 1) * 128], in_=v3), [bk.buf], [b_kT[sl]])

        def out_proj_norm2(T, nsub, X, Xn=None, nsub_n=0):
            w0, wb0 = ws_get("wo0")
            w1, wb1 = ws_get("wo1", hold=1)
            for j in range(max(nsub, nsub_n) + 2):
                if j < nsub:
                    for dh, (w, wb) in enumerate(((w0, wb0), (w1, wb1))):
                        bk = nextbank()
                        for c in range(8):
                            mm(bk.ap[:, :], mixT[:, c, j * 128:(j + 1) * 128], w[:, c, :], c == 0, c == 7,
                               [wb, b_mix[c]], [bk.buf], inc=(c == 7))
                        DVE(lambda j=j, dh=dh, bk=bk: nc.vector.tensor_tensor(
                            out=X.x[:, j, dh * 512:(dh + 1) * 512], in0=X.x[:, j, dh * 512:(dh + 1) * 512], in1=bk.ap[:, :],
                            op=ALU.add), [X.bx[j], bk.buf], [X.bx[j]])
                if Xn is not None and j < nsub_n:
                    stg, bstg = norm1_stage(j)
                    norm_a(j, Xn, stg, bstg, 2)
                    norm_b(j, P_N1W, Xn, stg, bstg)
                if 0 <= j - 1 < nsub:
                    norm_a(j - 1, X, xs_bf[(j - 1) % 2], b_xsbf[(j - 1) % 2], (j - 1) % 2)
                if 0 <= j - 2 < nsub:
                    norm_b(j - 2, P_N2W, X, xs_bf[j % 2], b_xsbf[j % 2])

        def mlp_gen(T, nsub, X, y_rows):
            xnT = X.xnT

            def up(p):
                w, wb = ws_get("up%d" % p)
                for fc in range(4):
                    bk = nextbank_mlp()
                    for kc in range(8):
                        mm(bk.ap[:, 0:T], w[:, kc, fc * 128:(fc + 1) * 128], xnT[:, kc, 0:T], kc == 0, kc == 7,
                           [wb, X.bxn], [bk.buf], inc=(kc == 7))
                    r = rtmp[:, fc % 2, 0:T]
                    ACT(lambda bk=bk, r=r: nc.scalar.activation(out=r, in_=bk.ap[:, 0:T], func=AF.Relu),
                        [bk.buf], [b_rtmp[fc % 2]])
                    eng, h = (DVE, nc.vector) if fc % 2 == 0 else (POOL, nc.gpsimd)
                    eng(lambda r=r, fc=fc, h=h: h.tensor_tensor(out=uT[:, p % 2, fc, 0:T], in0=r, in1=r, op=ALU.mult),
                        [b_rtmp[fc % 2]], [b_uT[p % 2]])
                    yield 2.1 * T / 512
                yield "P"

            def down(p):
                w, wb = ws_get("dn%d" % p)
                for j in range(nsub):
                    for dh in range(2):
                        bk = nextbank_mlp()
                        for fcl in range(4):
                            mm(bk.ap[:, :], uT[:, p % 2, fcl, j * 128:(j + 1) * 128], w[:, fcl, dh * 512:(dh + 1) * 512],
                               fcl == 0, fcl == 3, [wb, b_uT[p % 2]], [bk.buf], inc=(fcl == 3))
                        DVE(lambda j=j, dh=dh, bk=bk: nc.vector.tensor_tensor(
                            out=X.x[:, j, dh * 512:(dh + 1) * 512], in0=X.x[:, j, dh * 512:(dh + 1) * 512],
                            in1=bk.ap[:, :], op=ALU.add), [X.bx[j], bk.buf], [X.bx[j]])
                        yield 1.05
                yield "P"

            yield from up(0)
            for p in range(1, 8):
                yield from up(p)
                yield from down(p - 1)
            yield from down(7)
            DVE(lambda: nc.vector.memset(stat[:, 12:16], 0.0), [], [b_stat])
            for j in range(nsub):
                ACT(lambda j=j: nc.scalar.activation(out=rtmp[:].rearrange("p a b -> p (a b)").bitcast(BF16)[:, 0:D],
                                                     in_=X.x[:, j, :], func=AF.Square,
                                                     accum_out=stat[:, 12 + j:13 + j]), [X.bx[j]], [b_rtmp[0], b_stat],
                    strict=True)
                yield 0.5
            ACT(lambda: nc.scalar.activation(out=stat[:, 12:12 + nsub], in_=stat[:, 12:12 + nsub], func=AF.Ln,
                                             scale=1.0 / D, bias=EPS), [b_stat], [b_stat])
            ACT(lambda: nc.scalar.activation(out=stat[:, 12:12 + nsub], in_=stat[:, 12:12 + nsub], func=AF.Exp,
                                             scale=-0.5), [b_stat], [b_stat])
            for j in range(nsub):
                DVE(lambda j=j: nc.vector.scalar_tensor_tensor(out=X.x[:, j, :], in0=X.x[:, j, :],
                                                               scalar=stat[:, 12 + j:13 + j], in1=fnw[:],
                                                               op0=ALU.mult, op1=ALU.mult),
                    [X.bx[j], b_stat, b_fnw], [X.bx[j]])
                fw.dma(fw.sp, y_rows(j), X.x[:, j, :], reads=[X.bx[j]])
                yield 0.5

        def state_out(si, dst):
            bk = nextbank()
            for hl in range(4):
                PE(lambda hl=hl: nc.tensor.transpose(bk.ap[0:64, hl * 128:(hl + 1) * 128], hT[si][:, hl * 64:(hl + 1) * 64],
                                                     identf[:, :]), [b_hT[si], b_identf], [bk.buf], inc=(hl == 3))
            DVE(lambda: nc.vector.tensor_copy(out=stT[:].rearrange("p a b -> p (a b)"), in_=bk.ap[0:64, :]),
                [bk.buf], [b_stT])
            for g in range(2):
                fw.dma(fw.sp, dst[4 * g:4 * g + 4].rearrange("h p n -> p h n"), stT[:, :, 64 * g:64 * g + 64],
                       reads=[b_stT])

        def state_in(si, src):
            for g in range(2):
                fw.dma(fw.sp, stT[:, :, 64 * g:64 * g + 64], src[4 * g:4 * g + 4].rearrange("h p n -> p h n"),
                       writes=[b_stT])
            bk = nextbank()
            for hl in range(4):
                PE(lambda hl=hl: nc.tensor.transpose(bk.ap[:, hl * 64:(hl + 1) * 64], stT[:, hl, :], identf[0:64, 0:64]),
                   [b_stT, b_identf], [bk.buf], inc=(hl == 3))
            DVE(lambda: nc.vector.tensor_copy(out=hT[si][:], in_=bk.ap[:, 0:256]), [bk.buf], [b_hT[si]])
            ACT(lambda: nc.scalar.copy(out=hTb[si][:], in_=bk.ap[:, 0:256]), [bk.buf], [b_hTb[si]])

        def conv_state_out(nseg, L, dst_of_seg):
            for s in range(nseg):
                for c in range(6):
                    fw.dma(fw.sp, dst_of_seg(s)[:, c * 128:(c + 1) * 128].rearrange("r p -> p r"),
                           xbcT[:, c, s * (3 + L) + L: s * (3 + L) + L + 3], reads=[b_xbcT[c]],
                           allow_slow_non_contiguous=True)

        T = 512
        back = {"gen": None, "credit": 0.0, "left": 0.0}

        def tick(us):
            g = back["gen"]
            if g is None:
                return
            back["credit"] += us
            while back["credit"] > 0:
                try:
                    v = next(g)
                    if v != "P":
                        back["credit"] -= v
                        back["left"] -= v
                except StopIteration:
                    back["gen"] = None
                    back["credit"] = 0.0
                    return

        def tick_pieces(k):
            g = back["gen"]
            if g is None:
                return
            while k > 0:
                v = next(g)
                if v == "P":
                    k -= 1
                else:
                    back["left"] -= v

        def drain():
            g = back["gen"]
            if g is not None:
                for _ in g:
                    pass
            back["gen"] = None
            back["credit"] = 0.0

        tiles = []
        for b in range(NBP):
            for Q in range(SEQ // T):
                tiles.append(("p", b, Q))
        if NBS:
            tiles.append(("s", 0, 0))

        def load_x(n):
            kind, b, Q = tiles[n]
            X = XS[n % 2]
            if kind == "p":
                r0 = b * SEQ + Q * T
                fw.dma(fw.sp, X.x[:, :, :], xp[r0:r0 + T, :].rearrange("(j p) d -> p j d", p=128), writes=list(X.bx))
                return 4
            nsub_ = NBS // 2
            fw.dma(fw.sp, X.x[:, 0:nsub_, :], xs[:, :].rearrange("(j p) d -> p j d", p=128), writes=list(X.bx[0:nsub_]))
            return nsub_

        nsub0 = load_x(0)
        norm_T(nsub0, P_N1W, XS[0])
        for n, (kind, b, Q) in enumerate(tiles):
            X = XS[n % 2]
            if kind == "p":
                r0 = b * SEQ + Q * T
                tick_pieces(2)
                fw.inherit(conv_bufs, attn_bufs)
                if Q == 0:
                    DVE(lambda: nc.vector.memset(hT[0][:], 0.0), [], [b_hT[0]])
                    DVE(lambda: nc.vector.memset(hTb[0][:], 0.0), [], [b_hTb[0]])
                    DVE(lambda: nc.vector.memset(xbcT[:, :, 0:3], 0.0), [], list(b_xbcT))
                else:
                    DVE(lambda: nc.vector.tensor_copy(out=xbcT[:, :, 0:3], in_=cstate[:]), [b_cstate], list(b_xbcT))
                in_proj(T, 4, 1, T,
                        kT_dst=lambda cc, Q=Q: kT[:, cc, Q * T:(Q + 1) * T], kT_bufs=[b_kT[0], b_kT[1]],
                        k_rows=lambda j, r0=r0: kp[r0 + j * 128: r0 + (j + 1) * 128, :],
                        v_rows=lambda j, r0=r0: vp[r0 + j * 128: r0 + (j + 1) * 128, :],
                        v_dst=lambda j, Q=Q: v_bf[:, 4 * Q + j, :], v_bufs=[b_v[0], b_v[1]], X=X)
                DVE(lambda: nc.vector.tensor_copy(out=cstate[:], in_=xbcT[:, :, T:T + 3]), list(b_xbcT), [b_cstate])
                if Q == SEQ // T - 1:
                    conv_state_out(1, T, lambda s_, b=b: convp[b])
                conv_silu(T, 1, T, tick)
                ssd_prepare_dt(4)
                for j in range(4):
                    ssd_subtile(j, 4, (0, 0), True, (0, 0), tick)
                ssd_gate_norm(T, tick)
                if Q == SEQ // T - 1:
                    state_out(0, ssmp[b])
                fw.inherit(attn_bufs, conv_bufs)
                attn_prompt(Q, T, tick)
                drain()
                Tn, nsub = T, 4
                y_rows = (lambda j, r0=r0: yp[r0 + j * 128: r0 + (j + 1) * 128, :])
            else:
                Ts = NBS * 64
                nsub = NBS // 2
                npb = PAST // 128
                tick_pieces(2)
                fw.inherit(conv_bufs, attn_bufs)
                for q in range(NBS):
                    for c in range(6):
                        fw.dma(fw.sp, xbcT[:, c, q * 67:q * 67 + 3],
                               sconv[q][:, c * 128:(c + 1) * 128].rearrange("r p -> p r"),
                               writes=[b_xbcT[c]], allow_slow_non_contiguous=True)
                    state_in(q, sssm[q])
                in_proj(Ts, nsub, NBS, 64,
                        kT_dst=lambda cc: kTn[:, cc, 0:Ts], kT_bufs=[b_kTn],
                        k_rows=lambda j: ksm[j * 128:(j + 1) * 128, :],
                        v_rows=lambda j: vsm[j * 128:(j + 1) * 128, :],
                        v_dst=lambda j: vnew[:, j, :], v_bufs=[b_vnew], X=X)
                conv_state_out(NBS, 64, lambda s_: convs[s_])
                conv_silu(Ts, NBS, 64, tick)
                ssd_prepare_dt(nsub)
                for j in range(nsub):
                    ssd_subtile(j, nsub, (2 * j, 2 * j + 1), False, (2 * j, 2 * j + 1), tick)
                ssd_gate_norm(Ts, tick)
                for q in range(NBS):
                    state_out(q, ssms[q])
                fw.inherit(attn_bufs, conv_bufs)
                drain()
                for pr in range(NBS // 2):
                    load_past(0, 2 * pr, npb)
                    load_past(1, 2 * pr + 1, npb)
                    attn_sample_pair(2 * pr, 2 * pr + 1, npb)
                Tn = Ts
                y_rows = (lambda j: ys[j * 128:(j + 1) * 128, :])
            if n + 1 < len(tiles):
                nsub_n = load_x(n + 1)
                out_proj_norm2(Tn, nsub, X, XS[(n + 1) % 2], nsub_n)
            else:
                out_proj_norm2(Tn, nsub, X)
            back["gen"] = mlp_gen(Tn, nsub, X, y_rows)
            back["left"] = 140.0 * Tn / 512
        drain()

        fw.finish()
        build.nins = fw.nins
    return nc


def pack_params(norm1_w, norm2_w, conv_w, conv_b, dt_bias, a_log, d_skip, ssm_norm_w):
    prm = np.zeros((128, NPRM), np.float32)
    prm[:, P_N1W:P_N1W + 8] = norm1_w.reshape(8, 128).T
    prm[:, P_N2W:P_N2W + 8] = norm2_w.reshape(8, 128).T
    prm[:, P_CW:P_CW + 24] = conv_w.reshape(4, 6, 128).transpose(2, 1, 0).reshape(128, 24)
    prm[:, P_CB:P_CB + 6] = conv_b.reshape(6, 128).T
    prm[:, P_DSK:P_DSK + 4] = np.repeat(d_skip, 64).reshape(4, 128).T
    prm[:, P_SNW:P_SNW + 4] = ssm_norm_w.reshape(4, 128).T
    prm[:, P_DTB:P_DTB + 8] = np.broadcast_to(dt_bias.reshape(1, 8), (128, 8))
    prm[:, P_ALOG:P_ALOG + 8] = np.broadcast_to(a_log.reshape(1, 8), (128, 8))
    return prm


def make_in_maps(inputs, ncores, NBP, NBS):
    f = lambda a: np.ascontiguousarray(np.asarray(a, dtype=np.float32))
    x_prompt = f(inputs["x_prompt"])
    x_sample = f(inputs["x_sample"])
    cache_k = f(inputs["cache_k"])[0]
    cache_v = f(inputs["cache_v"])[0]
    state_conv = f(inputs["state_conv"])[0]
    state_ssm = f(inputs["state_ssm"])[0]
    SEQ = x_prompt.shape[1]
    PAST = cache_k.shape[1]
    prm = pack_params(f(inputs["norm1_w"])[0], f(inputs["norm2_w"])[0], f(inputs["conv_w"])[0], f(inputs["conv_b"])[0],
                      f(inputs["dt_bias"])[0], f(inputs["a_log"])[0], f(inputs["d_skip"])[0], f(inputs["ssm_norm_w"])[0])
    fnw = np.ascontiguousarray(np.broadcast_to(f(inputs["final_norm_w"]).reshape(1, D), (128, D)))
    shared = {
        "w_in": f(inputs["w_in"])[0], "w_out": f(inputs["w_out"])[0], "w_up": f(inputs["w_up"])[0],
        "w_down": f(inputs["w_down"])[0], "prm": prm, "fnw": fnw, "cst": make_consts(),
        "identf": np.eye(128, dtype=np.float32),
    }
    maps = []
    for c in range(ncores):
        m = dict(shared)
        m["xp"] = np.ascontiguousarray(x_prompt[c * NBP:(c + 1) * NBP].reshape(NBP * SEQ, D))
        m["xs"] = np.ascontiguousarray(x_sample[c * NBS:(c + 1) * NBS].reshape(NBS * 64, D))
        m["ck"] = np.ascontiguousarray(cache_k[c * NBS:(c + 1) * NBS].reshape(NBS, PAST, 512))
        m["cv"] = np.ascontiguousarray(cache_v[c * NBS:(c + 1) * NBS].reshape(NBS, PAST, 512))
        m["sconv"] = np.ascontiguousarray(state_conv[c * NBS:(c + 1) * NBS])
        m["sssm"] = np.ascontiguousarray(state_ssm[c * NBS:(c + 1) * NBS])
        maps.append(m)
    return maps


def gather(results, NBP, SEQ, NBS):
    cat = lambda k: np.concatenate([np.asarray(r[k]) for r in results], axis=0)
    nb = NBP * len(results)
    ns = NBS * len(results)
    y_prompt = cat("yp").reshape(nb, SEQ, D)
    y_sample = cat("ys").reshape(ns, 64, D)
    k_prompt = cat("kp").reshape(1, nb, SEQ, 8, 64)
    v_prompt = cat("vp").reshape(1, nb, SEQ, 8, 64)
    conv_prompt = cat("convp").reshape(1, nb, 3, 768)
    ssm_prompt = cat("ssmp").reshape(1, nb, 8, 64, 64)
    k_sample = cat("ksm").reshape(1, ns, 64, 8, 64)
    v_sample = cat("vsm").reshape(1, ns, 64, 8, 64)
    conv_sample = cat("convs").reshape(1, ns, 3, 768)
    ssm_sample = cat("ssms").reshape(1, ns, 8, 64, 64)
    return tuple(np.ascontiguousarray(a.astype(np.float32)) for a in (
        y_prompt, y_sample, k_prompt, v_prompt, conv_prompt, ssm_prompt, k_sample, v_sample, conv_sample, ssm_sample))


def kernel(**inputs):
    NBP, NBS = 4, 4
    SEQ = int(np.asarray(inputs["x_prompt"]).shape[1])
    PAST = int(np.asarray(inputs["cache_k"]).shape[2])
    nc = build(NBP=NBP, SEQ=SEQ, NBS=NBS, PAST=PAST)
    in_maps = make_in_maps(inputs, NCORES, NBP, NBS)
    res = run_bass_kernel_spmd(nc, in_maps, core_ids=list(range(NCORES)))
    return gather(res.results, NBP, SEQ, NBS)
```
